# Optimizing a Trainium2 kernel written in Bass

```python
import jax, jax.numpy as jnp
from jax import lax
import numpy as np

D_MODEL = 1024
BATCH = 8
SEQ = 8192
DEPTH = 1

D_MIX = D_MODEL
W_CONV = D_MIX // 2
W_LRU = D_MIX - W_CONV
N_CONV_GROUPS = 8
N_LRU_HEADS = 8
LRU_HEAD_DIM = W_LRU // N_LRU_HEADS
CONV_WIDTH = 31
LRU_CONV_WIDTH = 4
LRU_C = 8.0
D_FF = 2816
FFN_RES_SCALE = 0.5
RMS_EPS = 1e-6
LN_EPS = 1e-5

kernel_name = "macaron_conformer_conv_rglru_hybrid"


def rmsnorm(x, g):
    xf = x.astype(jnp.float32)
    y = xf * lax.rsqrt(jnp.mean(xf * xf, axis=-1, keepdims=True) + RMS_EPS)
    return (y * g.astype(jnp.float32)).astype(x.dtype)


def layernorm(x, g, b):
    xf = x.astype(jnp.float32)
    mu = jnp.mean(xf, axis=-1, keepdims=True)
    xc = xf - mu
    var = jnp.mean(xc * xc, axis=-1, keepdims=True)
    y = xc * lax.rsqrt(var + LN_EPS)
    return (y * g.astype(jnp.float32) + b.astype(jnp.float32)).astype(x.dtype)


def swiglu_ffn(x, w_gate, w_up, w_down):
    return (jax.nn.silu(x @ w_gate) * (x @ w_up)) @ w_down


def causal_depthwise_conv(x, w, b):
    k = w.shape[0]
    c = x.shape[-1]
    out = lax.conv_general_dilated(
        x, w[:, None, :].astype(x.dtype), window_strides=(1,), padding=[(k - 1, 0)],
        dimension_numbers=("NWC", "WIO", "NWC"), feature_group_count=c)
    return out + b.astype(x.dtype)


def block_diag_linear(x, w, b):
    bsz, s, _ = x.shape
    xh = x.reshape(bsz, s, N_LRU_HEADS, LRU_HEAD_DIM)
    y = jnp.einsum("bshi,hij->bshj", xh, w.astype(x.dtype))
    return y.reshape(bsz, s, W_LRU) + b.astype(x.dtype)


def rg_lru(x, w_a, b_a, w_x, b_x, lam):
    xf = x.astype(jnp.float32)
    r = jax.nn.sigmoid(block_diag_linear(x, w_a, b_a).astype(jnp.float32))
    i = jax.nn.sigmoid(block_diag_linear(x, w_x, b_x).astype(jnp.float32))
    log_a = -LRU_C * r * jax.nn.softplus(-lam.astype(jnp.float32))
    a = jnp.exp(log_a)
    mult = jnp.sqrt(-jnp.expm1(2.0 * log_a))
    bterm = mult * (i * xf)

    def combine(lhs, rhs):
        a1, b1 = lhs
        a2, b2 = rhs
        return a1 * a2, a2 * b1 + b2

    _, h = lax.associative_scan(combine, (a, bterm), axis=1)
    return h.astype(x.dtype)


def hybrid_mixer(h, w_in, conv_dw, conv_dw_bias, conv_ln_g, conv_ln_b,
                 lru_conv_w, lru_conv_b, lru_w_a, lru_b_a, lru_w_x, lru_b_x, lru_lambda, w_out):
    z = h @ w_in
    c_val, c_gate, r_x, r_gate = jnp.split(
        z, [W_CONV, 2 * W_CONV, 2 * W_CONV + W_LRU], axis=-1)
    u = c_val * jax.nn.sigmoid(c_gate)
    u = causal_depthwise_conv(u, conv_dw, conv_dw_bias)
    u = jax.nn.silu(layernorm(u, conv_ln_g, conv_ln_b))
    xr = causal_depthwise_conv(r_x, lru_conv_w, lru_conv_b)
    yr = rg_lru(xr, lru_w_a, lru_b_a, lru_w_x, lru_b_x, lru_lambda)
    yr = yr * jax.nn.gelu(r_gate, approximate=True)
    return jnp.concatenate([u, yr], axis=-1) @ w_out


def setup_inputs(seed: int = 0) -> dict:
    key = jax.random.key(seed)
    ks = jax.random.split(key, 32)
    f32 = jnp.float32
    nrm = lambda k, shape, scale: jax.random.normal(k, shape, f32) * scale
    gain = lambda k, n: 1.0 + 0.02 * jax.random.normal(k, (n,), f32)
    d_in = 2 * W_CONV + 2 * W_LRU
    u = jax.random.uniform(ks[20], (W_LRU,), f32, 0.9, 0.999)
    a0 = u ** (1.0 / LRU_C)
    lru_lambda = jnp.log(a0) - jnp.log1p(-a0)
    return {
        "x": jax.random.normal(ks[0], (BATCH, SEQ, D_MODEL), f32),
        "ffn1_norm": gain(ks[1], D_MODEL),
        "ffn1_w_gate": nrm(ks[2], (D_MODEL, D_FF), D_MODEL ** -0.5),
        "ffn1_w_up": nrm(ks[3], (D_MODEL, D_FF), D_MODEL ** -0.5),
        "ffn1_w_down": nrm(ks[4], (D_FF, D_MODEL), D_FF ** -0.5),
        "mix_norm": gain(ks[5], D_MODEL),
        "w_in": nrm(ks[6], (D_MODEL, d_in), D_MODEL ** -0.5),
        "conv_dw": nrm(ks[7], (CONV_WIDTH, W_CONV), CONV_WIDTH ** -0.5),
        "conv_dw_bias": nrm(ks[8], (W_CONV,), 0.02),
        "conv_ln_g": gain(ks[9], W_CONV),
        "conv_ln_b": nrm(ks[10], (W_CONV,), 0.02),
        "lru_conv_w": nrm(ks[11], (LRU_CONV_WIDTH, W_LRU), LRU_CONV_WIDTH ** -0.5),
        "lru_conv_b": nrm(ks[12], (W_LRU,), 0.02),
        "lru_w_a": nrm(ks[13], (N_LRU_HEADS, LRU_HEAD_DIM, LRU_HEAD_DIM), LRU_HEAD_DIM ** -0.5),
        "lru_b_a": nrm(ks[14], (W_LRU,), 0.02),
        "lru_w_x": nrm(ks[15], (N_LRU_HEADS, LRU_HEAD_DIM, LRU_HEAD_DIM), LRU_HEAD_DIM ** -0.5),
        "lru_b_x": nrm(ks[16], (W_LRU,), 0.02),
        "lru_lambda": lru_lambda,
        "w_out": nrm(ks[17], (D_MIX, D_MODEL), D_MIX ** -0.5),
        "ffn2_norm": gain(ks[18], D_MODEL),
        "ffn2_w_gate": nrm(ks[19], (D_MODEL, D_FF), D_MODEL ** -0.5),
        "ffn2_w_up": nrm(ks[21], (D_MODEL, D_FF), D_MODEL ** -0.5),
        "ffn2_w_down": nrm(ks[22], (D_FF, D_MODEL), D_FF ** -0.5),
        "final_norm": gain(ks[23], D_MODEL),
    }


def reference(x, ffn1_norm, ffn1_w_gate, ffn1_w_up, ffn1_w_down, mix_norm, w_in,
              conv_dw, conv_dw_bias, conv_ln_g, conv_ln_b, lru_conv_w, lru_conv_b,
              lru_w_a, lru_b_a, lru_w_x, lru_b_x, lru_lambda, w_out,
              ffn2_norm, ffn2_w_gate, ffn2_w_up, ffn2_w_down, final_norm):
    for _ in range(DEPTH):
        x = x + FFN_RES_SCALE * swiglu_ffn(rmsnorm(x, ffn1_norm), ffn1_w_gate, ffn1_w_up, ffn1_w_down)
        x = x + hybrid_mixer(rmsnorm(x, mix_norm), w_in, conv_dw, conv_dw_bias, conv_ln_g, conv_ln_b,
                             lru_conv_w, lru_conv_b, lru_w_a, lru_b_a, lru_w_x, lru_b_x,
                             lru_lambda, w_out)
        x = x + FFN_RES_SCALE * swiglu_ffn(rmsnorm(x, ffn2_norm), ffn2_w_gate, ffn2_w_up, ffn2_w_down)
    return rmsnorm(x, final_norm)
```

```python
import numpy as np
import concourse.bass as bass
import concourse.mybir as mybir
from concourse.bass_utils import run_bass_kernel_spmd

F32 = mybir.dt.float32
BF16 = mybir.dt.bfloat16
AF = mybir.ActivationFunctionType
ALU = mybir.AluOpType

D = 1024
S = 8192
T = 512
NT = S // T
DFF = 2816
KC = D // 128
FC = DFF // 128
NCORES = 8
NS = 4
SLOT = 4096
NTMP = 8
XY_ENG = "sp"
RMS_EPS = 1e-6
LN_EPS = 1e-5

C_G1, C_G2, C_G3, C_G4 = 0, 8, 16, 24
C_CB, C_LG, C_LB = 32, 36, 40
C_W4, C_B4 = 44, 60
C_BA, C_BX, C_LAM = 64, 68, 72
C_CW = 76
NCST = 200


class Buf:
    __slots__ = ("name", "last_w", "readers")

    def __init__(self, name):
        self.name = name
        self.last_w = None
        self.readers = []


class Prog:
    ENG = ("pe", "act", "dve", "pool", "sp")

    def __init__(self):
        self.q = {e: [] for e in self.ENG}
        self.count = {e: 0 for e in self.ENG}
        self.seen = {e: {} for e in self.ENG}
        self.dcount = {}
        self.need = {e: set() for e in self.ENG}

    def _deps(self, eng, reads, writes, is_dma):
        deps = []
        for b in reads:
            if b.last_w is not None:
                deps.append((b.last_w, True))
        for b in writes:
            if is_dma and b.readers:
                for t in b.readers:
                    deps.append((t, False))
                continue
            if b.last_w is not None:
                deps.append((b.last_w, False))
            for t in b.readers:
                deps.append((t, False))
        for (key, val), raw in deps:
            if key == eng:
                if eng in ("pe", "sp"):
                    continue
                pass
            if self.seen[eng].get(key, 0) >= val:
                continue
            self.seen[eng][key] = val
            if key in self.need:
                self.need[key].add(val)
            self.q[eng].append(("wait", key, val))

    def _commit(self, tok, reads, writes):
        for b in writes:
            b.last_w = tok
            b.readers = []
        for b in reads:
            if b in writes:
                continue
            rs = [t for t in b.readers if t[0] != tok[0]]
            rs.append(tok)
            b.readers = rs

    def op(self, eng, fn, reads=(), writes=(), inc=True):
        self._deps(eng, reads, writes, False)
        if inc:
            self.count[eng] += 1
            tok = (eng, self.count[eng])
            self.q[eng].append(("op", fn, eng, self.count[eng]))
        else:
            tok = (eng, self.count[eng] + 1)
            self.q[eng].append(("op", fn, None, 0))
        self._commit(tok, reads, writes)
        return tok

    def dma(self, eng, fn, key, reads=(), writes=()):
        self._deps(eng, reads, writes, True)
        self.dcount[key] = self.dcount.get(key, 0) + 16
        tok = (key, self.dcount[key])
        self.q[eng].append(("op", fn, key, 16))
        self._commit(tok, reads, writes)
        return tok

    def wait(self, eng, tok):
        key, val = tok
        if self.seen[eng].get(key, 0) >= val:
            return
        self.seen[eng][key] = val
        if key in self.need:
            self.need[key].add(val)
        self.q[eng].append(("wait", key, val))

    def finalize(self):
        rank = {}
        for e in self.ENG:
            rank[e] = {idx: r + 1 for r, idx in enumerate(sorted(self.need[e]))}
        out = {}
        for e in self.ENG:
            lst = []
            for it in self.q[e]:
                if it[0] == "wait":
                    key, val = it[1], it[2]
                    if key in rank:
                        lst.append(("wait", key, rank[key][val]))
                    else:
                        lst.append(it)
                else:
                    _, fn, key, amt = it
                    if key in rank:
                        if amt in rank[key]:
                            lst.append(("op", fn, key, 1))
                        else:
                            lst.append(("op", fn, None, 0))
                    else:
                        lst.append(it)
            out[e] = lst
        return out


_LAST_PROG = None


def build_nc(S=S):
    NT = S // T
    nc = bass.Bass("TRN2", target_bir_lowering=False)
    P = Prog()
    global _LAST_PROG
    _LAST_PROG = P

    def din(name, shape):
        return nc.dram_tensor(name, list(shape), F32, kind="ExternalInput").ap()

    xT = din("xT", [D, S])
    yT = nc.dram_tensor("yT", [D, S], F32, kind="ExternalOutput").ap()
    cst_d = din("cst", [128, NCST])
    ident_d = din("ident", [128, 128])
    wg_d = [din("wg1", [D, DFF]), din("wg2", [D, DFF])]
    wu_d = [din("wu1", [D, DFF]), din("wu2", [D, DFF])]
    wd_d = [din("wd1", [DFF, D]), din("wd2", [DFF, D])]
    win_d = din("win", [D, 2048])
    wout_d = din("wout", [D, D])
    lwa_d = din("lwa", [8, 64, 64])
    lwx_d = din("lwx", [8, 64, 64])

    def dscr(name, shape):
        return nc.dram_tensor(name, list(shape), BF16, kind="Internal").ap()

    gu_s = [dscr("gu1s", [11, 128, 4096]), dscr("gu2s", [11, 128, 4096])]
    dd_s = [dscr("dd1s", [8, 128, 2816]), dscr("dd2s", [8, 128, 2816])]
    win_s = dscr("wins", [4, 128, 4096])
    wout_s = dscr("wouts", [2, 128, 4096])
    cd_s = dscr("cds", [2, 128, 3968])

    sb = nc.alloc_sbuf_tensor
    cst = sb("cst_sb", [128, NCST], F32)
    der = sb("der_sb", [128, 32], F32)
    ident = sb("ident_sb", [128, 128], F32)
    onesD = sb("onesD", [128, 128], BF16)
    onesC = sb("onesC", [128, 128], BF16)
    gatew = sb("gatew", [128, 8, 128], BF16)
    xs = [sb("x0", [128, KC, T], F32), sb("x1", [128, KC, T], F32)]
    xn = sb("xn", [128, KC, T], BF16)
    xn2 = sb("xn2", [128, KC, T], BF16)
    tmpa = sb("tmpa", [128, 4, T], F32)
    sq = sb("sq", [128, 8, T], BF16)
    h = sb("h", [128, FC, T], BF16)
    ring = sb("ring", [128, NS, SLOT], BF16)
    ubuf = sb("ubuf", [128, 2, 30 + T], F32)
    ubb = sb("ubb", [128, 2, 30 + T], BF16)
    v = sb("v", [128, 4, T], F32)
    xbr = sb("xbr", [128, 4, 3 + T], F32)
    xr = sb("xr", [128, 4, T], F32)
    xrb = sb("xrb", [128, 4, T], BF16)
    hout = sb("hout", [128, 4, T], F32)
    hst = sb("hst", [128, 4], F32)
    mixo = sb("mixo", [128, KC, T], BF16)
    tmpt = sb("tmpt", [128, NTMP, T], F32)
    glt = sb("glt", [128, 4, T], F32)
    igt = sb("igt", [128, 4, T], F32)
    banks = [nc.alloc_psum_tensor("bank%d" % i, [128, T], F32) for i in range(8)]

    CONST = Buf("const")
    DER = Buf("der")
    GATEW = Buf("gatew")
    X = [[Buf("x%d_%d" % (b, k)) for k in range(KC)] for b in range(2)]
    XN = [Buf("xn%d" % k) for k in range(KC)]
    XN2 = [Buf("xn2_%d" % k) for k in range(KC)]
    TMPA = [Buf("tmpa%d" % k) for k in range(4)]
    SQ = [Buf("sq%d" % k) for k in range(8)]
    H = [Buf("h%d" % k) for k in range(FC)]
    RING = [Buf("ring%d" % k) for k in range(NS)]
    UB = [Buf("ub%d" % k) for k in range(4)]
    V = [Buf("v%d" % k) for k in range(4)]
    XBR = [Buf("xbr%d" % k) for k in range(4)]
    XR = [Buf("xr%d" % k) for k in range(4)]
    XRB = [Buf("xrb%d" % k) for k in range(4)]
    HOUT = [Buf("hout%d" % k) for k in range(4)]
    HST = [Buf("hst%d" % k) for k in range(4)]
    MIXO = [Buf("mixo%d" % k) for k in range(KC)]
    TMP = [Buf("tmp%d" % k) for k in range(NTMP)]
    GL = [Buf("gl%d" % k) for k in range(4)]
    IG = [Buf("ig%d" % k) for k in range(4)]
    BANK = [Buf("bank%d" % k) for k in range(8)]
    FAM = {n: Buf(n) for n in ("gu0", "gu1", "dd0", "dd1", "win", "wout", "cd")}

    st = {"bank": 0, "tmp": 0, "ring": 0}

    def bank():
        i = st["bank"]
        st["bank"] = (i + 1) % 8
        return banks[i][:], BANK[i]

    def tmp():
        i = st["tmp"]
        st["tmp"] = (i + 1) % NTMP
        return tmpt[:, i, :], TMP[i]

    def tmp_a():
        i = st.get("tmpa", 0)
        st["tmpa"] = (i + 1) % 4
        return tmpa[:, i, :], TMPA[i]

    def col(c):
        return cst[:, c:c + 1]

    def ring_load(src, U, fam):
        s = st["ring"]
        st["ring"] = (s + 1) % NS
        dst = ring[:, s, 0:U]
        P.dma("sp", lambda e: e.dma_start(out=dst, in_=src), ("ring", s),
              reads=[FAM[fam]], writes=[RING[s]])
        return ring[:, s, 0:U], RING[s]

    def xview(ap, i):
        return ap.rearrange("(kc p) t -> p kc t", p=128)[:, :, i * T:(i + 1) * T]

    def load_x(i):
        b = i % 2
        dst = xs[b][:]
        src = xview(xT, i)
        P.dma(XY_ENG, lambda e: e.dma_start(out=dst, in_=src), ("xld", b),
              reads=[], writes=X[b])

    def store_y(i):
        b = i % 2
        src = xs[b][:]
        dst = xview(yT, i)
        return P.dma(XY_ENG, lambda e: e.dma_start(out=dst, in_=src), ("yst", b),
                     reads=X[b], writes=[])

    load_x(0)
    P.dma("sp", lambda e: e.dma_start(out=cst[:], in_=cst_d), ("setup", 0), writes=[CONST])
    P.dma("sp", lambda e: e.dma_start(out=ident[:], in_=ident_d), ("setup", 1), writes=[DER])
    P.op("dve", lambda e: e.memset(gatew[:], 0.0), writes=[GATEW])
    P.op("dve", lambda e: e.memset(onesD[:], 1.0 / 1024.0), writes=[CONST])
    P.op("dve", lambda e: e.memset(onesC[:], 1.0 / 512.0), writes=[CONST])
    P.op("dve", lambda e: e.memset(ubuf[:, :, 0:30], 0.0), writes=UB[0:2])
    P.op("dve", lambda e: e.memset(ubb[:, :, 0:30], 0.0), writes=UB[2:4])
    P.op("dve", lambda e: e.memset(xbr[:, :, 0:3], 0.0), writes=XBR)
    P.op("dve", lambda e: e.memset(hst[:], 0.0), writes=HST)
    stg = [xs[1][:].rearrange("p a b -> p (a b)"), tmpt[:].rearrange("p a b -> p (a b)")]
    STG = [Buf("stg0"), Buf("stg1")]
    cvt = {"i": 0, "cast": 0, "slot": 0}

    stgG = stg[0][:, 0:1024].rearrange("p (g m) -> p g m", g=8)
    P.op("dve", lambda e: e.memset(stgG, 0.0), writes=[STG[0]])
    gtok = None
    for g, src in enumerate((lwa_d, lwx_d)):
        for hd in range(8):
            ch, half = hd // 2, hd % 2
            dst = stgG[half * 64:(half + 1) * 64, g * 4 + ch, half * 64:(half + 1) * 64]
            s_ap = src[hd]
            gtok = P.dma("sp", lambda e, dst=dst, s_ap=s_ap: e.dma_start(out=dst, in_=s_ap),
                         ("gwld", 0), reads=[], writes=[STG[0]] if gtok is None else [])
    STG[0].last_w = gtok
    STG[0].readers = []
    P.op("act", lambda e: e.activation(out=gatew[:], in_=stgG, func=AF.Copy), reads=[STG[0]], writes=[GATEW])

    def cv_job(srcs, slot, c0):
        k = cvt["i"] % 2
        cvt["i"] += 1
        off = 0
        tok = None
        for src, n, a_ in srcs:
            dstv = stg[k][:, off:off + n].rearrange("p (a b) -> p a b", a=a_)
            tok = P.dma("sp", lambda e, dstv=dstv, src=src: e.dma_start(out=dstv, in_=src), ("cvld", k),
                        reads=[], writes=[STG[k]] if tok is None else [])
            off += n
        STG[k].last_w = tok
        STG[k].readers = []
        dsts = ring[:, slot, c0:c0 + off]
        srcs_ = stg[k][:, 0:off]
        if cvt["cast"] % 2 == 0:
            P.op("act", lambda e: e.activation(out=dsts, in_=srcs_, func=AF.Copy), reads=[STG[k]], writes=[RING[slot]])
        else:
            P.op("dve", lambda e: e.tensor_copy(out=dsts, in_=srcs_), reads=[STG[k]], writes=[RING[slot]])
        cvt["cast"] += 1

    units = []
    for l in range(2):
        for u in range(11):
            sg_ = wg_d[l].rearrange("(kc p) (u j) -> u p kc j", p=128, j=256)[u]
            su_ = wu_d[l].rearrange("(kc p) (u j) -> u p kc j", p=128, j=256)[u]
            units.append(([[(sg_, 2048, 8), (su_, 2048, 8)]], gu_s[l][u], 4096))
        for u in range(8):
            sd_ = wd_d[l].rearrange("(kc p) (u j) -> u p kc j", p=128, j=128)[u]
            units.append(([[(sd_, 2816, 22)]], dd_s[l][u], 2816))
        if l == 0:
            for u in range(4):
                units.append(([[(win_d.rearrange("(kc p) (u j) -> u p kc j", p=128, j=512)[u], 4096, 8)]], win_s[u], 4096))
            for u in range(2):
                units.append(([[(wout_d.rearrange("(kc p) (u j) -> u p kc j", p=128, j=512)[u], 4096, 8)]], wout_s[u], 4096))
    cv_store_toks = {}

    def cv_store(slot, dram, U):
        cv_store_toks[slot] = P.dma("sp", lambda e: e.dma_start(out=dram, in_=ring[:, slot, 0:U]), ("cvst", slot),
                                    reads=[RING[slot]], writes=[])

    prev = None
    for jobs, dram, U in units:
        slot = cvt["slot"]
        cvt["slot"] = (slot + 1) % NS
        c0 = 0
        for srcs in jobs:
            cv_job(srcs, slot, c0)
            c0 += sum(n for _, n, _ in srcs)
        if prev is not None:
            cv_store(*prev)
        prev = (slot, dram, U)
    cv_store(*prev)

    cd_toks = []
    for j, ch in enumerate((2, 3)):
        hb = H[0] if j == 0 else H[8]
        base = 0 if j == 0 else 8
        stage = h[:, base:base + 8, :].rearrange("p a b -> p (a b)")[:, 0:3968]
        stage3 = stage.rearrange("p (k j) -> p k j", k=31)
        for k in range(31):
            c = C_CW + k * 4 + ch
            P.op("dve", lambda e, k=k, c=c, stage3=stage3: e.tensor_scalar(
                out=stage3[:, k, :], in0=ident[:], scalar1=col(c), scalar2=None, op0=ALU.mult),
                reads=[CONST, DER], writes=[hb] if k == 0 else [], inc=(k == 30))
        hb.last_w = ("dve", P.count["dve"])
        hb.readers = []
        cd_toks.append(P.dma("sp", lambda e, j=j, stage=stage: e.dma_start(out=cd_s[j], in_=stage),
                             ("cdst", j), reads=[hb], writes=[]))

    def dcol(a, n=4):
        return der[:, a:a + n]

    lam = cst[:, C_LAM:C_LAM + 4]
    dops = []

    def dv(fn, reads=(DER, CONST)):
        P.op("dve", fn, reads=list(reads), writes=[DER])

    dv(lambda e: e.tensor_scalar(out=dcol(12), in0=lam, scalar1=-1.0, scalar2=None, op0=ALU.mult))
    dv(lambda e: e.tensor_tensor(out=dcol(16), in0=lam, in1=dcol(12), op=ALU.max))
    P.op("act", lambda e: e.activation(out=dcol(20), in_=dcol(16), func=AF.Exp, scale=-1.0),
         reads=[DER], writes=[DER])
    dv(lambda e: e.tensor_scalar(out=dcol(24), in0=dcol(20), scalar1=2.0, scalar2=None, op0=ALU.add))
    dv(lambda e: e.reciprocal(out=dcol(24), in_=dcol(24)))
    dv(lambda e: e.tensor_tensor(out=dcol(24), in0=dcol(24), in1=dcol(20), op=ALU.mult))
    dv(lambda e: e.tensor_tensor(out=dcol(28), in0=dcol(24), in1=dcol(24), op=ALU.mult))
    dv(lambda e: e.tensor_scalar(out=dcol(20), in0=dcol(28), scalar1=1.0 / 15.0, scalar2=1.0 / 13.0,
                                 op0=ALU.mult, op1=ALU.add))
    for cc in (1.0 / 11.0, 1.0 / 9.0, 1.0 / 7.0, 1.0 / 5.0, 1.0 / 3.0, 1.0):
        dv(lambda e: e.tensor_tensor(out=dcol(20), in0=dcol(20), in1=dcol(28), op=ALU.mult))
        dv(lambda e, cc=cc: e.tensor_scalar(out=dcol(20), in0=dcol(20), scalar1=cc, scalar2=None, op0=ALU.add))
    dv(lambda e: e.tensor_tensor(out=dcol(20), in0=dcol(20), in1=dcol(24), op=ALU.mult))
    dv(lambda e: e.tensor_scalar(out=dcol(12), in0=dcol(12), scalar1=0.0, scalar2=None, op0=ALU.max))
    dv(lambda e: e.scalar_tensor_tensor(out=dcol(12), in0=dcol(20), scalar=2.0, in1=dcol(12),
                                        op0=ALU.mult, op1=ALU.add))
    dv(lambda e: e.tensor_scalar(out=dcol(0), in0=dcol(12), scalar1=-8.0, scalar2=None, op0=ALU.mult))
    dv(lambda e: e.tensor_scalar(out=dcol(4), in0=dcol(12), scalar1=-16.0, scalar2=None, op0=ALU.mult))
    dv(lambda e: e.tensor_scalar(out=dcol(8), in0=dcol(12), scalar1=8.0, scalar2=None, op0=ALU.mult))

    def rms(b, gcol, final, xn_t=None, XN_t=None, tmpf=None):
        tmpf = tmpf or tmp
        ps, psb = bank()
        for kc in range(KC):
            P.op("act", lambda e, kc=kc: e.activation(out=sq[:, kc, :], in_=xs[b][:, kc, :], func=AF.Square),
                 reads=[X[b][kc]], writes=[SQ[kc]])
        for kc in range(KC):
            P.op("pe", lambda e, kc=kc: e.matmul(ps, lhsT=onesD[:], rhs=sq[:, kc, :],
                                                 start=(kc == 0), stop=(kc == KC - 1)),
                 reads=[SQ[kc], CONST], writes=[psb], inc=(kc == KC - 1))
        t1, t1b = tmpf()
        P.op("act", lambda e: e.activation(out=t1, in_=ps, func=AF.Sqrt, bias=RMS_EPS, scale=1.0),
             reads=[psb], writes=[t1b])
        t2, t2b = tmpf()
        P.op("dve", lambda e: e.reciprocal(out=t2, in_=t1), reads=[t1b], writes=[t2b])
        for kc in range(KC):
            out = xs[b][:, kc, :] if final else xn_t[:, kc, :]
            P.op("dve", lambda e, kc=kc, out=out: e.scalar_tensor_tensor(
                out=out, in0=xs[b][:, kc, :], scalar=col(gcol + kc), in1=t2, op0=ALU.mult, op1=ALU.mult),
                reads=[X[b][kc], t2b, CONST], writes=[X[b][kc] if final else XN_t[kc]])

    def ffn(b, l, xn_t, XN_t, tmpf):
        for u in range(11):
            slot, sbuf_ = ring_load(gu_s[l][u], 4096, "gu%d" % l)
            sv = slot.rearrange("p (a k j) -> p a k j", a=2, k=8)
            for jj in range(2):
                f = 2 * u + jj
                pg, pgb = bank()
                pu, pub = bank()
                for a_, (pp, ppb) in enumerate(((pg, pgb), (pu, pub))):
                    for kc in range(KC):
                        P.op("pe", lambda e, a_=a_, kc=kc, pp=pp, sv=sv, jj=jj: e.matmul(
                            pp, lhsT=sv[:, a_, kc, jj * 128:(jj + 1) * 128], rhs=xn_t[:, kc, :],
                            start=(kc == 0), stop=(kc == KC - 1)),
                            reads=[sbuf_, XN_t[kc]], writes=[ppb], inc=(kc == KC - 1))
                t, tb = tmpf()
                P.op("act", lambda e, t=t, pg=pg: e.activation(out=t, in_=pg, func=AF.Silu),
                     reads=[pgb], writes=[tb])
                P.op("dve", lambda e, t=t, pu=pu, f=f: e.tensor_tensor(out=h[:, f, :], in0=t, in1=pu, op=ALU.mult),
                     reads=[tb, pub], writes=[H[f]])
            yield
        for m in range(8):
            slot, sbuf_ = ring_load(dd_s[l][m], 2816, "dd%d" % l)
            sv = slot.rearrange("p (k j) -> p k j", k=FC)
            pd, pdb = bank()
            for kc in range(FC):
                P.op("pe", lambda e, kc=kc, pd=pd, sv=sv: e.matmul(
                    pd, lhsT=sv[:, kc, :], rhs=h[:, kc, :], start=(kc == 0), stop=(kc == FC - 1)),
                    reads=[sbuf_, H[kc]], writes=[pdb], inc=(kc == FC - 1))
            P.op("dve", lambda e, pd=pd, m=m: e.scalar_tensor_tensor(
                out=xs[b][:, m, :], in0=pd, scalar=0.5, in1=xs[b][:, m, :], op0=ALU.mult, op1=ALU.add),
                reads=[pdb, X[b][m]], writes=[X[b][m]])
            yield

    def proj4(unit_ap, fam, evac):
        slot, sbuf_ = ring_load(unit_ap, 4096, fam)
        sv = slot.rearrange("p (k j) -> p k j", k=KC)
        for ch in range(4):
            ps, psb = bank()
            for kc in range(KC):
                P.op("pe", lambda e, kc=kc, ps=ps, sv=sv, ch=ch: e.matmul(
                    ps, lhsT=sv[:, kc, ch * 128:(ch + 1) * 128], rhs=xn2[:, kc, :],
                    start=(kc == 0), stop=(kc == KC - 1)),
                    reads=[sbuf_, XN2[kc]], writes=[psb], inc=(kc == KC - 1))
            evac(ch, ps, psb)

    def mixer(b):
        sg = [tmp() for _ in range(4)]

        def ev_gate(ch, ps, psb):
            P.op("act", lambda e: e.activation(out=sg[ch][0], in_=ps, func=AF.Sigmoid),
                 reads=[psb], writes=[sg[ch][1]])

        def ev_val(ch, ps, psb):
            dst_u = ubuf[:, ch, 30:30 + T] if ch < 2 else ubb[:, ch - 2, 30:30 + T]
            P.op("dve", lambda e: e.tensor_tensor(out=dst_u, in0=ps, in1=sg[ch][0], op=ALU.mult),
                 reads=[psb, sg[ch][1]], writes=[UB[ch]])

        def ev_rx(ch, ps, psb):
            P.op("act", lambda e: e.activation(out=xbr[:, ch, 3:3 + T], in_=ps, func=AF.Copy),
                 reads=[psb], writes=[XBR[ch]])

        def ev_rg(ch, ps, psb):
            P.op("act", lambda e: e.activation(out=glt[:, ch, :], in_=ps, func=AF.Gelu_apprx_tanh),
                 reads=[psb], writes=[GL[ch]])

        def conv4_part():
            for ch in range(4):
                P.op("dve", lambda e, ch=ch: e.tensor_scalar(
                    out=xr[:, ch, :], in0=xbr[:, ch, 0:T], scalar1=col(C_W4 + ch), scalar2=col(C_B4 + ch),
                    op0=ALU.mult, op1=ALU.add), reads=[XBR[ch], CONST], writes=[XR[ch]])
                for k in range(1, 4):
                    P.op("dve", lambda e, ch=ch, k=k: e.scalar_tensor_tensor(
                        out=xr[:, ch, :], in0=xbr[:, ch, k:k + T], scalar=col(C_W4 + k * 4 + ch), in1=xr[:, ch, :],
                        op0=ALU.mult, op1=ALU.add), reads=[XBR[ch], XR[ch], CONST], writes=[XR[ch]])
                P.op("act", lambda e, ch=ch: e.activation(out=xbr[:, ch, 0:3], in_=xbr[:, ch, T:T + 3], func=AF.Copy),
                     reads=[XBR[ch]], writes=[XBR[ch]])
                P.op("act", lambda e, ch=ch: e.activation(out=xrb[:, ch, :], in_=xr[:, ch, :], func=AF.Copy),
                     reads=[XR[ch]], writes=[XRB[ch]])

        proj4(win_s[1], "win", ev_gate)
        yield
        proj4(win_s[0], "win", ev_val)
        yield
        proj4(win_s[2], "win", ev_rx)
        yield
        proj4(win_s[3], "win", ev_rg)
        yield
        conv4_part()

        for j, ch in enumerate((2, 3)):
            slot, sbuf_ = ring_load(cd_s[j], 3968, "cd")
            sv = slot.rearrange("p (k j) -> p k j", k=31)
            ps, psb = bank()
            for k in range(31):
                P.op("pe", lambda e, k=k, ps=ps, sv=sv, j=j: e.matmul(
                    ps, lhsT=sv[:, k, :], rhs=ubb[:, j, k:k + T], start=(k == 0), stop=(k == 30)),
                    reads=[sbuf_, UB[ch]], writes=[psb], inc=(k == 30))
            P.op("act", lambda e, ps=ps, ch=ch: e.activation(out=v[:, ch, :], in_=ps, func=AF.Identity,
                                                             bias=col(C_CB + ch), scale=1.0),
                 reads=[psb, CONST], writes=[V[ch]])
            yield
        for k in range(31):
            acc, ACC = (v, V) if k % 2 == 0 else (hout, HOUT)
            for ch in range(2):
                if k == 0:
                    P.op("dve", lambda e, ch=ch: e.tensor_scalar(
                        out=v[:, ch, :], in0=ubuf[:, ch, 0:T], scalar1=col(C_CW + ch), scalar2=col(C_CB + ch),
                        op0=ALU.mult, op1=ALU.add), reads=[UB[ch], CONST], writes=[V[ch]], inc=(ch == 1))
                elif k == 1:
                    P.op("dve", lambda e, ch=ch: e.tensor_scalar(
                        out=hout[:, ch, :], in0=ubuf[:, ch, 1:1 + T], scalar1=col(C_CW + 4 + ch), scalar2=None,
                        op0=ALU.mult), reads=[UB[ch], CONST], writes=[HOUT[ch]], inc=(ch == 1))
                else:
                    P.op("dve", lambda e, ch=ch, k=k, acc=acc: e.scalar_tensor_tensor(
                        out=acc[:, ch, :], in0=ubuf[:, ch, k:k + T], scalar=col(C_CW + k * 4 + ch), in1=acc[:, ch, :],
                        op0=ALU.mult, op1=ALU.add), reads=[UB[ch], ACC[ch], CONST], writes=[ACC[ch]], inc=(ch == 1))
            if k % 4 == 3:
                yield
        for ch in range(2):
            P.op("dve", lambda e, ch=ch: e.tensor_tensor(out=v[:, ch, :], in0=v[:, ch, :], in1=hout[:, ch, :], op=ALU.add),
                 reads=[V[ch], HOUT[ch]], writes=[V[ch]])
        for ch in range(4):
            P.op("act", lambda e, ch=ch: e.activation(out=sq[:, 4 + ch, :], in_=v[:, ch, :], func=AF.Square),
                 reads=[V[ch]], writes=[SQ[4 + ch]])
            P.op("act", lambda e, ch=ch: e.activation(out=sq[:, ch, :], in_=v[:, ch, :], func=AF.Copy),
                 reads=[V[ch]], writes=[SQ[ch]])
            ub_ = ubuf[:, ch, :] if ch < 2 else ubb[:, ch - 2, :]
            P.op("act", lambda e, ub_=ub_: e.activation(out=ub_[:, 0:30], in_=ub_[:, T:T + 30], func=AF.Copy),
                 reads=[UB[ch]], writes=[UB[ch]])
        yield
        pm, pmb = bank()
        pe2, pe2b = bank()
        for ch in range(4):
            P.op("pe", lambda e, ch=ch: e.matmul(pm, lhsT=onesC[:], rhs=sq[:, ch, :], start=(ch == 0), stop=(ch == 3)),
                 reads=[SQ[ch], CONST], writes=[pmb], inc=(ch == 3))
        for ch in range(4):
            P.op("pe", lambda e, ch=ch: e.matmul(pe2, lhsT=onesC[:], rhs=sq[:, 4 + ch, :], start=(ch == 0), stop=(ch == 3)),
                 reads=[SQ[4 + ch], CONST], writes=[pe2b], inc=(ch == 3))
        yield
        mean, meanb = tmp()
        msq, msqb = tmp()
        P.op("act", lambda e: e.activation(out=mean, in_=pm, func=AF.Copy), reads=[pmb], writes=[meanb])
        P.op("act", lambda e: e.activation(out=msq, in_=pm, func=AF.Square), reads=[pmb], writes=[msqb])
        var, varb = tmp()
        P.op("dve", lambda e: e.tensor_tensor(out=var, in0=pe2, in1=msq, op=ALU.subtract),
             reads=[pe2b, msqb], writes=[varb])
        sd, sdb = tmp()
        P.op("act", lambda e: e.activation(out=sd, in_=var, func=AF.Sqrt, bias=LN_EPS, scale=1.0),
             reads=[varb], writes=[sdb])
        rstd, rstdb = tmp()
        P.op("dve", lambda e: e.reciprocal(out=rstd, in_=sd), reads=[sdb], writes=[rstdb])
        for ch in range(4):
            t1, t1b = tmp()
            P.op("dve", lambda e, ch=ch, t1=t1: e.tensor_tensor(out=t1, in0=v[:, ch, :], in1=mean, op=ALU.subtract),
                 reads=[V[ch], meanb], writes=[t1b])
            P.op("dve", lambda e, t1=t1: e.tensor_tensor(out=t1, in0=t1, in1=rstd, op=ALU.mult),
                 reads=[t1b, rstdb], writes=[t1b])
            P.op("act", lambda e, ch=ch, t1=t1: e.activation(out=mixo[:, ch, :], in_=t1, func=AF.Silu,
                                                             bias=col(C_LB + ch), scale=col(C_LG + ch)),
                 reads=[t1b, CONST], writes=[MIXO[ch]])

        yield
        for ch in range(4):
            pa, pab = bank()
            px, pxb = bank()
            P.op("pe", lambda e, ch=ch, pa=pa: e.matmul(pa, lhsT=gatew[:, ch, :], rhs=xrb[:, ch, :], start=True, stop=True),
                 reads=[GATEW, XRB[ch]], writes=[pab])
            P.op("pe", lambda e, ch=ch, px=px: e.matmul(px, lhsT=gatew[:, 4 + ch, :], rhs=xrb[:, ch, :], start=True, stop=True),
                 reads=[GATEW, XRB[ch]], writes=[pxb])
            P.op("act", lambda e, ch=ch, pa=pa: e.activation(out=v[:, ch, :], in_=pa, func=AF.Sigmoid,
                                                             bias=col(C_BA + ch), scale=1.0),
                 reads=[pab, CONST], writes=[V[ch]])
            P.op("act", lambda e, ch=ch, px=px: e.activation(out=igt[:, ch, :], in_=px, func=AF.Sigmoid,
                                                             bias=col(C_BX + ch), scale=1.0),
                 reads=[pxb, CONST], writes=[IG[ch]])
        yield
        for ch in range(4):
            P.op("dve", lambda e, ch=ch: e.tensor_tensor(out=igt[:, ch, :], in0=igt[:, ch, :], in1=xr[:, ch, :], op=ALU.mult),
                 reads=[IG[ch], XR[ch]], writes=[IG[ch]])
        yield
        for ch in range(4):
            P.op("act", lambda e, ch=ch: e.activation(out=xr[:, ch, :], in_=v[:, ch, :], func=AF.Exp, scale=der[:, ch:ch + 1]),
                 reads=[V[ch], DER], writes=[XR[ch]])
            P.op("act", lambda e, ch=ch: e.activation(out=hout[:, ch, :], in_=v[:, ch, :], func=AF.Tanh, scale=der[:, 8 + ch:9 + ch]),
                 reads=[V[ch], DER], writes=[HOUT[ch]])
            P.op("act", lambda e, ch=ch: e.activation(out=v[:, ch, :], in_=v[:, ch, :], func=AF.Exp, scale=der[:, 4 + ch:5 + ch]),
                 reads=[V[ch], DER], writes=[V[ch]])
        yield
        for ch in range(4):
            P.op("dve", lambda e, ch=ch: e.scalar_tensor_tensor(out=v[:, ch, :], in0=v[:, ch, :], scalar=1.0, in1=hout[:, ch, :],
                                                                op0=ALU.add, op1=ALU.mult),
                 reads=[V[ch], HOUT[ch]], writes=[V[ch]])
        yield
        for ch in range(4):
            P.op("act", lambda e, ch=ch: e.activation(out=v[:, ch, :], in_=v[:, ch, :], func=AF.Sqrt),
                 reads=[V[ch]], writes=[V[ch]])
        yield
        for ch in range(4):
            P.op("dve", lambda e, ch=ch: e.tensor_tensor(out=igt[:, ch, :], in0=igt[:, ch, :], in1=v[:, ch, :], op=ALU.mult),
                 reads=[IG[ch], V[ch]], writes=[IG[ch]])
            P.op("dve", lambda e, ch=ch: e.tensor_tensor_scan(
                out=hout[:, ch, :], data0=xr[:, ch, :], data1=igt[:, ch, :], initial=hst[:, ch:ch + 1], op0=ALU.mult, op1=ALU.add),
                reads=[XR[ch], IG[ch], HST[ch]], writes=[HOUT[ch]])
            P.op("act", lambda e, ch=ch: e.activation(out=hst[:, ch:ch + 1], in_=hout[:, ch, T - 1:T], func=AF.Copy),
                 reads=[HOUT[ch]], writes=[HST[ch]])
            P.op("dve", lambda e, ch=ch: e.tensor_tensor(out=mixo[:, 4 + ch, :], in0=hout[:, ch, :], in1=glt[:, ch, :], op=ALU.mult),
                 reads=[HOUT[ch], GL[ch]], writes=[MIXO[4 + ch]])

        yield
        yield
        for u in range(2):
            slot, sbuf_ = ring_load(wout_s[u], 4096, "wout")
            sv = slot.rearrange("p (k j) -> p k j", k=KC)
            for jj in range(4):
                m = 4 * u + jj
                ps, psb = bank()
                for kc in range(KC):
                    P.op("pe", lambda e, kc=kc, ps=ps, sv=sv, jj=jj: e.matmul(
                        ps, lhsT=sv[:, kc, jj * 128:(jj + 1) * 128], rhs=mixo[:, kc, :],
                        start=(kc == 0), stop=(kc == KC - 1)),
                        reads=[sbuf_, MIXO[kc]], writes=[psb], inc=(kc == KC - 1))
                P.op("dve", lambda e, ps=ps, m=m: e.tensor_tensor(out=xs[b][:, m, :], in0=ps, in1=xs[b][:, m, :], op=ALU.add),
                     reads=[psb, X[b][m]], writes=[X[b][m]])

    bar = [("act", P.count["act"]), ("dve", P.count["dve"])] + list(cv_store_toks.values()) + list(cd_toks)
    for e_ in ("pe", "act", "dve", "sp"):
        for t_ in bar:
            if t_[0] != e_ and t_[1] > 0:
                P.wait(e_, t_)

    def phaseA(i):
        b = i % 2
        rms(b, C_G1, False, xn, XN, tmp_a)
        yield
        yield from ffn(b, 0, xn, XN, tmp_a)

    def phaseB(i):
        b = i % 2
        rms(b, C_G2, False, xn2, XN2, tmp)
        yield
        yield from mixer(b)

    def phaseC(i, held):
        b = i % 2
        rms(b, C_G3, False, xn2, XN2, tmp)
        for _ in held:
            pass
        for _ in ffn(b, 1, xn2, XN2, tmp):
            pass
        rms(b, C_G4, True)
        return store_y(i)

    if NT > 1:
        load_x(1)
    for _ in phaseA(0):
        pass
    last_store = {}
    HOLD = 3
    for i in range(NT):
        steps = list(range(21)) if i + 1 < NT else []
        ga = phaseA(i + 1) if i + 1 < NT else iter(())
        given = 0
        for _ in phaseB(i):
            if given < len(steps) - HOLD:
                next(ga, None)
                given += 1
        while given < len(steps) - HOLD:
            next(ga, None)
            given += 1
        last_store[i % 2] = phaseC(i, ga)
        if i + 2 < NT:
            load_x(i + 2)
    for b in last_store:
        P.wait(XY_ENG, last_store[b])

    sems = {}

    def sem_of(key):
        if key not in sems:
            nm = key if isinstance(key, str) else "%s%d" % key
            sems[key] = nc.alloc_semaphore("s_" + nm)
        return sems[key]

    FQ = P.finalize()
    P.fq = FQ
    for e in Prog.ENG:
        for it in FQ[e]:
            if it[0] == "wait":
                sem_of(it[1])
            elif it[2] is not None:
                sem_of(it[2])

    def emit(eng, name):
        for it in FQ[name]:
            if it[0] == "wait":
                eng.wait_ge(sems[it[1]], it[2])
            else:
                ins = it[1](eng)
                if it[2] is not None:
                    ins.then_inc(sems[it[2]], it[3])

    with nc.Block() as block:
        @block.sync
        def _(e):
            emit(e, "sp")

        @block.scalar
        def _(e):
            emit(e, "act")

        @block.vector
        def _(e):
            emit(e, "dve")

        @block.gpsimd
        def _(e):
            emit(e, "pool")

        @block.tensor
        def _(e):
            emit(e, "pe")
    return nc


def _cols(vv, n):
    return np.ascontiguousarray(np.asarray(vv, np.float32).reshape(n, 128).T)


def kernel(x, ffn1_norm, ffn1_w_gate, ffn1_w_up, ffn1_w_down, mix_norm, w_in,
           conv_dw, conv_dw_bias, conv_ln_g, conv_ln_b, lru_conv_w, lru_conv_b,
           lru_w_a, lru_b_a, lru_w_x, lru_b_x, lru_lambda, w_out,
           ffn2_norm, ffn2_w_gate, ffn2_w_up, ffn2_w_down, final_norm):
    f = lambda a: np.ascontiguousarray(np.asarray(a, np.float32))
    x = f(x)
    cst = np.concatenate([
        _cols(ffn1_norm, 8), _cols(mix_norm, 8), _cols(ffn2_norm, 8), _cols(final_norm, 8),
        _cols(conv_dw_bias, 4), _cols(conv_ln_g, 4), _cols(conv_ln_b, 4),
        f(lru_conv_w).reshape(4, 4, 128).transpose(2, 0, 1).reshape(128, 16),
        _cols(lru_conv_b, 4), _cols(lru_b_a, 4), _cols(lru_b_x, 4), _cols(lru_lambda, 4),
        f(conv_dw).reshape(31, 4, 128).transpose(2, 0, 1).reshape(128, 124),
    ], axis=1)
    cst = np.ascontiguousarray(cst, np.float32)
    assert cst.shape == (128, NCST)
    common = {
        "cst": cst, "ident": np.eye(128, dtype=np.float32),
        "wg1": f(ffn1_w_gate), "wu1": f(ffn1_w_up), "wd1": f(ffn1_w_down),
        "wg2": f(ffn2_w_gate), "wu2": f(ffn2_w_up), "wd2": f(ffn2_w_down),
        "win": f(w_in), "wout": f(w_out), "lwa": f(lru_w_a), "lwx": f(lru_w_x),
    }
    in_maps = []
    for c in range(NCORES):
        m = dict(common)
        m["xT"] = np.ascontiguousarray(x[c].T)
        in_maps.append(m)
    nc = build_nc()
    res = run_bass_kernel_spmd(nc, in_maps, core_ids=list(range(NCORES)))
    out = np.empty((NCORES, S, D), np.float32)
    for c in range(NCORES):
        out[c] = np.asarray(res.results[c]["yT"]).T
    return out
```

```python
import numpy as np
import concourse.bass as bass
import concourse.mybir as mybir
from concourse.bass_utils import run_bass_kernel_spmd

F32 = mybir.dt.float32
BF16 = mybir.dt.bfloat16
AF = mybir.ActivationFunctionType
ALU = mybir.AluOpType

D = 1024
S = 8192
T = 512
NT = S // T
DFF = 2816
KC = D // 128
FC = DFF // 128
NCORES = 8
NS = 4
SLOT = 4096
NTMP = 8
XY_ENG = "sp"
RMS_EPS = 1e-6
LN_EPS = 1e-5

C_G1, C_G2, C_G3, C_G4 = 0, 8, 16, 24
C_CB, C_LG, C_LB = 32, 36, 40
C_W4, C_B4 = 44, 60
C_BA, C_BX, C_LAM = 64, 68, 72
C_CW = 76
NCST = 200


class Buf:
    __slots__ = ("name", "last_w", "readers")

    def __init__(self, name):
        self.name = name
        self.last_w = None
        self.readers = []


class Prog:
    ENG = ("pe", "act", "dve", "pool", "sp")

    def __init__(self):
        self.q = {e: [] for e in self.ENG}
        self.count = {e: 0 for e in self.ENG}
        self.seen = {e: {} for e in self.ENG}
        self.dcount = {}
        self.need = {e: set() for e in self.ENG}

    def _deps(self, eng, reads, writes, is_dma):
        deps = []
        for b in reads:
            if b.last_w is not None:
                deps.append((b.last_w, True))
        for b in writes:
            if is_dma and b.readers:
                for t in b.readers:
                    deps.append((t, False))
                continue
            if b.last_w is not None:
                deps.append((b.last_w, False))
            for t in b.readers:
                deps.append((t, False))
        for (key, val), raw in deps:
            if key == eng:
                if eng in ("pe", "sp"):
                    continue
                pass
            if self.seen[eng].get(key, 0) >= val:
                continue
            self.seen[eng][key] = val
            if key in self.need:
                self.need[key].add(val)
            self.q[eng].append(("wait", key, val))

    def _commit(self, tok, reads, writes):
        for b in writes:
            b.last_w = tok
            b.readers = []
        for b in reads:
            if b in writes:
                continue
            rs = [t for t in b.readers if t[0] != tok[0]]
            rs.append(tok)
            b.readers = rs

    def op(self, eng, fn, reads=(), writes=(), inc=True):
        self._deps(eng, reads, writes, False)
        if inc:
            self.count[eng] += 1
            tok = (eng, self.count[eng])
            self.q[eng].append(("op", fn, eng, self.count[eng]))
        else:
            tok = (eng, self.count[eng] + 1)
            self.q[eng].append(("op", fn, None, 0))
        self._commit(tok, reads, writes)
        return tok

    def dma(self, eng, fn, key, reads=(), writes=()):
        self._deps(eng, reads, writes, True)
        self.dcount[key] = self.dcount.get(key, 0) + 16
        tok = (key, self.dcount[key])
        self.q[eng].append(("op", fn, key, 16))
        self._commit(tok, reads, writes)
        return tok

    def wait(self, eng, tok):
        key, val = tok
        if self.seen[eng].get(key, 0) >= val:
            return
        self.seen[eng][key] = val
        if key in self.need:
            self.need[key].add(val)
        self.q[eng].append(("wait", key, val))

    def finalize(self):
        rank = {}
        for e in self.ENG:
            rank[e] = {idx: r + 1 for r, idx in enumerate(sorted(self.need[e]))}
        out = {}
        for e in self.ENG:
            lst = []
            for it in self.q[e]:
                if it[0] == "wait":
                    key, val = it[1], it[2]
                    if key in rank:
                        lst.append(("wait", key, rank[key][val]))
                    else:
                        lst.append(it)
                else:
                    _, fn, key, amt = it
                    if key in rank:
                        if amt in rank[key]:
                            lst.append(("op", fn, key, 1))
                        else:
                            lst.append(("op", fn, None, 0))
                    else:
                        lst.append(it)
            out[e] = lst
        return out


_LAST_PROG = None


def build_nc(S=S):
    NT = S // T
    nc = bass.Bass("TRN2", target_bir_lowering=False)
    P = Prog()
    global _LAST_PROG
    _LAST_PROG = P

    def din(name, shape):
        return nc.dram_tensor(name, list(shape), F32, kind="ExternalInput").ap()

    xT = din("xT", [D, S])
    yT = nc.dram_tensor("yT", [D, S], F32, kind="ExternalOutput").ap()
    cst_d = din("cst", [128, NCST])
    ident_d = din("ident", [128, 128])
    wg_d = [din("wg1", [D, DFF]), din("wg2", [D, DFF])]
    wu_d = [din("wu1", [D, DFF]), din("wu2", [D, DFF])]
    wd_d = [din("wd1", [DFF, D]), din("wd2", [DFF, D])]
    win_d = din("win", [D, 2048])
    wout_d = din("wout", [D, D])
    lwa_d = din("lwa", [8, 64, 64])
    lwx_d = din("lwx", [8, 64, 64])

    def dscr(name, shape):
        return nc.dram_tensor(name, list(shape), BF16, kind="Internal").ap()

    gu_s = [dscr("gu1s", [11, 128, 4096]), dscr("gu2s", [11, 128, 4096])]
    dd_s = [dscr("dd1s", [8, 128, 2816]), dscr("dd2s", [8, 128, 2816])]
    win_s = dscr("wins", [4, 128, 4096])
    wout_s = dscr("wouts", [2, 128, 4096])
    cd_s = dscr("cds", [2, 128, 3968])

    sb = nc.alloc_sbuf_tensor
    cst = sb("cst_sb", [128, NCST], F32)
    der = sb("der_sb", [128, 32], F32)
    ident = sb("ident_sb", [128, 128], F32)
    onesD = sb("onesD", [128, 128], BF16)
    onesC = sb("onesC", [128, 128], BF16)
    gatew = sb("gatew", [128, 8, 128], BF16)
    xs = [sb("x0", [128, KC, T], F32), sb("x1", [128, KC, T], F32)]
    xn = sb("xn", [128, KC, T], BF16)
    xn2 = sb("xn2", [128, KC, T], BF16)
    tmpa = sb("tmpa", [128, 4, T], F32)
    sq = sb("sq", [128, 8, T], BF16)
    h = sb("h", [128, FC, T], BF16)
    ring = sb("ring", [128, NS, SLOT], BF16)
    ubuf = sb("ubuf", [128, 2, 30 + T], F32)
    ubb = sb("ubb", [128, 2, 30 + T], BF16)
    v = sb("v", [128, 4, T], F32)
    xbr = sb("xbr", [128, 4, 3 + T], F32)
    xr = sb("xr", [128, 4, T], F32)
    xrb = sb("xrb", [128, 4, T], BF16)
    hout = sb("hout", [128, 4, T], F32)
    hst = sb("hst", [128, 4], F32)
    mixo = sb("mixo", [128, KC, T], BF16)
    tmpt = sb("tmpt", [128, NTMP, T], F32)
    glt = sb("glt", [128, 4, T], F32)
    igt = sb("igt", [128, 4, T], F32)
    banks = [nc.alloc_psum_tensor("bank%d" % i, [128, T], F32) for i in range(8)]

    CONST = Buf("const")
    DER = Buf("der")
    GATEW = Buf("gatew")
    X = [[Buf("x%d_%d" % (b, k)) for k in range(KC)] for b in range(2)]
    XN = [Buf("xn%d" % k) for k in range(KC)]
    XN2 = [Buf("xn2_%d" % k) for k in range(KC)]
    TMPA = [Buf("tmpa%d" % k) for k in range(4)]
    SQ = [Buf("sq%d" % k) for k in range(8)]
    H = [Buf("h%d" % k) for k in range(FC)]
    RING = [Buf("ring%d" % k) for k in range(NS)]
    UB = [Buf("ub%d" % k) for k in range(4)]
    V = [Buf("v%d" % k) for k in range(4)]
    XBR = [Buf("xbr%d" % k) for k in range(4)]
    XR = [Buf("xr%d" % k) for k in range(4)]
    XRB = [Buf("xrb%d" % k) for k in range(4)]
    HOUT = [Buf("hout%d" % k) for k in range(4)]
    HST = [Buf("hst%d" % k) for k in range(4)]
    MIXO = [Buf("mixo%d" % k) for k in range(KC)]
    TMP = [Buf("tmp%d" % k) for k in range(NTMP)]
    GL = [Buf("gl%d" % k) for k in range(4)]
    IG = [Buf("ig%d" % k) for k in range(4)]
    BANK = [Buf("bank%d" % k) for k in range(8)]
    FAM = {n: Buf(n) for n in ("gu0", "gu1", "dd0", "dd1", "win", "wout", "cd")}

    st = {"bank": 0, "tmp": 0, "ring": 0}

    def bank():
        i = st["bank"]
        st["bank"] = (i + 1) % 8
        return banks[i][:], BANK[i]

    def tmp():
        i = st["tmp"]
        st["tmp"] = (i + 1) % NTMP
        return tmpt[:, i, :], TMP[i]

    def tmp_a():
        i = st.get("tmpa", 0)
        st["tmpa"] = (i + 1) % 4
        return tmpa[:, i, :], TMPA[i]

    def col(c):
        return cst[:, c:c + 1]

    ring_pre = []

    def ring_issue(src, U, fam):
        s = st["ring"]
        st["ring"] = (s + 1) % NS
        dst = ring[:, s, 0:U]
        P.dma("sp", lambda e: e.dma_start(out=dst, in_=src), ("ring", s),
              reads=[FAM[fam]], writes=[RING[s]])
        return ring[:, s, 0:U], RING[s]

    def ring_prefetch(key, src, U, fam):
        ring_pre.append((key, ring_issue(src, U, fam)))

    def ring_load(src, U, fam, key=None):
        if key is not None and ring_pre and ring_pre[0][0] == key:
            return ring_pre.pop(0)[1]
        assert key is None or all(k_ != key for k_, _ in ring_pre)
        return ring_issue(src, U, fam)

    def xview(ap, i):
        return ap.rearrange("(kc p) t -> p kc t", p=128)[:, :, i * T:(i + 1) * T]

    def load_x(i):
        b = i % 2
        dst = xs[b][:]
        src = xview(xT, i)
        P.dma(XY_ENG, lambda e: e.dma_start(out=dst, in_=src), ("xld", b),
              reads=[], writes=X[b])

    def store_y(i):
        b = i % 2
        src = xs[b][:]
        dst = xview(yT, i)
        return P.dma(XY_ENG, lambda e: e.dma_start(out=dst, in_=src), ("yst", b),
                     reads=X[b], writes=[])

    load_x(0)
    P.dma("sp", lambda e: e.dma_start(out=cst[:], in_=cst_d), ("setup", 0), writes=[CONST])
    P.dma("sp", lambda e: e.dma_start(out=ident[:], in_=ident_d), ("setup", 1), writes=[DER])
    P.op("dve", lambda e: e.memset(gatew[:], 0.0), writes=[GATEW])
    P.op("dve", lambda e: e.memset(onesD[:], 1.0 / 1024.0), writes=[CONST])
    P.op("dve", lambda e: e.memset(onesC[:], 1.0 / 512.0), writes=[CONST])
    P.op("dve", lambda e: e.memset(ubuf[:, :, 0:30], 0.0), writes=UB[0:2])
    P.op("dve", lambda e: e.memset(ubb[:, :, 0:30], 0.0), writes=UB[2:4])
    P.op("dve", lambda e: e.memset(xbr[:, :, 0:3], 0.0), writes=XBR)
    P.op("dve", lambda e: e.memset(hst[:], 0.0), writes=HST)
    stg = [xs[1][:].rearrange("p a b -> p (a b)"), tmpt[:].rearrange("p a b -> p (a b)")]
    STG = [Buf("stg0"), Buf("stg1")]
    cvt = {"i": 0, "cast": 0, "slot": 0}

    stgG = stg[0][:, 0:1024].rearrange("p (g m) -> p g m", g=8)
    P.op("dve", lambda e: e.memset(stgG, 0.0), writes=[STG[0]])
    gtok = None
    for g, src in enumerate((lwa_d, lwx_d)):
        for hd in range(8):
            ch, half = hd // 2, hd % 2
            dst = stgG[half * 64:(half + 1) * 64, g * 4 + ch, half * 64:(half + 1) * 64]
            s_ap = src[hd]
            gtok = P.dma("sp", lambda e, dst=dst, s_ap=s_ap: e.dma_start(out=dst, in_=s_ap),
                         ("gwld", 0), reads=[], writes=[STG[0]] if gtok is None else [])
    STG[0].last_w = gtok
    STG[0].readers = []
    P.op("act", lambda e: e.activation(out=gatew[:], in_=stgG, func=AF.Copy), reads=[STG[0]], writes=[GATEW])

    def cv_job(srcs, slot, c0):
        k = cvt["i"] % 2
        cvt["i"] += 1
        off = 0
        tok = None
        for src, n, a_ in srcs:
            dstv = stg[k][:, off:off + n].rearrange("p (a b) -> p a b", a=a_)
            tok = P.dma("sp", lambda e, dstv=dstv, src=src: e.dma_start(out=dstv, in_=src), ("cvld", k),
                        reads=[], writes=[STG[k]] if tok is None else [])
            off += n
        STG[k].last_w = tok
        STG[k].readers = []
        dsts = ring[:, slot, c0:c0 + off]
        srcs_ = stg[k][:, 0:off]
        if cvt["cast"] % 2 == 0:
            P.op("act", lambda e: e.activation(out=dsts, in_=srcs_, func=AF.Copy), reads=[STG[k]], writes=[RING[slot]])
        else:
            P.op("dve", lambda e: e.tensor_copy(out=dsts, in_=srcs_), reads=[STG[k]], writes=[RING[slot]])
        cvt["cast"] += 1

    units = []
    for l in range(2):
        for u in range(11):
            sg_ = wg_d[l].rearrange("(kc p) (u j) -> u p kc j", p=128, j=256)[u]
            su_ = wu_d[l].rearrange("(kc p) (u j) -> u p kc j", p=128, j=256)[u]
            units.append(([[(sg_, 2048, 8), (su_, 2048, 8)]], gu_s[l][u], 4096))
        for u in range(8):
            sd_ = wd_d[l].rearrange("(kc p) (u j) -> u p kc j", p=128, j=128)[u]
            units.append(([[(sd_, 2816, 22)]], dd_s[l][u], 2816))
        if l == 0:
            for u in range(4):
                units.append(([[(win_d.rearrange("(kc p) (u j) -> u p kc j", p=128, j=512)[u], 4096, 8)]], win_s[u], 4096))
            for u in range(2):
                units.append(([[(wout_d.rearrange("(kc p) (u j) -> u p kc j", p=128, j=512)[u], 4096, 8)]], wout_s[u], 4096))
    cv_store_toks = {}

    def cv_store(slot, dram, U):
        cv_store_toks[slot] = P.dma("sp", lambda e: e.dma_start(out=dram, in_=ring[:, slot, 0:U]), ("cvst", slot),
                                    reads=[RING[slot]], writes=[])

    prev = None
    for jobs, dram, U in units:
        slot = cvt["slot"]
        cvt["slot"] = (slot + 1) % NS
        c0 = 0
        for srcs in jobs:
            cv_job(srcs, slot, c0)
            c0 += sum(n for _, n, _ in srcs)
        if prev is not None:
            cv_store(*prev)
        prev = (slot, dram, U)
    cv_store(*prev)

    cd_toks = []
    for j, ch in enumerate((2, 3)):
        hb = H[0] if j == 0 else H[8]
        base = 0 if j == 0 else 8
        stage = h[:, base:base + 8, :].rearrange("p a b -> p (a b)")[:, 0:3968]
        stage3 = stage.rearrange("p (k j) -> p k j", k=31)
        for k in range(31):
            c = C_CW + k * 4 + ch
            P.op("dve", lambda e, k=k, c=c, stage3=stage3: e.tensor_scalar(
                out=stage3[:, k, :], in0=ident[:], scalar1=col(c), scalar2=None, op0=ALU.mult),
                reads=[CONST, DER], writes=[hb] if k == 0 else [], inc=(k == 30))
        hb.last_w = ("dve", P.count["dve"])
        hb.readers = []
        cd_toks.append(P.dma("sp", lambda e, j=j, stage=stage: e.dma_start(out=cd_s[j], in_=stage),
                             ("cdst", j), reads=[hb], writes=[]))

    def dcol(a, n=4):
        return der[:, a:a + n]

    lam = cst[:, C_LAM:C_LAM + 4]
    dops = []

    def dv(fn, reads=(DER, CONST)):
        P.op("dve", fn, reads=list(reads), writes=[DER])

    dv(lambda e: e.tensor_scalar(out=dcol(12), in0=lam, scalar1=-1.0, scalar2=None, op0=ALU.mult))
    dv(lambda e: e.tensor_tensor(out=dcol(16), in0=lam, in1=dcol(12), op=ALU.max))
    P.op("act", lambda e: e.activation(out=dcol(20), in_=dcol(16), func=AF.Exp, scale=-1.0),
         reads=[DER], writes=[DER])
    dv(lambda e: e.tensor_scalar(out=dcol(24), in0=dcol(20), scalar1=2.0, scalar2=None, op0=ALU.add))
    dv(lambda e: e.reciprocal(out=dcol(24), in_=dcol(24)))
    dv(lambda e: e.tensor_tensor(out=dcol(24), in0=dcol(24), in1=dcol(20), op=ALU.mult))
    dv(lambda e: e.tensor_tensor(out=dcol(28), in0=dcol(24), in1=dcol(24), op=ALU.mult))
    dv(lambda e: e.tensor_scalar(out=dcol(20), in0=dcol(28), scalar1=1.0 / 15.0, scalar2=1.0 / 13.0,
                                 op0=ALU.mult, op1=ALU.add))
    for cc in (1.0 / 11.0, 1.0 / 9.0, 1.0 / 7.0, 1.0 / 5.0, 1.0 / 3.0, 1.0):
        dv(lambda e: e.tensor_tensor(out=dcol(20), in0=dcol(20), in1=dcol(28), op=ALU.mult))
        dv(lambda e, cc=cc: e.tensor_scalar(out=dcol(20), in0=dcol(20), scalar1=cc, scalar2=None, op0=ALU.add))
    dv(lambda e: e.tensor_tensor(out=dcol(20), in0=dcol(20), in1=dcol(24), op=ALU.mult))
    dv(lambda e: e.tensor_scalar(out=dcol(12), in0=dcol(12), scalar1=0.0, scalar2=None, op0=ALU.max))
    dv(lambda e: e.scalar_tensor_tensor(out=dcol(12), in0=dcol(20), scalar=2.0, in1=dcol(12),
                                        op0=ALU.mult, op1=ALU.add))
    dv(lambda e: e.tensor_scalar(out=dcol(0), in0=dcol(12), scalar1=-8.0, scalar2=None, op0=ALU.mult))
    dv(lambda e: e.tensor_scalar(out=dcol(4), in0=dcol(12), scalar1=-16.0, scalar2=None, op0=ALU.mult))
    dv(lambda e: e.tensor_scalar(out=dcol(8), in0=dcol(12), scalar1=8.0, scalar2=None, op0=ALU.mult))

    def rms(b, gcol, final, xn_t=None, XN_t=None, tmpf=None):
        tmpf = tmpf or tmp
        ps, psb = bank()
        for kc in range(KC):
            P.op("act", lambda e, kc=kc: e.activation(out=sq[:, kc, :], in_=xs[b][:, kc, :], func=AF.Square),
                 reads=[X[b][kc]], writes=[SQ[kc]])
        for kc in range(KC):
            P.op("pe", lambda e, kc=kc: e.matmul(ps, lhsT=onesD[:], rhs=sq[:, kc, :],
                                                 start=(kc == 0), stop=(kc == KC - 1)),
                 reads=[SQ[kc], CONST], writes=[psb], inc=(kc == KC - 1))
        t1, t1b = tmpf()
        P.op("act", lambda e: e.activation(out=t1, in_=ps, func=AF.Sqrt, bias=RMS_EPS, scale=1.0),
             reads=[psb], writes=[t1b])
        t2, t2b = tmpf()
        P.op("dve", lambda e: e.reciprocal(out=t2, in_=t1), reads=[t1b], writes=[t2b])
        for kc in range(KC):
            out = xs[b][:, kc, :] if final else xn_t[:, kc, :]
            P.op("dve", lambda e, kc=kc, out=out: e.scalar_tensor_tensor(
                out=out, in0=xs[b][:, kc, :], scalar=col(gcol + kc), in1=t2, op0=ALU.mult, op1=ALU.mult),
                reads=[X[b][kc], t2b, CONST], writes=[X[b][kc] if final else XN_t[kc]])

    def ffn(b, l, xn_t, XN_t, tmpf):
        for u in range(11):
            slot, sbuf_ = ring_load(gu_s[l][u], 4096, "gu%d" % l)
            sv = slot.rearrange("p (a k j) -> p a k j", a=2, k=8)
            for jj in range(2):
                f = 2 * u + jj
                pg, pgb = bank()
                pu, pub = bank()
                for a_, (pp, ppb) in enumerate(((pg, pgb), (pu, pub))):
                    for kc in range(KC):
                        P.op("pe", lambda e, a_=a_, kc=kc, pp=pp, sv=sv, jj=jj: e.matmul(
                            pp, lhsT=sv[:, a_, kc, jj * 128:(jj + 1) * 128], rhs=xn_t[:, kc, :],
                            start=(kc == 0), stop=(kc == KC - 1)),
                            reads=[sbuf_, XN_t[kc]], writes=[ppb], inc=(kc == KC - 1))
                t, tb = tmpf()
                P.op("act", lambda e, t=t, pg=pg: e.activation(out=t, in_=pg, func=AF.Silu),
                     reads=[pgb], writes=[tb])
                P.op("dve", lambda e, t=t, pu=pu, f=f: e.tensor_tensor(out=h[:, f, :], in0=t, in1=pu, op=ALU.mult),
                     reads=[tb, pub], writes=[H[f]])
            yield
        for m in range(8):
            slot, sbuf_ = ring_load(dd_s[l][m], 2816, "dd%d" % l)
            sv = slot.rearrange("p (k j) -> p k j", k=FC)
            pd, pdb = bank()
            for kc in range(FC):
                P.op("pe", lambda e, kc=kc, pd=pd, sv=sv: e.matmul(
                    pd, lhsT=sv[:, kc, :], rhs=h[:, kc, :], start=(kc == 0), stop=(kc == FC - 1)),
                    reads=[sbuf_, H[kc]], writes=[pdb], inc=(kc == FC - 1))
            P.op("dve", lambda e, pd=pd, m=m: e.scalar_tensor_tensor(
                out=xs[b][:, m, :], in0=pd, scalar=0.5, in1=xs[b][:, m, :], op0=ALU.mult, op1=ALU.add),
                reads=[pdb, X[b][m]], writes=[X[b][m]])
            yield

    def proj4(unit_ap, fam, evac, key=None):
        slot, sbuf_ = ring_load(unit_ap, 4096, fam, key=key)
        sv = slot.rearrange("p (k j) -> p k j", k=KC)
        for ch in range(4):
            ps, psb = bank()
            for kc in range(KC):
                P.op("pe", lambda e, kc=kc, ps=ps, sv=sv, ch=ch: e.matmul(
                    ps, lhsT=sv[:, kc, ch * 128:(ch + 1) * 128], rhs=xn2[:, kc, :],
                    start=(kc == 0), stop=(kc == KC - 1)),
                    reads=[sbuf_, XN2[kc]], writes=[psb], inc=(kc == KC - 1))
            evac(ch, ps, psb)

    def mixer(b):
        sg = [tmp() for _ in range(4)]

        def ev_gate(ch, ps, psb):
            P.op("act", lambda e: e.activation(out=sg[ch][0], in_=ps, func=AF.Sigmoid),
                 reads=[psb], writes=[sg[ch][1]])

        def ev_val(ch, ps, psb):
            dst_u = ubuf[:, ch, 30:30 + T] if ch < 2 else ubb[:, ch - 2, 30:30 + T]
            P.op("dve", lambda e: e.tensor_tensor(out=dst_u, in0=ps, in1=sg[ch][0], op=ALU.mult),
                 reads=[psb, sg[ch][1]], writes=[UB[ch]])

        def ev_rx(ch, ps, psb):
            P.op("act", lambda e: e.activation(out=xbr[:, ch, 3:3 + T], in_=ps, func=AF.Copy),
                 reads=[psb], writes=[XBR[ch]])

        def ev_rg(ch, ps, psb):
            P.op("act", lambda e: e.activation(out=glt[:, ch, :], in_=ps, func=AF.Gelu_apprx_tanh),
                 reads=[psb], writes=[GL[ch]])

        def conv4_part():
            for ch in range(4):
                P.op("dve", lambda e, ch=ch: e.tensor_scalar(
                    out=xr[:, ch, :], in0=xbr[:, ch, 0:T], scalar1=col(C_W4 + ch), scalar2=col(C_B4 + ch),
                    op0=ALU.mult, op1=ALU.add), reads=[XBR[ch], CONST], writes=[XR[ch]])
                for k in range(1, 4):
                    P.op("dve", lambda e, ch=ch, k=k: e.scalar_tensor_tensor(
                        out=xr[:, ch, :], in0=xbr[:, ch, k:k + T], scalar=col(C_W4 + k * 4 + ch), in1=xr[:, ch, :],
                        op0=ALU.mult, op1=ALU.add), reads=[XBR[ch], XR[ch], CONST], writes=[XR[ch]])
                P.op("act", lambda e, ch=ch: e.activation(out=xbr[:, ch, 0:3], in_=xbr[:, ch, T:T + 3], func=AF.Copy),
                     reads=[XBR[ch]], writes=[XBR[ch]])
                P.op("act", lambda e, ch=ch: e.activation(out=xrb[:, ch, :], in_=xr[:, ch, :], func=AF.Copy),
                     reads=[XR[ch]], writes=[XRB[ch]])

        proj4(win_s[1], "win", ev_gate, key="win1")
        yield
        proj4(win_s[0], "win", ev_val, key="win0")
        yield
        proj4(win_s[2], "win", ev_rx, key="win2")
        yield
        proj4(win_s[3], "win", ev_rg)
        yield
        conv4_part()

        for j, ch in enumerate((2, 3)):
            slot, sbuf_ = ring_load(cd_s[j], 3968, "cd")
            sv = slot.rearrange("p (k j) -> p k j", k=31)
            ps, psb = bank()
            for k in range(31):
                P.op("pe", lambda e, k=k, ps=ps, sv=sv, j=j: e.matmul(
                    ps, lhsT=sv[:, k, :], rhs=ubb[:, j, k:k + T], start=(k == 0), stop=(k == 30)),
                    reads=[sbuf_, UB[ch]], writes=[psb], inc=(k == 30))
            P.op("act", lambda e, ps=ps, ch=ch: e.activation(out=v[:, ch, :], in_=ps, func=AF.Identity,
                                                             bias=col(C_CB + ch), scale=1.0),
                 reads=[psb, CONST], writes=[V[ch]])
            yield
        for k in range(31):
            acc, ACC = (v, V) if k % 2 == 0 else (hout, HOUT)
            for ch in range(2):
                if k == 0:
                    P.op("dve", lambda e, ch=ch: e.tensor_scalar(
                        out=v[:, ch, :], in0=ubuf[:, ch, 0:T], scalar1=col(C_CW + ch), scalar2=col(C_CB + ch),
                        op0=ALU.mult, op1=ALU.add), reads=[UB[ch], CONST], writes=[V[ch]], inc=(ch == 1))
                elif k == 1:
                    P.op("dve", lambda e, ch=ch: e.tensor_scalar(
                        out=hout[:, ch, :], in0=ubuf[:, ch, 1:1 + T], scalar1=col(C_CW + 4 + ch), scalar2=None,
                        op0=ALU.mult), reads=[UB[ch], CONST], writes=[HOUT[ch]], inc=(ch == 1))
                else:
                    P.op("dve", lambda e, ch=ch, k=k, acc=acc: e.scalar_tensor_tensor(
                        out=acc[:, ch, :], in0=ubuf[:, ch, k:k + T], scalar=col(C_CW + k * 4 + ch), in1=acc[:, ch, :],
                        op0=ALU.mult, op1=ALU.add), reads=[UB[ch], ACC[ch], CONST], writes=[ACC[ch]], inc=(ch == 1))
            if k % 4 == 3:
                yield
        for ch in range(2):
            P.op("dve", lambda e, ch=ch: e.tensor_tensor(out=v[:, ch, :], in0=v[:, ch, :], in1=hout[:, ch, :], op=ALU.add),
                 reads=[V[ch], HOUT[ch]], writes=[V[ch]])
        for ch in range(4):
            P.op("act", lambda e, ch=ch: e.activation(out=sq[:, 4 + ch, :], in_=v[:, ch, :], func=AF.Square),
                 reads=[V[ch]], writes=[SQ[4 + ch]])
            P.op("act", lambda e, ch=ch: e.activation(out=sq[:, ch, :], in_=v[:, ch, :], func=AF.Copy),
                 reads=[V[ch]], writes=[SQ[ch]])
            ub_ = ubuf[:, ch, :] if ch < 2 else ubb[:, ch - 2, :]
            P.op("act", lambda e, ub_=ub_: e.activation(out=ub_[:, 0:30], in_=ub_[:, T:T + 30], func=AF.Copy),
                 reads=[UB[ch]], writes=[UB[ch]])
        yield
        pm, pmb = bank()
        pe2, pe2b = bank()
        for ch in range(4):
            P.op("pe", lambda e, ch=ch: e.matmul(pm, lhsT=onesC[:], rhs=sq[:, ch, :], start=(ch == 0), stop=(ch == 3)),
                 reads=[SQ[ch], CONST], writes=[pmb], inc=(ch == 3))
        for ch in range(4):
            P.op("pe", lambda e, ch=ch: e.matmul(pe2, lhsT=onesC[:], rhs=sq[:, 4 + ch, :], start=(ch == 0), stop=(ch == 3)),
                 reads=[SQ[4 + ch], CONST], writes=[pe2b], inc=(ch == 3))
        yield
        mean, meanb = tmp()
        msq, msqb = tmp()
        P.op("act", lambda e: e.activation(out=mean, in_=pm, func=AF.Copy), reads=[pmb], writes=[meanb])
        P.op("act", lambda e: e.activation(out=msq, in_=pm, func=AF.Square), reads=[pmb], writes=[msqb])
        var, varb = tmp()
        P.op("dve", lambda e: e.tensor_tensor(out=var, in0=pe2, in1=msq, op=ALU.subtract),
             reads=[pe2b, msqb], writes=[varb])
        sd, sdb = tmp()
        P.op("act", lambda e: e.activation(out=sd, in_=var, func=AF.Sqrt, bias=LN_EPS, scale=1.0),
             reads=[varb], writes=[sdb])
        rstd, rstdb = tmp()
        P.op("dve", lambda e: e.reciprocal(out=rstd, in_=sd), reads=[sdb], writes=[rstdb])
        for ch in range(4):
            t1, t1b = tmp()
            P.op("dve", lambda e, ch=ch, t1=t1: e.tensor_tensor(out=t1, in0=v[:, ch, :], in1=mean, op=ALU.subtract),
                 reads=[V[ch], meanb], writes=[t1b])
            P.op("dve", lambda e, t1=t1: e.tensor_tensor(out=t1, in0=t1, in1=rstd, op=ALU.mult),
                 reads=[t1b, rstdb], writes=[t1b])
            P.op("act", lambda e, ch=ch, t1=t1: e.activation(out=mixo[:, ch, :], in_=t1, func=AF.Silu,
                                                             bias=col(C_LB + ch), scale=col(C_LG + ch)),
                 reads=[t1b, CONST], writes=[MIXO[ch]])

        yield
        for ch in range(4):
            pa, pab = bank()
            px, pxb = bank()
            P.op("pe", lambda e, ch=ch, pa=pa: e.matmul(pa, lhsT=gatew[:, ch, :], rhs=xrb[:, ch, :], start=True, stop=True),
                 reads=[GATEW, XRB[ch]], writes=[pab])
            P.op("pe", lambda e, ch=ch, px=px: e.matmul(px, lhsT=gatew[:, 4 + ch, :], rhs=xrb[:, ch, :], start=True, stop=True),
                 reads=[GATEW, XRB[ch]], writes=[pxb])
            P.op("act", lambda e, ch=ch, pa=pa: e.activation(out=v[:, ch, :], in_=pa, func=AF.Sigmoid,
                                                             bias=col(C_BA + ch), scale=1.0),
                 reads=[pab, CONST], writes=[V[ch]])
            P.op("act", lambda e, ch=ch, px=px: e.activation(out=igt[:, ch, :], in_=px, func=AF.Sigmoid,
                                                             bias=col(C_BX + ch), scale=1.0),
                 reads=[pxb, CONST], writes=[IG[ch]])
        yield
        for ch in range(4):
            P.op("dve", lambda e, ch=ch: e.tensor_tensor(out=igt[:, ch, :], in0=igt[:, ch, :], in1=xr[:, ch, :], op=ALU.mult),
                 reads=[IG[ch], XR[ch]], writes=[IG[ch]])
        yield
        for ch in range(4):
            P.op("act", lambda e, ch=ch: e.activation(out=xr[:, ch, :], in_=v[:, ch, :], func=AF.Exp, scale=der[:, ch:ch + 1]),
                 reads=[V[ch], DER], writes=[XR[ch]])
            P.op("act", lambda e, ch=ch: e.activation(out=hout[:, ch, :], in_=v[:, ch, :], func=AF.Tanh, scale=der[:, 8 + ch:9 + ch]),
                 reads=[V[ch], DER], writes=[HOUT[ch]])
            P.op("act", lambda e, ch=ch: e.activation(out=v[:, ch, :], in_=v[:, ch, :], func=AF.Exp, scale=der[:, 4 + ch:5 + ch]),
                 reads=[V[ch], DER], writes=[V[ch]])
        yield
        for ch in range(4):
            P.op("dve", lambda e, ch=ch: e.scalar_tensor_tensor(out=v[:, ch, :], in0=v[:, ch, :], scalar=1.0, in1=hout[:, ch, :],
                                                                op0=ALU.add, op1=ALU.mult),
                 reads=[V[ch], HOUT[ch]], writes=[V[ch]])
        yield
        for ch in range(4):
            P.op("act", lambda e, ch=ch: e.activation(out=v[:, ch, :], in_=v[:, ch, :], func=AF.Sqrt),
                 reads=[V[ch]], writes=[V[ch]])
        yield
        for ch in range(4):
            P.op("dve", lambda e, ch=ch: e.tensor_tensor(out=igt[:, ch, :], in0=igt[:, ch, :], in1=v[:, ch, :], op=ALU.mult),
                 reads=[IG[ch], V[ch]], writes=[IG[ch]])
            P.op("dve", lambda e, ch=ch: e.tensor_tensor_scan(
                out=hout[:, ch, :], data0=xr[:, ch, :], data1=igt[:, ch, :], initial=hst[:, ch:ch + 1], op0=ALU.mult, op1=ALU.add),
                reads=[XR[ch], IG[ch], HST[ch]], writes=[HOUT[ch]])
            P.op("act", lambda e, ch=ch: e.activation(out=hst[:, ch:ch + 1], in_=hout[:, ch, T - 1:T], func=AF.Copy),
                 reads=[HOUT[ch]], writes=[HST[ch]])
            P.op("dve", lambda e, ch=ch: e.tensor_tensor(out=mixo[:, 4 + ch, :], in0=hout[:, ch, :], in1=glt[:, ch, :], op=ALU.mult),
                 reads=[HOUT[ch], GL[ch]], writes=[MIXO[4 + ch]])

        yield
        yield
        for u in range(2):
            slot, sbuf_ = ring_load(wout_s[u], 4096, "wout")
            sv = slot.rearrange("p (k j) -> p k j", k=KC)
            for jj in range(4):
                m = 4 * u + jj
                ps, psb = bank()
                for kc in range(KC):
                    P.op("pe", lambda e, kc=kc, ps=ps, sv=sv, jj=jj: e.matmul(
                        ps, lhsT=sv[:, kc, jj * 128:(jj + 1) * 128], rhs=mixo[:, kc, :],
                        start=(kc == 0), stop=(kc == KC - 1)),
                        reads=[sbuf_, MIXO[kc]], writes=[psb], inc=(kc == KC - 1))
                P.op("dve", lambda e, ps=ps, m=m: e.tensor_tensor(out=xs[b][:, m, :], in0=ps, in1=xs[b][:, m, :], op=ALU.add),
                     reads=[psb, X[b][m]], writes=[X[b][m]])

    bar = [("act", P.count["act"]), ("dve", P.count["dve"])] + list(cv_store_toks.values()) + list(cd_toks)
    for e_ in ("pe", "act", "dve", "sp"):
        for t_ in bar:
            if t_[0] != e_ and t_[1] > 0:
                P.wait(e_, t_)

    def phaseA(i):
        b = i % 2
        rms(b, C_G1, False, xn, XN, tmp_a)
        yield
        yield from ffn(b, 0, xn, XN, tmp_a)

    def phaseB(i):
        b = i % 2
        rms(b, C_G2, False, xn2, XN2, tmp)
        yield
        yield from mixer(b)

    def phaseC(i, held):
        b = i % 2
        rms(b, C_G3, False, xn2, XN2, tmp)
        for _ in held:
            pass
        for _ in ffn(b, 1, xn2, XN2, tmp):
            pass
        if i + 1 < NT:
            ring_prefetch("win1", win_s[1], 4096, "win")
            ring_prefetch("win0", win_s[0], 4096, "win")
            ring_prefetch("win2", win_s[2], 4096, "win")
        rms(b, C_G4, True)
        return store_y(i)

    if NT > 1:
        load_x(1)
    for _ in phaseA(0):
        pass
    last_store = {}
    HOLD = 3
    for i in range(NT):
        steps = list(range(21)) if i + 1 < NT else []
        ga = phaseA(i + 1) if i + 1 < NT else iter(())
        given = 0
        for yi, _ in enumerate(phaseB(i)):
            if yi != 3 and yi < 6:
                continue
            if given < len(steps) - HOLD:
                next(ga, None)
                given += 1
        while given < len(steps) - HOLD:
            next(ga, None)
            given += 1
        last_store[i % 2] = phaseC(i, ga)
        if i + 2 < NT:
            load_x(i + 2)
    for b in last_store:
        P.wait(XY_ENG, last_store[b])

    sems = {}

    def sem_of(key):
        if key not in sems:
            nm = key if isinstance(key, str) else "%s%d" % key
            sems[key] = nc.alloc_semaphore("s_" + nm)
        return sems[key]

    FQ = P.finalize()
    P.fq = FQ
    for e in Prog.ENG:
        for it in FQ[e]:
            if it[0] == "wait":
                sem_of(it[1])
            elif it[2] is not None:
                sem_of(it[2])

    def emit(eng, name):
        for it in FQ[name]:
            if it[0] == "wait":
                eng.wait_ge(sems[it[1]], it[2])
            else:
                ins = it[1](eng)
                if it[2] is not None:
                    ins.then_inc(sems[it[2]], it[3])

    with nc.Block() as block:
        @block.sync
        def _(e):
            emit(e, "sp")

        @block.scalar
        def _(e):
            emit(e, "act")

        @block.vector
        def _(e):
            emit(e, "dve")

        @block.gpsimd
        def _(e):
            emit(e, "pool")

        @block.tensor
        def _(e):
            emit(e, "pe")
    return nc


def _cols(vv, n):
    return np.ascontiguousarray(np.asarray(vv, np.float32).reshape(n, 128).T)


def kernel(x, ffn1_norm, ffn1_w_gate, ffn1_w_up, ffn1_w_down, mix_norm, w_in,
           conv_dw, conv_dw_bias, conv_ln_g, conv_ln_b, lru_conv_w, lru_conv_b,
           lru_w_a, lru_b_a, lru_w_x, lru_b_x, lru_lambda, w_out,
           ffn2_norm, ffn2_w_gate, ffn2_w_up, ffn2_w_down, final_norm):
    f = lambda a: np.ascontiguousarray(np.asarray(a, np.float32))
    x = f(x)
    cst = np.concatenate([
        _cols(ffn1_norm, 8), _cols(mix_norm, 8), _cols(ffn2_norm, 8), _cols(final_norm, 8),
        _cols(conv_dw_bias, 4), _cols(conv_ln_g, 4), _cols(conv_ln_b, 4),
        f(lru_conv_w).reshape(4, 4, 128).transpose(2, 0, 1).reshape(128, 16),
        _cols(lru_conv_b, 4), _cols(lru_b_a, 4), _cols(lru_b_x, 4), _cols(lru_lambda, 4),
        f(conv_dw).reshape(31, 4, 128).transpose(2, 0, 1).reshape(128, 124),
    ], axis=1)
    cst = np.ascontiguousarray(cst, np.float32)
    assert cst.shape == (128, NCST)
    common = {
        "cst": cst, "ident": np.eye(128, dtype=np.float32),
        "wg1": f(ffn1_w_gate), "wu1": f(ffn1_w_up), "wd1": f(ffn1_w_down),
        "wg2": f(ffn2_w_gate), "wu2": f(ffn2_w_up), "wd2": f(ffn2_w_down),
        "win": f(w_in), "wout": f(w_out), "lwa": f(lru_w_a), "lwx": f(lru_w_x),
    }
    in_maps = []
    for c in range(NCORES):
        m = dict(common)
        m["xT"] = np.ascontiguousarray(x[c].T)
        in_maps.append(m)
    nc = build_nc()
    res = run_bass_kernel_spmd(nc, in_maps, core_ids=list(range(NCORES)))
    out = np.empty((NCORES, S, D), np.float32)
    for c in range(NCORES):
        out[c] = np.asarray(res.results[c]["yT"]).T
    return out
```

```python
import numpy as np
import concourse.bass as bass
import concourse.mybir as mybir
from concourse.bass_utils import run_bass_kernel_spmd

F32 = mybir.dt.float32
BF16 = mybir.dt.bfloat16
AF = mybir.ActivationFunctionType
ALU = mybir.AluOpType

D = 1024
S = 8192
T = 512
NT = S // T
DFF = 2816
KC = D // 128
FC = DFF // 128
NCORES = 8
NS = 4
SLOT = 4096
NTMP = 8
XY_ENG = "sp"
RMS_EPS = 1e-6
LN_EPS = 1e-5

C_G1, C_G2, C_G3, C_G4 = 0, 8, 16, 24
C_CB, C_LG, C_LB = 32, 36, 40
C_W4, C_B4 = 44, 60
C_BA, C_BX, C_LAM = 64, 68, 72
C_CW = 76
NCST = 200


class Buf:
    __slots__ = ("name", "last_w", "readers")

    def __init__(self, name):
        self.name = name
        self.last_w = None
        self.readers = []


class Prog:
    ENG = ("pe", "act", "dve", "pool", "sp")

    def __init__(self):
        self.q = {e: [] for e in self.ENG}
        self.count = {e: 0 for e in self.ENG}
        self.seen = {e: {} for e in self.ENG}
        self.dcount = {}
        self.need = {e: set() for e in self.ENG}

    def _deps(self, eng, reads, writes, is_dma):
        deps = []
        for b in reads:
            if b.last_w is not None:
                deps.append((b.last_w, True))
        for b in writes:
            if is_dma and b.readers:
                for t in b.readers:
                    deps.append((t, False))
                continue
            if b.last_w is not None:
                deps.append((b.last_w, False))
            for t in b.readers:
                deps.append((t, False))
        for (key, val), raw in deps:
            if key == eng:
                if eng in ("pe", "sp"):
                    continue
                pass
            if self.seen[eng].get(key, 0) >= val:
                continue
            self.seen[eng][key] = val
            if key in self.need:
                self.need[key].add(val)
            self.q[eng].append(("wait", key, val))

    def _commit(self, tok, reads, writes):
        for b in writes:
            b.last_w = tok
            b.readers = []
        for b in reads:
            if b in writes:
                continue
            rs = [t for t in b.readers if t[0] != tok[0]]
            rs.append(tok)
            b.readers = rs

    def op(self, eng, fn, reads=(), writes=(), inc=True):
        self._deps(eng, reads, writes, False)
        if inc:
            self.count[eng] += 1
            tok = (eng, self.count[eng])
            self.q[eng].append(("op", fn, eng, self.count[eng]))
        else:
            tok = (eng, self.count[eng] + 1)
            self.q[eng].append(("op", fn, None, 0))
        self._commit(tok, reads, writes)
        return tok

    def dma(self, eng, fn, key, reads=(), writes=()):
        self._deps(eng, reads, writes, True)
        self.dcount[key] = self.dcount.get(key, 0) + 16
        tok = (key, self.dcount[key])
        self.q[eng].append(("op", fn, key, 16))
        self._commit(tok, reads, writes)
        return tok

    def wait(self, eng, tok):
        key, val = tok
        if self.seen[eng].get(key, 0) >= val:
            return
        self.seen[eng][key] = val
        if key in self.need:
            self.need[key].add(val)
        self.q[eng].append(("wait", key, val))

    def finalize(self):
        rank = {}
        for e in self.ENG:
            rank[e] = {idx: r + 1 for r, idx in enumerate(sorted(self.need[e]))}
        out = {}
        for e in self.ENG:
            lst = []
            for it in self.q[e]:
                if it[0] == "wait":
                    key, val = it[1], it[2]
                    if key in rank:
                        lst.append(("wait", key, rank[key][val]))
                    else:
                        lst.append(it)
                else:
                    _, fn, key, amt = it
                    if key in rank:
                        if amt in rank[key]:
                            lst.append(("op", fn, key, 1))
                        else:
                            lst.append(("op", fn, None, 0))
                    else:
                        lst.append(it)
            out[e] = lst
        return out


_LAST_PROG = None


def build_nc(S=S):
    NT = S // T
    nc = bass.Bass("TRN2", target_bir_lowering=False)
    P = Prog()
    global _LAST_PROG
    _LAST_PROG = P

    def din(name, shape):
        return nc.dram_tensor(name, list(shape), F32, kind="ExternalInput").ap()

    xT = din("xT", [D, S])
    yT = nc.dram_tensor("yT", [D, S], F32, kind="ExternalOutput").ap()
    cst_d = din("cst", [128, NCST])
    ident_d = din("ident", [128, 128])
    wg_d = [din("wg1", [D, DFF]), din("wg2", [D, DFF])]
    wu_d = [din("wu1", [D, DFF]), din("wu2", [D, DFF])]
    wd_d = [din("wd1", [DFF, D]), din("wd2", [DFF, D])]
    win_d = din("win", [D, 2048])
    wout_d = din("wout", [D, D])
    lwa_d = din("lwa", [8, 64, 64])
    lwx_d = din("lwx", [8, 64, 64])

    def dscr(name, shape):
        return nc.dram_tensor(name, list(shape), BF16, kind="Internal").ap()

    gu_s = [dscr("gu1s", [11, 128, 4096]), dscr("gu2s", [11, 128, 4096])]
    dd_s = [dscr("dd1s", [8, 128, 2816]), dscr("dd2s", [8, 128, 2816])]
    win_s = dscr("wins", [4, 128, 4096])
    wout_s = dscr("wouts", [2, 128, 4096])
    cd_s = dscr("cds", [2, 128, 3968])

    sb = nc.alloc_sbuf_tensor
    cst = sb("cst_sb", [128, NCST], F32)
    der = sb("der_sb", [128, 32], F32)
    ident = sb("ident_sb", [128, 128], F32)
    onesD = sb("onesD", [128, 128], BF16)
    onesC = sb("onesC", [128, 128], BF16)
    gatew = sb("gatew", [128, 8, 128], BF16)
    xs = [sb("x0", [128, KC, T], F32), sb("x1", [128, KC, T], F32)]
    xn = sb("xn", [128, KC, T], BF16)
    xn2 = sb("xn2", [128, KC, T], BF16)
    tmpa = sb("tmpa", [128, 4, T], F32)
    sq = sb("sq", [128, 8, T], BF16)
    h = sb("h", [128, FC, T], BF16)
    ring = sb("ring", [128, NS, SLOT], BF16)
    ubuf = sb("ubuf", [128, 2, 30 + T], F32)
    ubb = sb("ubb", [128, 2, 30 + T], BF16)
    v = sb("v", [128, 4, T], F32)
    xbr = sb("xbr", [128, 4, 3 + T], F32)
    xr = sb("xr", [128, 4, T], F32)
    xrb = sb("xrb", [128, 4, T], BF16)
    hout = sb("hout", [128, 4, T], F32)
    hst = sb("hst", [128, 4], F32)
    mixo = sb("mixo", [128, KC, T], BF16)
    tmpt = sb("tmpt", [128, NTMP, T], F32)
    glt = sb("glt", [128, 4, T], F32)
    igt = sb("igt", [128, 4, T], F32)
    banks = [nc.alloc_psum_tensor("bank%d" % i, [128, T], F32) for i in range(8)]

    CONST = Buf("const")
    DER = Buf("der")
    GATEW = Buf("gatew")
    X = [[Buf("x%d_%d" % (b, k)) for k in range(KC)] for b in range(2)]
    XN = [Buf("xn%d" % k) for k in range(KC)]
    XN2 = [Buf("xn2_%d" % k) for k in range(KC)]
    TMPA = [Buf("tmpa%d" % k) for k in range(4)]
    SQ = [Buf("sq%d" % k) for k in range(8)]
    H = [Buf("h%d" % k) for k in range(FC)]
    RING = [Buf("ring%d" % k) for k in range(NS)]
    UB = [Buf("ub%d" % k) for k in range(4)]
    V = [Buf("v%d" % k) for k in range(4)]
    XBR = [Buf("xbr%d" % k) for k in range(4)]
    XR = [Buf("xr%d" % k) for k in range(4)]
    XRB = [Buf("xrb%d" % k) for k in range(4)]
    HOUT = [Buf("hout%d" % k) for k in range(4)]
    HST = [Buf("hst%d" % k) for k in range(4)]
    MIXO = [Buf("mixo%d" % k) for k in range(KC)]
    TMP = [Buf("tmp%d" % k) for k in range(NTMP)]
    GL = [Buf("gl%d" % k) for k in range(4)]
    IG = [Buf("ig%d" % k) for k in range(4)]
    BANK = [Buf("bank%d" % k) for k in range(8)]
    FAM = {n: Buf(n) for n in ("gu0", "gu1", "dd0", "dd1", "win", "wout", "cd")}

    st = {"bank": 0, "tmp": 0, "ring": 0}

    def bank():
        i = st["bank"]
        st["bank"] = (i + 1) % 8
        return banks[i][:], BANK[i]

    def tmp():
        i = st["tmp"]
        st["tmp"] = (i + 1) % NTMP
        return tmpt[:, i, :], TMP[i]

    def tmp_a():
        i = st.get("tmpa", 0)
        st["tmpa"] = (i + 1) % 4
        return tmpa[:, i, :], TMPA[i]

    def col(c):
        return cst[:, c:c + 1]

    ring_pre = []

    def ring_issue(src, U, fam):
        s = st["ring"]
        st["ring"] = (s + 1) % NS
        dst = ring[:, s, 0:U]
        P.dma("sp", lambda e: e.dma_start(out=dst, in_=src), ("ring", s),
              reads=[FAM[fam]], writes=[RING[s]])
        return ring[:, s, 0:U], RING[s]

    def ring_prefetch(key, src, U, fam):
        ring_pre.append((key, ring_issue(src, U, fam)))

    def ring_load(src, U, fam, key=None):
        if key is not None and ring_pre and ring_pre[0][0] == key:
            return ring_pre.pop(0)[1]
        assert key is None or all(k_ != key for k_, _ in ring_pre)
        return ring_issue(src, U, fam)

    def xview(ap, i):
        return ap.rearrange("(kc p) t -> p kc t", p=128)[:, :, i * T:(i + 1) * T]

    def load_x(i):
        b = i % 2
        dst = xs[b][:]
        src = xview(xT, i)
        P.dma(XY_ENG, lambda e: e.dma_start(out=dst, in_=src), ("xld", b),
              reads=[], writes=X[b])

    def store_y(i):
        b = i % 2
        src = xs[b][:]
        dst = xview(yT, i)
        return P.dma(XY_ENG, lambda e: e.dma_start(out=dst, in_=src), ("yst", b),
                     reads=X[b], writes=[])

    load_x(0)
    P.dma("sp", lambda e: e.dma_start(out=cst[:], in_=cst_d), ("setup", 0), writes=[CONST])
    P.dma("sp", lambda e: e.dma_start(out=ident[:], in_=ident_d), ("setup", 1), writes=[DER])
    P.op("dve", lambda e: e.memset(gatew[:], 0.0), writes=[GATEW])
    P.op("dve", lambda e: e.memset(onesD[:], 1.0 / 1024.0), writes=[CONST])
    P.op("dve", lambda e: e.memset(onesC[:], 1.0 / 512.0), writes=[CONST])
    P.op("dve", lambda e: e.memset(ubuf[:, :, 0:30], 0.0), writes=UB[0:2])
    P.op("dve", lambda e: e.memset(ubb[:, :, 0:30], 0.0), writes=UB[2:4])
    P.op("dve", lambda e: e.memset(xbr[:, :, 0:3], 0.0), writes=XBR)
    P.op("dve", lambda e: e.memset(hst[:], 0.0), writes=HST)
    stg = [xs[1][:].rearrange("p a b -> p (a b)"), tmpt[:].rearrange("p a b -> p (a b)")]
    STG = [Buf("stg0"), Buf("stg1")]
    cvt = {"i": 0, "cast": 0, "slot": 0}

    stgG = stg[0][:, 0:1024].rearrange("p (g m) -> p g m", g=8)
    P.op("dve", lambda e: e.memset(stgG, 0.0), writes=[STG[0]])
    gtok = None
    for g, src in enumerate((lwa_d, lwx_d)):
        for hd in range(8):
            ch, half = hd // 2, hd % 2
            dst = stgG[half * 64:(half + 1) * 64, g * 4 + ch, half * 64:(half + 1) * 64]
            s_ap = src[hd]
            gtok = P.dma("sp", lambda e, dst=dst, s_ap=s_ap: e.dma_start(out=dst, in_=s_ap),
                         ("gwld", 0), reads=[], writes=[STG[0]] if gtok is None else [])
    STG[0].last_w = gtok
    STG[0].readers = []
    P.op("act", lambda e: e.activation(out=gatew[:], in_=stgG, func=AF.Copy), reads=[STG[0]], writes=[GATEW])

    def cv_job(srcs, slot, c0):
        k = cvt["i"] % 2
        cvt["i"] += 1
        off = 0
        tok = None
        for src, n, a_ in srcs:
            dstv = stg[k][:, off:off + n].rearrange("p (a b) -> p a b", a=a_)
            tok = P.dma("sp", lambda e, dstv=dstv, src=src: e.dma_start(out=dstv, in_=src), ("cvld", k),
                        reads=[], writes=[STG[k]] if tok is None else [])
            off += n
        STG[k].last_w = tok
        STG[k].readers = []
        dsts = ring[:, slot, c0:c0 + off]
        srcs_ = stg[k][:, 0:off]
        if cvt["cast"] % 2 == 0:
            P.op("act", lambda e: e.activation(out=dsts, in_=srcs_, func=AF.Copy), reads=[STG[k]], writes=[RING[slot]])
        else:
            P.op("dve", lambda e: e.tensor_copy(out=dsts, in_=srcs_), reads=[STG[k]], writes=[RING[slot]])
        cvt["cast"] += 1

    units = []
    for l in range(2):
        for u in range(11):
            sg_ = wg_d[l].rearrange("(kc p) (u j) -> u p kc j", p=128, j=256)[u]
            su_ = wu_d[l].rearrange("(kc p) (u j) -> u p kc j", p=128, j=256)[u]
            units.append(([[(sg_, 2048, 8), (su_, 2048, 8)]], gu_s[l][u], 4096))
        for u in range(8):
            sd_ = wd_d[l].rearrange("(kc p) (u j) -> u p kc j", p=128, j=128)[u]
            units.append(([[(sd_, 2816, 22)]], dd_s[l][u], 2816))
        if l == 0:
            for u in range(4):
                units.append(([[(win_d.rearrange("(kc p) (u j) -> u p kc j", p=128, j=512)[u], 4096, 8)]], win_s[u], 4096))
            for u in range(2):
                units.append(([[(wout_d.rearrange("(kc p) (u j) -> u p kc j", p=128, j=512)[u], 4096, 8)]], wout_s[u], 4096))
    cv_store_toks = {}

    def cv_store(slot, dram, U):
        cv_store_toks[slot] = P.dma("sp", lambda e: e.dma_start(out=dram, in_=ring[:, slot, 0:U]), ("cvst", slot),
                                    reads=[RING[slot]], writes=[])

    prev = None
    for jobs, dram, U in units:
        slot = cvt["slot"]
        cvt["slot"] = (slot + 1) % NS
        c0 = 0
        for srcs in jobs:
            cv_job(srcs, slot, c0)
            c0 += sum(n for _, n, _ in srcs)
        if prev is not None:
            cv_store(*prev)
        prev = (slot, dram, U)
    cv_store(*prev)

    cd_toks = []
    for j, ch in enumerate((2, 3)):
        hb = H[0] if j == 0 else H[8]
        base = 0 if j == 0 else 8
        stage = h[:, base:base + 8, :].rearrange("p a b -> p (a b)")[:, 0:3968]
        stage3 = stage.rearrange("p (k j) -> p k j", k=31)
        for k in range(31):
            c = C_CW + k * 4 + ch
            P.op("dve", lambda e, k=k, c=c, stage3=stage3: e.tensor_scalar(
                out=stage3[:, k, :], in0=ident[:], scalar1=col(c), scalar2=None, op0=ALU.mult),
                reads=[CONST, DER], writes=[hb] if k == 0 else [], inc=(k == 30))
        hb.last_w = ("dve", P.count["dve"])
        hb.readers = []
        cd_toks.append(P.dma("sp", lambda e, j=j, stage=stage: e.dma_start(out=cd_s[j], in_=stage),
                             ("cdst", j), reads=[hb], writes=[]))

    def dcol(a, n=4):
        return der[:, a:a + n]

    lam = cst[:, C_LAM:C_LAM + 4]
    dops = []

    def dv(fn, reads=(DER, CONST)):
        P.op("dve", fn, reads=list(reads), writes=[DER])

    dv(lambda e: e.tensor_scalar(out=dcol(12), in0=lam, scalar1=-1.0, scalar2=None, op0=ALU.mult))
    dv(lambda e: e.tensor_tensor(out=dcol(16), in0=lam, in1=dcol(12), op=ALU.max))
    P.op("act", lambda e: e.activation(out=dcol(20), in_=dcol(16), func=AF.Exp, scale=-1.0),
         reads=[DER], writes=[DER])
    dv(lambda e: e.tensor_scalar(out=dcol(24), in0=dcol(20), scalar1=2.0, scalar2=None, op0=ALU.add))
    dv(lambda e: e.reciprocal(out=dcol(24), in_=dcol(24)))
    dv(lambda e: e.tensor_tensor(out=dcol(24), in0=dcol(24), in1=dcol(20), op=ALU.mult))
    dv(lambda e: e.tensor_tensor(out=dcol(28), in0=dcol(24), in1=dcol(24), op=ALU.mult))
    dv(lambda e: e.tensor_scalar(out=dcol(20), in0=dcol(28), scalar1=1.0 / 15.0, scalar2=1.0 / 13.0,
                                 op0=ALU.mult, op1=ALU.add))
    for cc in (1.0 / 11.0, 1.0 / 9.0, 1.0 / 7.0, 1.0 / 5.0, 1.0 / 3.0, 1.0):
        dv(lambda e: e.tensor_tensor(out=dcol(20), in0=dcol(20), in1=dcol(28), op=ALU.mult))
        dv(lambda e, cc=cc: e.tensor_scalar(out=dcol(20), in0=dcol(20), scalar1=cc, scalar2=None, op0=ALU.add))
    dv(lambda e: e.tensor_tensor(out=dcol(20), in0=dcol(20), in1=dcol(24), op=ALU.mult))
    dv(lambda e: e.tensor_scalar(out=dcol(12), in0=dcol(12), scalar1=0.0, scalar2=None, op0=ALU.max))
    dv(lambda e: e.scalar_tensor_tensor(out=dcol(12), in0=dcol(20), scalar=2.0, in1=dcol(12),
                                        op0=ALU.mult, op1=ALU.add))
    dv(lambda e: e.tensor_scalar(out=dcol(0), in0=dcol(12), scalar1=-8.0, scalar2=None, op0=ALU.mult))
    dv(lambda e: e.tensor_scalar(out=dcol(4), in0=dcol(12), scalar1=-16.0, scalar2=None, op0=ALU.mult))
    dv(lambda e: e.tensor_scalar(out=dcol(8), in0=dcol(12), scalar1=8.0, scalar2=None, op0=ALU.mult))

    def rms(b, gcol, final, xn_t=None, XN_t=None, tmpf=None):
        tmpf = tmpf or tmp
        ps, psb = bank()
        for kc in range(KC):
            P.op("act", lambda e, kc=kc: e.activation(out=sq[:, kc, :], in_=xs[b][:, kc, :], func=AF.Square),
                 reads=[X[b][kc]], writes=[SQ[kc]])
        for kc in range(KC):
            P.op("pe", lambda e, kc=kc: e.matmul(ps, lhsT=onesD[:], rhs=sq[:, kc, :],
                                                 start=(kc == 0), stop=(kc == KC - 1)),
                 reads=[SQ[kc], CONST], writes=[psb], inc=(kc == KC - 1))
        t1, t1b = tmpf()
        P.op("act", lambda e: e.activation(out=t1, in_=ps, func=AF.Sqrt, bias=RMS_EPS, scale=1.0),
             reads=[psb], writes=[t1b])
        t2, t2b = tmpf()
        P.op("dve", lambda e: e.reciprocal(out=t2, in_=t1), reads=[t1b], writes=[t2b])
        for kc in range(KC):
            out = xs[b][:, kc, :] if final else xn_t[:, kc, :]
            P.op("dve", lambda e, kc=kc, out=out: e.scalar_tensor_tensor(
                out=out, in0=xs[b][:, kc, :], scalar=col(gcol + kc), in1=t2, op0=ALU.mult, op1=ALU.mult),
                reads=[X[b][kc], t2b, CONST], writes=[X[b][kc] if final else XN_t[kc]])

    def ffn(b, l, xn_t, XN_t, tmpf):
        for u in range(11):
            slot, sbuf_ = ring_load(gu_s[l][u], 4096, "gu%d" % l)
            sv = slot.rearrange("p (a k j) -> p a k j", a=2, k=8)
            for jj in range(2):
                f = 2 * u + jj
                pg, pgb = bank()
                pu, pub = bank()
                for a_, (pp, ppb) in enumerate(((pg, pgb), (pu, pub))):
                    for kc in range(KC):
                        P.op("pe", lambda e, a_=a_, kc=kc, pp=pp, sv=sv, jj=jj: e.matmul(
                            pp, lhsT=sv[:, a_, kc, jj * 128:(jj + 1) * 128], rhs=xn_t[:, kc, :],
                            start=(kc == 0), stop=(kc == KC - 1)),
                            reads=[sbuf_, XN_t[kc]], writes=[ppb], inc=(kc == KC - 1))
                t, tb = tmpf()
                P.op("act", lambda e, t=t, pg=pg: e.activation(out=t, in_=pg, func=AF.Silu),
                     reads=[pgb], writes=[tb])
                P.op("dve", lambda e, t=t, pu=pu, f=f: e.tensor_tensor(out=h[:, f, :], in0=t, in1=pu, op=ALU.mult),
                     reads=[tb, pub], writes=[H[f]])
            yield
        for m in range(8):
            slot, sbuf_ = ring_load(dd_s[l][m], 2816, "dd%d" % l)
            sv = slot.rearrange("p (k j) -> p k j", k=FC)
            pd, pdb = bank()
            for kc in range(FC):
                P.op("pe", lambda e, kc=kc, pd=pd, sv=sv: e.matmul(
                    pd, lhsT=sv[:, kc, :], rhs=h[:, kc, :], start=(kc == 0), stop=(kc == FC - 1)),
                    reads=[sbuf_, H[kc]], writes=[pdb], inc=(kc == FC - 1))
            P.op("dve", lambda e, pd=pd, m=m: e.scalar_tensor_tensor(
                out=xs[b][:, m, :], in0=pd, scalar=0.5, in1=xs[b][:, m, :], op0=ALU.mult, op1=ALU.add),
                reads=[pdb, X[b][m]], writes=[X[b][m]])
            yield

    def proj4(unit_ap, fam, evac, key=None):
        slot, sbuf_ = ring_load(unit_ap, 4096, fam, key=key)
        sv = slot.rearrange("p (k j) -> p k j", k=KC)
        for ch in range(4):
            ps, psb = bank()
            for kc in range(KC):
                P.op("pe", lambda e, kc=kc, ps=ps, sv=sv, ch=ch: e.matmul(
                    ps, lhsT=sv[:, kc, ch * 128:(ch + 1) * 128], rhs=xn2[:, kc, :],
                    start=(kc == 0), stop=(kc == KC - 1)),
                    reads=[sbuf_, XN2[kc]], writes=[psb], inc=(kc == KC - 1))
            evac(ch, ps, psb)

    def mixer(b):
        sg = [tmp() for _ in range(4)]

        def ev_gate(ch, ps, psb):
            P.op("act", lambda e: e.activation(out=sg[ch][0], in_=ps, func=AF.Sigmoid),
                 reads=[psb], writes=[sg[ch][1]])

        def ev_val(ch, ps, psb):
            dst_u = ubuf[:, ch, 30:30 + T] if ch < 2 else ubb[:, ch - 2, 30:30 + T]
            P.op("dve", lambda e: e.tensor_tensor(out=dst_u, in0=ps, in1=sg[ch][0], op=ALU.mult),
                 reads=[psb, sg[ch][1]], writes=[UB[ch]])

        def ev_rx(ch, ps, psb):
            P.op("act", lambda e: e.activation(out=xbr[:, ch, 3:3 + T], in_=ps, func=AF.Copy),
                 reads=[psb], writes=[XBR[ch]])

        def ev_rg(ch, ps, psb):
            P.op("act", lambda e: e.activation(out=glt[:, ch, :], in_=ps, func=AF.Gelu_apprx_tanh),
                 reads=[psb], writes=[GL[ch]])

        def conv4_part():
            for ch in range(4):
                P.op("dve", lambda e, ch=ch: e.tensor_scalar(
                    out=xr[:, ch, :], in0=xbr[:, ch, 0:T], scalar1=col(C_W4 + ch), scalar2=col(C_B4 + ch),
                    op0=ALU.mult, op1=ALU.add), reads=[XBR[ch], CONST], writes=[XR[ch]])
                for k in range(1, 4):
                    P.op("dve", lambda e, ch=ch, k=k: e.scalar_tensor_tensor(
                        out=xr[:, ch, :], in0=xbr[:, ch, k:k + T], scalar=col(C_W4 + k * 4 + ch), in1=xr[:, ch, :],
                        op0=ALU.mult, op1=ALU.add), reads=[XBR[ch], XR[ch], CONST], writes=[XR[ch]])
                P.op("act", lambda e, ch=ch: e.activation(out=xbr[:, ch, 0:3], in_=xbr[:, ch, T:T + 3], func=AF.Copy),
                     reads=[XBR[ch]], writes=[XBR[ch]])
                P.op("act", lambda e, ch=ch: e.activation(out=xrb[:, ch, :], in_=xr[:, ch, :], func=AF.Copy),
                     reads=[XR[ch]], writes=[XRB[ch]])

        proj4(win_s[1], "win", ev_gate, key="win1")
        yield
        proj4(win_s[0], "win", ev_val, key="win0")
        yield
        proj4(win_s[2], "win", ev_rx, key="win2")
        yield
        proj4(win_s[3], "win", ev_rg)
        yield
        conv4_part()

        for j, ch in enumerate((2, 3)):
            slot, sbuf_ = ring_load(cd_s[j], 3968, "cd")
            sv = slot.rearrange("p (k j) -> p k j", k=31)
            ps, psb = bank()
            for k in range(31):
                P.op("pe", lambda e, k=k, ps=ps, sv=sv, j=j: e.matmul(
                    ps, lhsT=sv[:, k, :], rhs=ubb[:, j, k:k + T], start=(k == 0), stop=(k == 30)),
                    reads=[sbuf_, UB[ch]], writes=[psb], inc=(k == 30))
            P.op("act", lambda e, ps=ps, ch=ch: e.activation(out=v[:, ch, :], in_=ps, func=AF.Identity,
                                                             bias=col(C_CB + ch), scale=1.0),
                 reads=[psb, CONST], writes=[V[ch]])
            yield
        for k in range(31):
            acc, ACC = (v, V) if k % 2 == 0 else (hout, HOUT)
            for ch in range(2):
                if k == 0:
                    P.op("dve", lambda e, ch=ch: e.tensor_scalar(
                        out=v[:, ch, :], in0=ubuf[:, ch, 0:T], scalar1=col(C_CW + ch), scalar2=col(C_CB + ch),
                        op0=ALU.mult, op1=ALU.add), reads=[UB[ch], CONST], writes=[V[ch]], inc=(ch == 1))
                elif k == 1:
                    P.op("dve", lambda e, ch=ch: e.tensor_scalar(
                        out=hout[:, ch, :], in0=ubuf[:, ch, 1:1 + T], scalar1=col(C_CW + 4 + ch), scalar2=None,
                        op0=ALU.mult), reads=[UB[ch], CONST], writes=[HOUT[ch]], inc=(ch == 1))
                else:
                    P.op("dve", lambda e, ch=ch, k=k, acc=acc: e.scalar_tensor_tensor(
                        out=acc[:, ch, :], in0=ubuf[:, ch, k:k + T], scalar=col(C_CW + k * 4 + ch), in1=acc[:, ch, :],
                        op0=ALU.mult, op1=ALU.add), reads=[UB[ch], ACC[ch], CONST], writes=[ACC[ch]], inc=(ch == 1))
            if k % 4 == 3:
                yield
        for ch in range(2):
            P.op("dve", lambda e, ch=ch: e.tensor_tensor(out=v[:, ch, :], in0=v[:, ch, :], in1=hout[:, ch, :], op=ALU.add),
                 reads=[V[ch], HOUT[ch]], writes=[V[ch]])
        for ch in range(4):
            P.op("act", lambda e, ch=ch: e.activation(out=sq[:, 4 + ch, :], in_=v[:, ch, :], func=AF.Square),
                 reads=[V[ch]], writes=[SQ[4 + ch]])
            P.op("act", lambda e, ch=ch: e.activation(out=sq[:, ch, :], in_=v[:, ch, :], func=AF.Copy),
                 reads=[V[ch]], writes=[SQ[ch]])
            ub_ = ubuf[:, ch, :] if ch < 2 else ubb[:, ch - 2, :]
            P.op("act", lambda e, ub_=ub_: e.activation(out=ub_[:, 0:30], in_=ub_[:, T:T + 30], func=AF.Copy),
                 reads=[UB[ch]], writes=[UB[ch]])
        yield
        pm, pmb = bank()
        pe2, pe2b = bank()
        for ch in range(4):
            P.op("pe", lambda e, ch=ch: e.matmul(pm, lhsT=onesC[:], rhs=sq[:, ch, :], start=(ch == 0), stop=(ch == 3)),
                 reads=[SQ[ch], CONST], writes=[pmb], inc=(ch == 3))
        for ch in range(4):
            P.op("pe", lambda e, ch=ch: e.matmul(pe2, lhsT=onesC[:], rhs=sq[:, 4 + ch, :], start=(ch == 0), stop=(ch == 3)),
                 reads=[SQ[4 + ch], CONST], writes=[pe2b], inc=(ch == 3))
        yield
        mean, meanb = tmp()
        msq, msqb = tmp()
        P.op("act", lambda e: e.activation(out=mean, in_=pm, func=AF.Copy), reads=[pmb], writes=[meanb])
        P.op("act", lambda e: e.activation(out=msq, in_=pm, func=AF.Square), reads=[pmb], writes=[msqb])
        var, varb = tmp()
        P.op("dve", lambda e: e.tensor_tensor(out=var, in0=pe2, in1=msq, op=ALU.subtract),
             reads=[pe2b, msqb], writes=[varb])
        sd, sdb = tmp()
        P.op("act", lambda e: e.activation(out=sd, in_=var, func=AF.Sqrt, bias=LN_EPS, scale=1.0),
             reads=[varb], writes=[sdb])
        rstd, rstdb = tmp()
        P.op("dve", lambda e: e.reciprocal(out=rstd, in_=sd), reads=[sdb], writes=[rstdb])
        for ch in range(4):
            t1, t1b = tmp()
            P.op("dve", lambda e, ch=ch, t1=t1: e.tensor_tensor(out=t1, in0=v[:, ch, :], in1=mean, op=ALU.subtract),
                 reads=[V[ch], meanb], writes=[t1b])
            P.op("dve", lambda e, t1=t1: e.tensor_tensor(out=t1, in0=t1, in1=rstd, op=ALU.mult),
                 reads=[t1b, rstdb], writes=[t1b])
            P.op("act", lambda e, ch=ch, t1=t1: e.activation(out=mixo[:, ch, :], in_=t1, func=AF.Silu,
                                                             bias=col(C_LB + ch), scale=col(C_LG + ch)),
                 reads=[t1b, CONST], writes=[MIXO[ch]])

        yield
        for ch in range(4):
            pa, pab = bank()
            px, pxb = bank()
            P.op("pe", lambda e, ch=ch, pa=pa: e.matmul(pa, lhsT=gatew[:, ch, :], rhs=xrb[:, ch, :], start=True, stop=True),
                 reads=[GATEW, XRB[ch]], writes=[pab])
            P.op("pe", lambda e, ch=ch, px=px: e.matmul(px, lhsT=gatew[:, 4 + ch, :], rhs=xrb[:, ch, :], start=True, stop=True),
                 reads=[GATEW, XRB[ch]], writes=[pxb])
            P.op("act", lambda e, ch=ch, pa=pa: e.activation(out=v[:, ch, :], in_=pa, func=AF.Sigmoid,
                                                             bias=col(C_BA + ch), scale=1.0),
                 reads=[pab, CONST], writes=[V[ch]])
            P.op("act", lambda e, ch=ch, px=px: e.activation(out=igt[:, ch, :], in_=px, func=AF.Sigmoid,
                                                             bias=col(C_BX + ch), scale=1.0),
                 reads=[pxb, CONST], writes=[IG[ch]])
        yield
        for ch in range(4):
            P.op("dve", lambda e, ch=ch: e.tensor_tensor(out=igt[:, ch, :], in0=igt[:, ch, :], in1=xr[:, ch, :], op=ALU.mult),
                 reads=[IG[ch], XR[ch]], writes=[IG[ch]])
        yield
        for ch in range(4):
            P.op("act", lambda e, ch=ch: e.activation(out=xr[:, ch, :], in_=v[:, ch, :], func=AF.Exp, scale=der[:, ch:ch + 1]),
                 reads=[V[ch], DER], writes=[XR[ch]])
            P.op("act", lambda e, ch=ch: e.activation(out=hout[:, ch, :], in_=v[:, ch, :], func=AF.Tanh, scale=der[:, 8 + ch:9 + ch]),
                 reads=[V[ch], DER], writes=[HOUT[ch]])
            P.op("act", lambda e, ch=ch: e.activation(out=v[:, ch, :], in_=v[:, ch, :], func=AF.Exp, scale=der[:, 4 + ch:5 + ch]),
                 reads=[V[ch], DER], writes=[V[ch]])
        yield
        for ch in range(4):
            P.op("dve", lambda e, ch=ch: e.scalar_tensor_tensor(out=v[:, ch, :], in0=v[:, ch, :], scalar=1.0, in1=hout[:, ch, :],
                                                                op0=ALU.add, op1=ALU.mult),
                 reads=[V[ch], HOUT[ch]], writes=[V[ch]])
        yield
        for ch in range(4):
            P.op("act", lambda e, ch=ch: e.activation(out=v[:, ch, :], in_=v[:, ch, :], func=AF.Sqrt),
                 reads=[V[ch]], writes=[V[ch]])
        yield
        for ch in range(4):
            P.op("dve", lambda e, ch=ch: e.tensor_tensor(out=igt[:, ch, :], in0=igt[:, ch, :], in1=v[:, ch, :], op=ALU.mult),
                 reads=[IG[ch], V[ch]], writes=[IG[ch]])
            P.op("dve", lambda e, ch=ch: e.tensor_tensor_scan(
                out=hout[:, ch, :], data0=xr[:, ch, :], data1=igt[:, ch, :], initial=hst[:, ch:ch + 1], op0=ALU.mult, op1=ALU.add),
                reads=[XR[ch], IG[ch], HST[ch]], writes=[HOUT[ch]])
            P.op("act", lambda e, ch=ch: e.activation(out=hst[:, ch:ch + 1], in_=hout[:, ch, T - 1:T], func=AF.Copy),
                 reads=[HOUT[ch]], writes=[HST[ch]])
            P.op("dve", lambda e, ch=ch: e.tensor_tensor(out=mixo[:, 4 + ch, :], in0=hout[:, ch, :], in1=glt[:, ch, :], op=ALU.mult),
                 reads=[HOUT[ch], GL[ch]], writes=[MIXO[4 + ch]])

        yield
        yield
        for u in range(2):
            slot, sbuf_ = ring_load(wout_s[u], 4096, "wout")
            sv = slot.rearrange("p (k j) -> p k j", k=KC)
            for jj in range(4):
                m = 4 * u + jj
                ps, psb = bank()
                for kc in range(KC):
                    P.op("pe", lambda e, kc=kc, ps=ps, sv=sv, jj=jj: e.matmul(
                        ps, lhsT=sv[:, kc, jj * 128:(jj + 1) * 128], rhs=mixo[:, kc, :],
                        start=(kc == 0), stop=(kc == KC - 1)),
                        reads=[sbuf_, MIXO[kc]], writes=[psb], inc=(kc == KC - 1))
                P.op("dve", lambda e, ps=ps, m=m: e.tensor_tensor(out=xs[b][:, m, :], in0=ps, in1=xs[b][:, m, :], op=ALU.add),
                     reads=[psb, X[b][m]], writes=[X[b][m]])

    bar = [("act", P.count["act"]), ("dve", P.count["dve"])] + list(cv_store_toks.values()) + list(cd_toks)
    for e_ in ("pe", "act", "dve", "sp"):
        for t_ in bar:
            if t_[0] != e_ and t_[1] > 0:
                P.wait(e_, t_)

    def phaseA(i):
        b = i % 2
        rms(b, C_G1, False, xn, XN, tmp_a)
        yield
        yield from ffn(b, 0, xn, XN, tmp_a)

    def phaseB(i):
        b = i % 2
        rms(b, C_G2, False, xn2, XN2, tmp)
        yield
        yield from mixer(b)

    def phaseC(i, held, gb_next=None):
        b = i % 2
        rms(b, C_G3, False, xn2, XN2, tmp)
        for _ in held:
            pass
        for _ in ffn(b, 1, xn2, XN2, tmp):
            pass
        if i + 1 < NT:
            ring_prefetch("win1", win_s[1], 4096, "win")
            ring_prefetch("win0", win_s[0], 4096, "win")
            ring_prefetch("win2", win_s[2], 4096, "win")
        if gb_next is not None:
            next(gb_next)
        rms(b, C_G4, True)
        return store_y(i)

    if NT > 1:
        load_x(1)
    for _ in phaseA(0):
        pass
    last_store = {}
    HOLD = 3
    gb = phaseB(0)
    yoff = 0
    for i in range(NT):
        steps = list(range(21)) if i + 1 < NT else []
        ga = phaseA(i + 1) if i + 1 < NT else iter(())
        given = 0
        for yi, _ in enumerate(gb, start=yoff):
            if yi != 3 and yi < 6:
                continue
            if given < len(steps) - HOLD:
                next(ga, None)
                given += 1
        while given < len(steps) - HOLD:
            next(ga, None)
            given += 1
        gb = phaseB(i + 1) if i + 1 < NT else None
        yoff = 1
        last_store[i % 2] = phaseC(i, ga, gb)
        if i + 2 < NT:
            load_x(i + 2)
    for b in last_store:
        P.wait(XY_ENG, last_store[b])

    sems = {}

    def sem_of(key):
        if key not in sems:
            nm = key if isinstance(key, str) else "%s%d" % key
            sems[key] = nc.alloc_semaphore("s_" + nm)
        return sems[key]

    FQ = P.finalize()
    P.fq = FQ
    for e in Prog.ENG:
        for it in FQ[e]:
            if it[0] == "wait":
                sem_of(it[1])
            elif it[2] is not None:
                sem_of(it[2])

    def emit(eng, name):
        for it in FQ[name]:
            if it[0] == "wait":
                eng.wait_ge(sems[it[1]], it[2])
            else:
                ins = it[1](eng)
                if it[2] is not None:
                    ins.then_inc(sems[it[2]], it[3])

    with nc.Block() as block:
        @block.sync
        def _(e):
            emit(e, "sp")

        @block.scalar
        def _(e):
            emit(e, "act")

        @block.vector
        def _(e):
            emit(e, "dve")

        @block.gpsimd
        def _(e):
            emit(e, "pool")

        @block.tensor
        def _(e):
            emit(e, "pe")
    return nc


def _cols(vv, n):
    return np.ascontiguousarray(np.asarray(vv, np.float32).reshape(n, 128).T)


def kernel(x, ffn1_norm, ffn1_w_gate, ffn1_w_up, ffn1_w_down, mix_norm, w_in,
           conv_dw, conv_dw_bias, conv_ln_g, conv_ln_b, lru_conv_w, lru_conv_b,
           lru_w_a, lru_b_a, lru_w_x, lru_b_x, lru_lambda, w_out,
           ffn2_norm, ffn2_w_gate, ffn2_w_up, ffn2_w_down, final_norm):
    f = lambda a: np.ascontiguousarray(np.asarray(a, np.float32))
    x = f(x)
    cst = np.concatenate([
        _cols(ffn1_norm, 8), _cols(mix_norm, 8), _cols(ffn2_norm, 8), _cols(final_norm, 8),
        _cols(conv_dw_bias, 4), _cols(conv_ln_g, 4), _cols(conv_ln_b, 4),
        f(lru_conv_w).reshape(4, 4, 128).transpose(2, 0, 1).reshape(128, 16),
        _cols(lru_conv_b, 4), _cols(lru_b_a, 4), _cols(lru_b_x, 4), _cols(lru_lambda, 4),
        f(conv_dw).reshape(31, 4, 128).transpose(2, 0, 1).reshape(128, 124),
    ], axis=1)
    cst = np.ascontiguousarray(cst, np.float32)
    assert cst.shape == (128, NCST)
    common = {
        "cst": cst, "ident": np.eye(128, dtype=np.float32),
        "wg1": f(ffn1_w_gate), "wu1": f(ffn1_w_up), "wd1": f(ffn1_w_down),
        "wg2": f(ffn2_w_gate), "wu2": f(ffn2_w_up), "wd2": f(ffn2_w_down),
        "win": f(w_in), "wout": f(w_out), "lwa": f(lru_w_a), "lwx": f(lru_w_x),
    }
    in_maps = []
    for c in range(NCORES):
        m = dict(common)
        m["xT"] = np.ascontiguousarray(x[c].T)
        in_maps.append(m)
    nc = build_nc()
    res = run_bass_kernel_spmd(nc, in_maps, core_ids=list(range(NCORES)))
    out = np.empty((NCORES, S, D), np.float32)
    for c in range(NCORES):
        out[c] = np.asarray(res.results[c]["yT"]).T
    return out
```

```python
import numpy as np
import concourse.bass as bass
import concourse.mybir as mybir
from concourse.bass_utils import run_bass_kernel_spmd

F32 = mybir.dt.float32
BF16 = mybir.dt.bfloat16
AF = mybir.ActivationFunctionType
ALU = mybir.AluOpType

D = 1024
S = 8192
T = 512
NT = S // T
DFF = 2816
KC = D // 128
FC = DFF // 128
NCORES = 8
NS = 4
SLOT = 4096
NTMP = 8
XY_ENG = "sp"
RMS_EPS = 1e-6
LN_EPS = 1e-5

C_G1, C_G2, C_G3, C_G4 = 0, 8, 16, 24
C_CB, C_LG, C_LB = 32, 36, 40
C_W4, C_B4 = 44, 60
C_BA, C_BX, C_LAM = 64, 68, 72
C_CW = 76
NCST = 200


class Buf:
    __slots__ = ("name", "last_w", "readers")

    def __init__(self, name):
        self.name = name
        self.last_w = None
        self.readers = []


class Prog:
    ENG = ("pe", "act", "dve", "pool", "sp")

    def __init__(self):
        self.q = {e: [] for e in self.ENG}
        self.count = {e: 0 for e in self.ENG}
        self.seen = {e: {} for e in self.ENG}
        self.dcount = {}
        self.need = {e: set() for e in self.ENG}

    def _deps(self, eng, reads, writes, is_dma):
        deps = []
        for b in reads:
            if b.last_w is not None:
                deps.append((b.last_w, True))
        for b in writes:
            if is_dma and b.readers:
                for t in b.readers:
                    deps.append((t, False))
                continue
            if b.last_w is not None:
                deps.append((b.last_w, False))
            for t in b.readers:
                deps.append((t, False))
        for (key, val), raw in deps:
            if key == eng:
                if eng in ("pe", "sp"):
                    continue
                pass
            if self.seen[eng].get(key, 0) >= val:
                continue
            self.seen[eng][key] = val
            if key in self.need:
                self.need[key].add(val)
            self.q[eng].append(("wait", key, val))

    def _commit(self, tok, reads, writes):
        for b in writes:
            b.last_w = tok
            b.readers = []
        for b in reads:
            if b in writes:
                continue
            rs = [t for t in b.readers if t[0] != tok[0]]
            rs.append(tok)
            b.readers = rs

    def op(self, eng, fn, reads=(), writes=(), inc=True):
        self._deps(eng, reads, writes, False)
        if inc:
            self.count[eng] += 1
            tok = (eng, self.count[eng])
            self.q[eng].append(("op", fn, eng, self.count[eng]))
        else:
            tok = (eng, self.count[eng] + 1)
            self.q[eng].append(("op", fn, None, 0))
        self._commit(tok, reads, writes)
        return tok

    def dma(self, eng, fn, key, reads=(), writes=()):
        self._deps(eng, reads, writes, True)
        self.dcount[key] = self.dcount.get(key, 0) + 16
        tok = (key, self.dcount[key])
        self.q[eng].append(("op", fn, key, 16))
        self._commit(tok, reads, writes)
        return tok

    def wait(self, eng, tok):
        key, val = tok
        if self.seen[eng].get(key, 0) >= val:
            return
        self.seen[eng][key] = val
        if key in self.need:
            self.need[key].add(val)
        self.q[eng].append(("wait", key, val))

    def finalize(self):
        rank = {}
        for e in self.ENG:
            rank[e] = {idx: r + 1 for r, idx in enumerate(sorted(self.need[e]))}
        out = {}
        for e in self.ENG:
            lst = []
            for it in self.q[e]:
                if it[0] == "wait":
                    key, val = it[1], it[2]
                    if key in rank:
                        lst.append(("wait", key, rank[key][val]))
                    else:
                        lst.append(it)
                else:
                    _, fn, key, amt = it
                    if key in rank:
                        if amt in rank[key]:
                            lst.append(("op", fn, key, 1))
                        else:
                            lst.append(("op", fn, None, 0))
                    else:
                        lst.append(it)
            out[e] = lst
        return out


_LAST_PROG = None


def build_nc(S=S):
    NT = S // T
    nc = bass.Bass("TRN2", target_bir_lowering=False)
    P = Prog()
    global _LAST_PROG
    _LAST_PROG = P

    def din(name, shape):
        return nc.dram_tensor(name, list(shape), F32, kind="ExternalInput").ap()

    xT = din("xT", [D, S])
    yT = nc.dram_tensor("yT", [D, S], F32, kind="ExternalOutput").ap()
    cst_d = din("cst", [128, NCST])
    ident_d = din("ident", [128, 128])
    wg_d = [din("wg1", [D, DFF]), din("wg2", [D, DFF])]
    wu_d = [din("wu1", [D, DFF]), din("wu2", [D, DFF])]
    wd_d = [din("wd1", [DFF, D]), din("wd2", [DFF, D])]
    win_d = din("win", [D, 2048])
    wout_d = din("wout", [D, D])
    lwa_d = din("lwa", [8, 64, 64])
    lwx_d = din("lwx", [8, 64, 64])

    def dscr(name, shape):
        return nc.dram_tensor(name, list(shape), BF16, kind="Internal").ap()

    gu_s = [dscr("gu1s", [11, 128, 4096]), dscr("gu2s", [11, 128, 4096])]
    dd_s = [dscr("dd1s", [8, 128, 2816]), dscr("dd2s", [8, 128, 2816])]
    win_s = dscr("wins", [4, 128, 4096])
    wout_s = dscr("wouts", [2, 128, 4096])
    cd_s = dscr("cds", [2, 128, 3968])

    sb = nc.alloc_sbuf_tensor
    cst = sb("cst_sb", [128, NCST], F32)
    der = sb("der_sb", [128, 32], F32)
    ident = sb("ident_sb", [128, 128], F32)
    onesD = sb("onesD", [128, 128], BF16)
    onesC = sb("onesC", [128, 128], BF16)
    gatew = sb("gatew", [128, 8, 128], BF16)
    xs = [sb("x0", [128, KC, T], F32), sb("x1", [128, KC, T], F32)]
    xn = sb("xn", [128, KC, T], BF16)
    xn2 = sb("xn2", [128, KC, T], BF16)
    tmpa = sb("tmpa", [128, 4, T], F32)
    sq = sb("sq", [128, 8, T], BF16)
    h = sb("h", [128, FC, T], BF16)
    ring = sb("ring", [128, NS, SLOT], BF16)
    ubuf = sb("ubuf", [128, 2, 30 + T], F32)
    ubb = sb("ubb", [128, 2, 30 + T], BF16)
    v = sb("v", [128, 4, T], F32)
    xbr = sb("xbr", [128, 4, 3 + T], F32)
    xr = sb("xr", [128, 4, T], F32)
    xrb = sb("xrb", [128, 4, T], BF16)
    hout = sb("hout", [128, 4, T], F32)
    hst = sb("hst", [128, 4], F32)
    mixo = sb("mixo", [128, KC, T], BF16)
    tmpt = sb("tmpt", [128, NTMP, T], F32)
    glt = sb("glt", [128, 4, T], F32)
    igt = sb("igt", [128, 4, T], F32)
    banks = [nc.alloc_psum_tensor("bank%d" % i, [128, T], F32) for i in range(8)]

    CONST = Buf("const")
    DER = Buf("der")
    GATEW = Buf("gatew")
    X = [[Buf("x%d_%d" % (b, k)) for k in range(KC)] for b in range(2)]
    XN = [Buf("xn%d" % k) for k in range(KC)]
    XN2 = [Buf("xn2_%d" % k) for k in range(KC)]
    TMPA = [Buf("tmpa%d" % k) for k in range(4)]
    SQ = [Buf("sq%d" % k) for k in range(8)]
    H = [Buf("h%d" % k) for k in range(FC)]
    RING = [Buf("ring%d" % k) for k in range(NS)]
    UB = [Buf("ub%d" % k) for k in range(4)]
    V = [Buf("v%d" % k) for k in range(4)]
    XBR = [Buf("xbr%d" % k) for k in range(4)]
    XR = [Buf("xr%d" % k) for k in range(4)]
    XRB = [Buf("xrb%d" % k) for k in range(4)]
    HOUT = [Buf("hout%d" % k) for k in range(4)]
    HST = [Buf("hst%d" % k) for k in range(4)]
    MIXO = [Buf("mixo%d" % k) for k in range(KC)]
    TMP = [Buf("tmp%d" % k) for k in range(NTMP)]
    GL = [Buf("gl%d" % k) for k in range(4)]
    IG = [Buf("ig%d" % k) for k in range(4)]
    BANK = [Buf("bank%d" % k) for k in range(8)]
    FAM = {n: Buf(n) for n in ("gu0", "gu1", "dd0", "dd1", "win", "wout", "cd")}

    st = {"bank": 0, "tmp": 0, "ring": 0}

    def bank():
        i = st["bank"]
        st["bank"] = (i + 1) % 8
        return banks[i][:], BANK[i]

    def tmp():
        i = st["tmp"]
        st["tmp"] = (i + 1) % NTMP
        return tmpt[:, i, :], TMP[i]

    def tmp_a():
        i = st.get("tmpa", 0)
        st["tmpa"] = (i + 1) % 4
        return tmpa[:, i, :], TMPA[i]

    def col(c):
        return cst[:, c:c + 1]

    ring_pre = []

    def ring_issue(src, U, fam):
        s = st["ring"]
        st["ring"] = (s + 1) % NS
        dst = ring[:, s, 0:U]
        P.dma("sp", lambda e: e.dma_start(out=dst, in_=src), ("ring", s),
              reads=[FAM[fam]], writes=[RING[s]])
        return ring[:, s, 0:U], RING[s]

    def ring_prefetch(key, src, U, fam):
        ring_pre.append((key, ring_issue(src, U, fam)))

    def ring_load(src, U, fam, key=None):
        if key is not None and ring_pre and ring_pre[0][0] == key:
            return ring_pre.pop(0)[1]
        assert key is None or all(k_ != key for k_, _ in ring_pre)
        return ring_issue(src, U, fam)

    def xview(ap, i):
        return ap.rearrange("(kc p) t -> p kc t", p=128)[:, :, i * T:(i + 1) * T]

    def load_x(i):
        b = i % 2
        dst = xs[b][:]
        src = xview(xT, i)
        P.dma(XY_ENG, lambda e: e.dma_start(out=dst, in_=src), ("xld", b),
              reads=[], writes=X[b])

    def store_y(i):
        b = i % 2
        src = xs[b][:]
        dst = xview(yT, i)
        return P.dma(XY_ENG, lambda e: e.dma_start(out=dst, in_=src), ("yst", b),
                     reads=X[b], writes=[])

    load_x(0)
    P.dma("sp", lambda e: e.dma_start(out=cst[:], in_=cst_d), ("setup", 0), writes=[CONST])
    P.dma("sp", lambda e: e.dma_start(out=ident[:], in_=ident_d), ("setup", 1), writes=[DER])
    P.op("dve", lambda e: e.memset(gatew[:], 0.0), writes=[GATEW])
    P.op("dve", lambda e: e.memset(onesD[:], 1.0 / 1024.0), writes=[CONST])
    P.op("dve", lambda e: e.memset(onesC[:], 1.0 / 512.0), writes=[CONST])
    P.op("dve", lambda e: e.memset(ubuf[:, :, 0:30], 0.0), writes=UB[0:2])
    P.op("dve", lambda e: e.memset(ubb[:, :, 0:30], 0.0), writes=UB[2:4])
    P.op("dve", lambda e: e.memset(xbr[:, :, 0:3], 0.0), writes=XBR)
    P.op("dve", lambda e: e.memset(hst[:], 0.0), writes=HST)
    stg = [xs[1][:].rearrange("p a b -> p (a b)"), tmpt[:].rearrange("p a b -> p (a b)")]
    STG = [Buf("stg0"), Buf("stg1")]
    cvt = {"i": 0, "cast": 0, "slot": 0}

    stgG = stg[0][:, 0:1024].rearrange("p (g m) -> p g m", g=8)
    P.op("dve", lambda e: e.memset(stgG, 0.0), writes=[STG[0]])
    gtok = None
    for g, src in enumerate((lwa_d, lwx_d)):
        for hd in range(8):
            ch, half = hd // 2, hd % 2
            dst = stgG[half * 64:(half + 1) * 64, g * 4 + ch, half * 64:(half + 1) * 64]
            s_ap = src[hd]
            gtok = P.dma("sp", lambda e, dst=dst, s_ap=s_ap: e.dma_start(out=dst, in_=s_ap),
                         ("gwld", 0), reads=[], writes=[STG[0]] if gtok is None else [])
    STG[0].last_w = gtok
    STG[0].readers = []
    P.op("act", lambda e: e.activation(out=gatew[:], in_=stgG, func=AF.Copy), reads=[STG[0]], writes=[GATEW])

    def cv_job(srcs, slot, c0):
        k = cvt["i"] % 2
        cvt["i"] += 1
        off = 0
        tok = None
        for src, n, a_ in srcs:
            dstv = stg[k][:, off:off + n].rearrange("p (a b) -> p a b", a=a_)
            tok = P.dma("sp", lambda e, dstv=dstv, src=src: e.dma_start(out=dstv, in_=src), ("cvld", k),
                        reads=[], writes=[STG[k]] if tok is None else [])
            off += n
        STG[k].last_w = tok
        STG[k].readers = []
        dsts = ring[:, slot, c0:c0 + off]
        srcs_ = stg[k][:, 0:off]
        if cvt["cast"] % 2 == 0:
            P.op("act", lambda e: e.activation(out=dsts, in_=srcs_, func=AF.Copy), reads=[STG[k]], writes=[RING[slot]])
        else:
            P.op("dve", lambda e: e.tensor_copy(out=dsts, in_=srcs_), reads=[STG[k]], writes=[RING[slot]])
        cvt["cast"] += 1

    units = []
    for l in range(2):
        for u in range(11):
            sg_ = wg_d[l].rearrange("(kc p) (u j) -> u p kc j", p=128, j=256)[u]
            su_ = wu_d[l].rearrange("(kc p) (u j) -> u p kc j", p=128, j=256)[u]
            units.append(([[(sg_, 2048, 8), (su_, 2048, 8)]], gu_s[l][u], 4096))
        for u in range(8):
            sd_ = wd_d[l].rearrange("(kc p) (u j) -> u p kc j", p=128, j=128)[u]
            units.append(([[(sd_, 2816, 22)]], dd_s[l][u], 2816))
        if l == 0:
            for u in range(4):
                units.append(([[(win_d.rearrange("(kc p) (u j) -> u p kc j", p=128, j=512)[u], 4096, 8)]], win_s[u], 4096))
            for u in range(2):
                units.append(([[(wout_d.rearrange("(kc p) (u j) -> u p kc j", p=128, j=512)[u], 4096, 8)]], wout_s[u], 4096))
    cv_store_toks = {}

    def cv_store(slot, dram, U):
        cv_store_toks[slot] = P.dma("sp", lambda e: e.dma_start(out=dram, in_=ring[:, slot, 0:U]), ("cvst", slot),
                                    reads=[RING[slot]], writes=[])

    prev = None
    for jobs, dram, U in units:
        slot = cvt["slot"]
        cvt["slot"] = (slot + 1) % NS
        c0 = 0
        for srcs in jobs:
            cv_job(srcs, slot, c0)
            c0 += sum(n for _, n, _ in srcs)
        if prev is not None:
            cv_store(*prev)
        prev = (slot, dram, U)
    cv_store(*prev)

    cd_toks = []
    for j, ch in enumerate((2, 3)):
        hb = H[0] if j == 0 else H[8]
        base = 0 if j == 0 else 8
        stage = h[:, base:base + 8, :].rearrange("p a b -> p (a b)")[:, 0:3968]
        stage3 = stage.rearrange("p (k j) -> p k j", k=31)
        for k in range(31):
            c = C_CW + k * 4 + ch
            P.op("dve", lambda e, k=k, c=c, stage3=stage3: e.tensor_scalar(
                out=stage3[:, k, :], in0=ident[:], scalar1=col(c), scalar2=None, op0=ALU.mult),
                reads=[CONST, DER], writes=[hb] if k == 0 else [], inc=(k == 30))
        hb.last_w = ("dve", P.count["dve"])
        hb.readers = []
        cd_toks.append(P.dma("sp", lambda e, j=j, stage=stage: e.dma_start(out=cd_s[j], in_=stage),
                             ("cdst", j), reads=[hb], writes=[]))

    def dcol(a, n=4):
        return der[:, a:a + n]

    lam = cst[:, C_LAM:C_LAM + 4]
    dops = []

    def dv(fn, reads=(DER, CONST)):
        P.op("dve", fn, reads=list(reads), writes=[DER])

    dv(lambda e: e.tensor_scalar(out=dcol(12), in0=lam, scalar1=-1.0, scalar2=None, op0=ALU.mult))
    dv(lambda e: e.tensor_tensor(out=dcol(16), in0=lam, in1=dcol(12), op=ALU.max))
    P.op("act", lambda e: e.activation(out=dcol(20), in_=dcol(16), func=AF.Exp, scale=-1.0),
         reads=[DER], writes=[DER])
    dv(lambda e: e.tensor_scalar(out=dcol(24), in0=dcol(20), scalar1=2.0, scalar2=None, op0=ALU.add))
    dv(lambda e: e.reciprocal(out=dcol(24), in_=dcol(24)))
    dv(lambda e: e.tensor_tensor(out=dcol(24), in0=dcol(24), in1=dcol(20), op=ALU.mult))
    dv(lambda e: e.tensor_tensor(out=dcol(28), in0=dcol(24), in1=dcol(24), op=ALU.mult))
    dv(lambda e: e.tensor_scalar(out=dcol(20), in0=dcol(28), scalar1=1.0 / 15.0, scalar2=1.0 / 13.0,
                                 op0=ALU.mult, op1=ALU.add))
    for cc in (1.0 / 11.0, 1.0 / 9.0, 1.0 / 7.0, 1.0 / 5.0, 1.0 / 3.0, 1.0):
        dv(lambda e: e.tensor_tensor(out=dcol(20), in0=dcol(20), in1=dcol(28), op=ALU.mult))
        dv(lambda e, cc=cc: e.tensor_scalar(out=dcol(20), in0=dcol(20), scalar1=cc, scalar2=None, op0=ALU.add))
    dv(lambda e: e.tensor_tensor(out=dcol(20), in0=dcol(20), in1=dcol(24), op=ALU.mult))
    dv(lambda e: e.tensor_scalar(out=dcol(12), in0=dcol(12), scalar1=0.0, scalar2=None, op0=ALU.max))
    dv(lambda e: e.scalar_tensor_tensor(out=dcol(12), in0=dcol(20), scalar=2.0, in1=dcol(12),
                                        op0=ALU.mult, op1=ALU.add))
    dv(lambda e: e.tensor_scalar(out=dcol(0), in0=dcol(12), scalar1=-8.0, scalar2=None, op0=ALU.mult))
    dv(lambda e: e.tensor_scalar(out=dcol(4), in0=dcol(12), scalar1=-16.0, scalar2=None, op0=ALU.mult))
    dv(lambda e: e.tensor_scalar(out=dcol(8), in0=dcol(12), scalar1=8.0, scalar2=None, op0=ALU.mult))

    def rms(b, gcol, final, xn_t=None, XN_t=None, tmpf=None):
        tmpf = tmpf or tmp
        ps, psb = bank()
        for kc in range(KC):
            P.op("act", lambda e, kc=kc: e.activation(out=sq[:, kc, :], in_=xs[b][:, kc, :], func=AF.Square),
                 reads=[X[b][kc]], writes=[SQ[kc]])
        for kc in range(KC):
            P.op("pe", lambda e, kc=kc: e.matmul(ps, lhsT=onesD[:], rhs=sq[:, kc, :],
                                                 start=(kc == 0), stop=(kc == KC - 1)),
                 reads=[SQ[kc], CONST], writes=[psb], inc=(kc == KC - 1))
        t1, t1b = tmpf()
        P.op("act", lambda e: e.activation(out=t1, in_=ps, func=AF.Sqrt, bias=RMS_EPS, scale=1.0),
             reads=[psb], writes=[t1b])
        t2, t2b = tmpf()
        P.op("dve", lambda e: e.reciprocal(out=t2, in_=t1), reads=[t1b], writes=[t2b])
        for kc in range(KC):
            out = xs[b][:, kc, :] if final else xn_t[:, kc, :]
            P.op("dve", lambda e, kc=kc, out=out: e.scalar_tensor_tensor(
                out=out, in0=xs[b][:, kc, :], scalar=col(gcol + kc), in1=t2, op0=ALU.mult, op1=ALU.mult),
                reads=[X[b][kc], t2b, CONST], writes=[X[b][kc] if final else XN_t[kc]])

    def ffn(b, l, xn_t, XN_t, tmpf):
        for u in range(11):
            slot, sbuf_ = ring_load(gu_s[l][u], 4096, "gu%d" % l)
            sv = slot.rearrange("p (a k j) -> p a k j", a=2, k=8)
            for jj in range(2):
                f = 2 * u + jj
                pg, pgb = bank()
                pu, pub = bank()
                for a_, (pp, ppb) in enumerate(((pg, pgb), (pu, pub))):
                    for kc in range(KC):
                        P.op("pe", lambda e, a_=a_, kc=kc, pp=pp, sv=sv, jj=jj: e.matmul(
                            pp, lhsT=sv[:, a_, kc, jj * 128:(jj + 1) * 128], rhs=xn_t[:, kc, :],
                            start=(kc == 0), stop=(kc == KC - 1)),
                            reads=[sbuf_, XN_t[kc]], writes=[ppb], inc=(kc == KC - 1))
                t, tb = tmpf()
                P.op("act", lambda e, t=t, pg=pg: e.activation(out=t, in_=pg, func=AF.Silu),
                     reads=[pgb], writes=[tb])
                P.op("dve", lambda e, t=t, pu=pu, f=f: e.tensor_tensor(out=h[:, f, :], in0=t, in1=pu, op=ALU.mult),
                     reads=[tb, pub], writes=[H[f]])
            yield
        for m in range(8):
            slot, sbuf_ = ring_load(dd_s[l][m], 2816, "dd%d" % l)
            sv = slot.rearrange("p (k j) -> p k j", k=FC)
            pd, pdb = bank()
            for kc in range(FC):
                P.op("pe", lambda e, kc=kc, pd=pd, sv=sv: e.matmul(
                    pd, lhsT=sv[:, kc, :], rhs=h[:, kc, :], start=(kc == 0), stop=(kc == FC - 1)),
                    reads=[sbuf_, H[kc]], writes=[pdb], inc=(kc == FC - 1))
            P.op("dve", lambda e, pd=pd, m=m: e.scalar_tensor_tensor(
                out=xs[b][:, m, :], in0=pd, scalar=0.5, in1=xs[b][:, m, :], op0=ALU.mult, op1=ALU.add),
                reads=[pdb, X[b][m]], writes=[X[b][m]])
            yield

    def proj4(unit_ap, fam, evac, key=None):
        slot, sbuf_ = ring_load(unit_ap, 4096, fam, key=key)
        sv = slot.rearrange("p (k j) -> p k j", k=KC)
        for ch in range(4):
            ps, psb = bank()
            for kc in range(KC):
                P.op("pe", lambda e, kc=kc, ps=ps, sv=sv, ch=ch: e.matmul(
                    ps, lhsT=sv[:, kc, ch * 128:(ch + 1) * 128], rhs=xn2[:, kc, :],
                    start=(kc == 0), stop=(kc == KC - 1)),
                    reads=[sbuf_, XN2[kc]], writes=[psb], inc=(kc == KC - 1))
            evac(ch, ps, psb)

    def mixer(b):
        sg = [tmp() for _ in range(4)]

        def ev_gate(ch, ps, psb):
            P.op("act", lambda e: e.activation(out=sg[ch][0], in_=ps, func=AF.Sigmoid),
                 reads=[psb], writes=[sg[ch][1]])

        def ev_val(ch, ps, psb):
            dst_u = ubuf[:, ch, 30:30 + T] if ch < 2 else ubb[:, ch - 2, 30:30 + T]
            P.op("dve", lambda e: e.tensor_tensor(out=dst_u, in0=ps, in1=sg[ch][0], op=ALU.mult),
                 reads=[psb, sg[ch][1]], writes=[UB[ch]])

        def ev_rx(ch, ps, psb):
            P.op("act", lambda e: e.activation(out=xbr[:, ch, 3:3 + T], in_=ps, func=AF.Copy),
                 reads=[psb], writes=[XBR[ch]])

        def ev_rg(ch, ps, psb):
            P.op("act", lambda e: e.activation(out=glt[:, ch, :], in_=ps, func=AF.Gelu_apprx_tanh),
                 reads=[psb], writes=[GL[ch]])

        def conv4_part():
            for ch in range(4):
                P.op("dve", lambda e, ch=ch: e.tensor_scalar(
                    out=xr[:, ch, :], in0=xbr[:, ch, 0:T], scalar1=col(C_W4 + ch), scalar2=col(C_B4 + ch),
                    op0=ALU.mult, op1=ALU.add), reads=[XBR[ch], CONST], writes=[XR[ch]])
                for k in range(1, 4):
                    P.op("dve", lambda e, ch=ch, k=k: e.scalar_tensor_tensor(
                        out=xr[:, ch, :], in0=xbr[:, ch, k:k + T], scalar=col(C_W4 + k * 4 + ch), in1=xr[:, ch, :],
                        op0=ALU.mult, op1=ALU.add), reads=[XBR[ch], XR[ch], CONST], writes=[XR[ch]])
                P.op("act", lambda e, ch=ch: e.activation(out=xbr[:, ch, 0:3], in_=xbr[:, ch, T:T + 3], func=AF.Copy),
                     reads=[XBR[ch]], writes=[XBR[ch]])
                P.op("act", lambda e, ch=ch: e.activation(out=xrb[:, ch, :], in_=xr[:, ch, :], func=AF.Copy),
                     reads=[XR[ch]], writes=[XRB[ch]])

        proj4(win_s[1], "win", ev_gate, key="win1")
        yield
        proj4(win_s[0], "win", ev_val, key="win0")
        yield
        proj4(win_s[2], "win", ev_rx, key="win2")
        yield
        proj4(win_s[3], "win", ev_rg)
        yield
        conv4_part()

        for j, ch in enumerate((2, 3)):
            slot, sbuf_ = ring_load(cd_s[j], 3968, "cd")
            sv = slot.rearrange("p (k j) -> p k j", k=31)
            ps, psb = bank()
            for k in range(31):
                P.op("pe", lambda e, k=k, ps=ps, sv=sv, j=j: e.matmul(
                    ps, lhsT=sv[:, k, :], rhs=ubb[:, j, k:k + T], start=(k == 0), stop=(k == 30)),
                    reads=[sbuf_, UB[ch]], writes=[psb], inc=(k == 30))
            P.op("act", lambda e, ps=ps, ch=ch: e.activation(out=v[:, ch, :], in_=ps, func=AF.Identity,
                                                             bias=col(C_CB + ch), scale=1.0),
                 reads=[psb, CONST], writes=[V[ch]])
            yield
        for k in range(31):
            acc, ACC = (v, V) if k % 2 == 0 else (hout, HOUT)
            for ch in range(2):
                if k == 0:
                    P.op("dve", lambda e, ch=ch: e.tensor_scalar(
                        out=v[:, ch, :], in0=ubuf[:, ch, 0:T], scalar1=col(C_CW + ch), scalar2=col(C_CB + ch),
                        op0=ALU.mult, op1=ALU.add), reads=[UB[ch], CONST], writes=[V[ch]], inc=(ch == 1))
                elif k == 1:
                    P.op("dve", lambda e, ch=ch: e.tensor_scalar(
                        out=hout[:, ch, :], in0=ubuf[:, ch, 1:1 + T], scalar1=col(C_CW + 4 + ch), scalar2=None,
                        op0=ALU.mult), reads=[UB[ch], CONST], writes=[HOUT[ch]], inc=(ch == 1))
                else:
                    P.op("dve", lambda e, ch=ch, k=k, acc=acc: e.scalar_tensor_tensor(
                        out=acc[:, ch, :], in0=ubuf[:, ch, k:k + T], scalar=col(C_CW + k * 4 + ch), in1=acc[:, ch, :],
                        op0=ALU.mult, op1=ALU.add), reads=[UB[ch], ACC[ch], CONST], writes=[ACC[ch]], inc=(ch == 1))
            if k % 4 == 3:
                yield
        for ch in range(2):
            P.op("dve", lambda e, ch=ch: e.tensor_tensor(out=v[:, ch, :], in0=v[:, ch, :], in1=hout[:, ch, :], op=ALU.add),
                 reads=[V[ch], HOUT[ch]], writes=[V[ch]])
        for ch in range(4):
            P.op("act", lambda e, ch=ch: e.activation(out=sq[:, 4 + ch, :], in_=v[:, ch, :], func=AF.Square),
                 reads=[V[ch]], writes=[SQ[4 + ch]])
            P.op("act", lambda e, ch=ch: e.activation(out=sq[:, ch, :], in_=v[:, ch, :], func=AF.Copy),
                 reads=[V[ch]], writes=[SQ[ch]])
            ub_ = ubuf[:, ch, :] if ch < 2 else ubb[:, ch - 2, :]
            P.op("act", lambda e, ub_=ub_: e.activation(out=ub_[:, 0:30], in_=ub_[:, T:T + 30], func=AF.Copy),
                 reads=[UB[ch]], writes=[UB[ch]])
        yield
        pm, pmb = bank()
        pe2, pe2b = bank()
        for ch in range(4):
            P.op("pe", lambda e, ch=ch: e.matmul(pm, lhsT=onesC[:], rhs=sq[:, ch, :], start=(ch == 0), stop=(ch == 3)),
                 reads=[SQ[ch], CONST], writes=[pmb], inc=(ch == 3))
        for ch in range(4):
            P.op("pe", lambda e, ch=ch: e.matmul(pe2, lhsT=onesC[:], rhs=sq[:, 4 + ch, :], start=(ch == 0), stop=(ch == 3)),
                 reads=[SQ[4 + ch], CONST], writes=[pe2b], inc=(ch == 3))
        yield
        mean, meanb = tmp()
        msq, msqb = tmp()
        P.op("act", lambda e: e.activation(out=mean, in_=pm, func=AF.Copy), reads=[pmb], writes=[meanb])
        P.op("act", lambda e: e.activation(out=msq, in_=pm, func=AF.Square), reads=[pmb], writes=[msqb])
        var, varb = tmp()
        P.op("dve", lambda e: e.tensor_tensor(out=var, in0=pe2, in1=msq, op=ALU.subtract),
             reads=[pe2b, msqb], writes=[varb])
        sd, sdb = tmp()
        P.op("act", lambda e: e.activation(out=sd, in_=var, func=AF.Sqrt, bias=LN_EPS, scale=1.0),
             reads=[varb], writes=[sdb])
        rstd, rstdb = tmp()
        P.op("dve", lambda e: e.reciprocal(out=rstd, in_=sd), reads=[sdb], writes=[rstdb])
        for ch in range(4):
            t1, t1b = tmp()
            P.op("dve", lambda e, ch=ch, t1=t1: e.tensor_tensor(out=t1, in0=v[:, ch, :], in1=mean, op=ALU.subtract),
                 reads=[V[ch], meanb], writes=[t1b])
            P.op("dve", lambda e, t1=t1: e.tensor_tensor(out=t1, in0=t1, in1=rstd, op=ALU.mult),
                 reads=[t1b, rstdb], writes=[t1b])
            P.op("act", lambda e, ch=ch, t1=t1: e.activation(out=mixo[:, ch, :], in_=t1, func=AF.Silu,
                                                             bias=col(C_LB + ch), scale=col(C_LG + ch)),
                 reads=[t1b, CONST], writes=[MIXO[ch]])

        yield
        for ch in range(4):
            pa, pab = bank()
            px, pxb = bank()
            P.op("pe", lambda e, ch=ch, pa=pa: e.matmul(pa, lhsT=gatew[:, ch, :], rhs=xrb[:, ch, :], start=True, stop=True),
                 reads=[GATEW, XRB[ch]], writes=[pab])
            P.op("pe", lambda e, ch=ch, px=px: e.matmul(px, lhsT=gatew[:, 4 + ch, :], rhs=xrb[:, ch, :], start=True, stop=True),
                 reads=[GATEW, XRB[ch]], writes=[pxb])
            P.op("act", lambda e, ch=ch, pa=pa: e.activation(out=v[:, ch, :], in_=pa, func=AF.Sigmoid,
                                                             bias=col(C_BA + ch), scale=1.0),
                 reads=[pab, CONST], writes=[V[ch]])
            P.op("act", lambda e, ch=ch, px=px: e.activation(out=igt[:, ch, :], in_=px, func=AF.Sigmoid,
                                                             bias=col(C_BX + ch), scale=1.0),
                 reads=[pxb, CONST], writes=[IG[ch]])
        yield
        for ch in range(4):
            P.op("dve", lambda e, ch=ch: e.tensor_tensor(out=igt[:, ch, :], in0=igt[:, ch, :], in1=xr[:, ch, :], op=ALU.mult),
                 reads=[IG[ch], XR[ch]], writes=[IG[ch]])
        yield
        for ch in range(4):
            P.op("act", lambda e, ch=ch: e.activation(out=xr[:, ch, :], in_=v[:, ch, :], func=AF.Exp, scale=der[:, ch:ch + 1]),
                 reads=[V[ch], DER], writes=[XR[ch]])
            P.op("act", lambda e, ch=ch: e.activation(out=hout[:, ch, :], in_=v[:, ch, :], func=AF.Tanh, scale=der[:, 8 + ch:9 + ch]),
                 reads=[V[ch], DER], writes=[HOUT[ch]])
            P.op("act", lambda e, ch=ch: e.activation(out=v[:, ch, :], in_=v[:, ch, :], func=AF.Exp, scale=der[:, 4 + ch:5 + ch]),
                 reads=[V[ch], DER], writes=[V[ch]])
        yield
        for ch in range(4):
            P.op("dve", lambda e, ch=ch: e.scalar_tensor_tensor(out=v[:, ch, :], in0=v[:, ch, :], scalar=1.0, in1=hout[:, ch, :],
                                                                op0=ALU.add, op1=ALU.mult),
                 reads=[V[ch], HOUT[ch]], writes=[V[ch]])
        yield
        for ch in range(4):
            P.op("act", lambda e, ch=ch: e.activation(out=v[:, ch, :], in_=v[:, ch, :], func=AF.Sqrt),
                 reads=[V[ch]], writes=[V[ch]])
        yield
        for ch in range(4):
            P.op("dve", lambda e, ch=ch: e.tensor_tensor(out=igt[:, ch, :], in0=igt[:, ch, :], in1=v[:, ch, :], op=ALU.mult),
                 reads=[IG[ch], V[ch]], writes=[IG[ch]])
            P.op("dve", lambda e, ch=ch: e.tensor_tensor_scan(
                out=hout[:, ch, :], data0=xr[:, ch, :], data1=igt[:, ch, :], initial=hst[:, ch:ch + 1], op0=ALU.mult, op1=ALU.add),
                reads=[XR[ch], IG[ch], HST[ch]], writes=[HOUT[ch]])
            P.op("act", lambda e, ch=ch: e.activation(out=hst[:, ch:ch + 1], in_=hout[:, ch, T - 1:T], func=AF.Copy),
                 reads=[HOUT[ch]], writes=[HST[ch]])
            P.op("dve", lambda e, ch=ch: e.tensor_tensor(out=mixo[:, 4 + ch, :], in0=hout[:, ch, :], in1=glt[:, ch, :], op=ALU.mult),
                 reads=[HOUT[ch], GL[ch]], writes=[MIXO[4 + ch]])

        yield
        yield
        for u in range(2):
            slot, sbuf_ = ring_load(wout_s[u], 4096, "wout")
            sv = slot.rearrange("p (k j) -> p k j", k=KC)
            for jj in range(4):
                m = 4 * u + jj
                ps, psb = bank()
                for kc in range(KC):
                    P.op("pe", lambda e, kc=kc, ps=ps, sv=sv, jj=jj: e.matmul(
                        ps, lhsT=sv[:, kc, jj * 128:(jj + 1) * 128], rhs=mixo[:, kc, :],
                        start=(kc == 0), stop=(kc == KC - 1)),
                        reads=[sbuf_, MIXO[kc]], writes=[psb], inc=(kc == KC - 1))
                P.op("dve", lambda e, ps=ps, m=m: e.tensor_tensor(out=xs[b][:, m, :], in0=ps, in1=xs[b][:, m, :], op=ALU.add),
                     reads=[psb, X[b][m]], writes=[X[b][m]])

    bar = [("act", P.count["act"]), ("dve", P.count["dve"])] + list(cv_store_toks.values()) + list(cd_toks)
    for e_ in ("pe", "act", "dve", "sp"):
        for t_ in bar:
            if t_[0] != e_ and t_[1] > 0:
                P.wait(e_, t_)

    def phaseA(i):
        b = i % 2
        rms(b, C_G1, False, xn, XN, tmp_a)
        yield
        yield from ffn(b, 0, xn, XN, tmp_a)

    def phaseB(i):
        b = i % 2
        rms(b, C_G2, False, xn2, XN2, tmp)
        yield
        yield from mixer(b)

    def phaseC(i, held, gb_next=None):
        b = i % 2
        rms(b, C_G3, False, xn2, XN2, tmp)
        for _ in held:
            pass
        for n_, _ in enumerate(ffn(b, 1, xn2, XN2, tmp)):
            if gb_next is not None and n_ == 15:
                next(gb_next)
        if i + 1 < NT:
            ring_prefetch("win1", win_s[1], 4096, "win")
            ring_prefetch("win0", win_s[0], 4096, "win")
            ring_prefetch("win2", win_s[2], 4096, "win")
        rms(b, C_G4, True)
        return store_y(i)

    if NT > 1:
        load_x(1)
    for _ in phaseA(0):
        pass
    last_store = {}
    HOLD = 3
    gb = phaseB(0)
    yoff = 0
    for i in range(NT):
        steps = list(range(21)) if i + 1 < NT else []
        ga = phaseA(i + 1) if i + 1 < NT else iter(())
        given = 0
        for yi, _ in enumerate(gb, start=yoff):
            if yi != 3 and yi < 6:
                continue
            if given < len(steps) - HOLD:
                next(ga, None)
                given += 1
        while given < len(steps) - HOLD:
            next(ga, None)
            given += 1
        gb = phaseB(i + 1) if i + 1 < NT else None
        yoff = 1
        last_store[i % 2] = phaseC(i, ga, gb)
        if i + 2 < NT:
            load_x(i + 2)
    for b in last_store:
        P.wait(XY_ENG, last_store[b])

    sems = {}

    def sem_of(key):
        if key not in sems:
            nm = key if isinstance(key, str) else "%s%d" % key
            sems[key] = nc.alloc_semaphore("s_" + nm)
        return sems[key]

    FQ = P.finalize()
    P.fq = FQ
    for e in Prog.ENG:
        for it in FQ[e]:
            if it[0] == "wait":
                sem_of(it[1])
            elif it[2] is not None:
                sem_of(it[2])

    def emit(eng, name):
        for it in FQ[name]:
            if it[0] == "wait":
                eng.wait_ge(sems[it[1]], it[2])
            else:
                ins = it[1](eng)
                if it[2] is not None:
                    ins.then_inc(sems[it[2]], it[3])

    with nc.Block() as block:
        @block.sync
        def _(e):
            emit(e, "sp")

        @block.scalar
        def _(e):
            emit(e, "act")

        @block.vector
        def _(e):
            emit(e, "dve")

        @block.gpsimd
        def _(e):
            emit(e, "pool")

        @block.tensor
        def _(e):
            emit(e, "pe")
    return nc


def _cols(vv, n):
    return np.ascontiguousarray(np.asarray(vv, np.float32).reshape(n, 128).T)


def kernel(x, ffn1_norm, ffn1_w_gate, ffn1_w_up, ffn1_w_down, mix_norm, w_in,
           conv_dw, conv_dw_bias, conv_ln_g, conv_ln_b, lru_conv_w, lru_conv_b,
           lru_w_a, lru_b_a, lru_w_x, lru_b_x, lru_lambda, w_out,
           ffn2_norm, ffn2_w_gate, ffn2_w_up, ffn2_w_down, final_norm):
    f = lambda a: np.ascontiguousarray(np.asarray(a, np.float32))
    x = f(x)
    cst = np.concatenate([
        _cols(ffn1_norm, 8), _cols(mix_norm, 8), _cols(ffn2_norm, 8), _cols(final_norm, 8),
        _cols(conv_dw_bias, 4), _cols(conv_ln_g, 4), _cols(conv_ln_b, 4),
        f(lru_conv_w).reshape(4, 4, 128).transpose(2, 0, 1).reshape(128, 16),
        _cols(lru_conv_b, 4), _cols(lru_b_a, 4), _cols(lru_b_x, 4), _cols(lru_lambda, 4),
        f(conv_dw).reshape(31, 4, 128).transpose(2, 0, 1).reshape(128, 124),
    ], axis=1)
    cst = np.ascontiguousarray(cst, np.float32)
    assert cst.shape == (128, NCST)
    common = {
        "cst": cst, "ident": np.eye(128, dtype=np.float32),
        "wg1": f(ffn1_w_gate), "wu1": f(ffn1_w_up), "wd1": f(ffn1_w_down),
        "wg2": f(ffn2_w_gate), "wu2": f(ffn2_w_up), "wd2": f(ffn2_w_down),
        "win": f(w_in), "wout": f(w_out), "lwa": f(lru_w_a), "lwx": f(lru_w_x),
    }
    in_maps = []
    for c in range(NCORES):
        m = dict(common)
        m["xT"] = np.ascontiguousarray(x[c].T)
        in_maps.append(m)
    nc = build_nc()
    res = run_bass_kernel_spmd(nc, in_maps, core_ids=list(range(NCORES)))
    out = np.empty((NCORES, S, D), np.float32)
    for c in range(NCORES):
        out[c] = np.asarray(res.results[c]["yT"]).T
    return out
```

```python
import numpy as np
import concourse.bass as bass
import concourse.mybir as mybir
from concourse.bass_utils import run_bass_kernel_spmd

F32 = mybir.dt.float32
BF16 = mybir.dt.bfloat16
AF = mybir.ActivationFunctionType
ALU = mybir.AluOpType

D = 1024
S = 8192
T = 512
NT = S // T
DFF = 2816
KC = D // 128
FC = DFF // 128
NCORES = 8
NS = 4
SLOT = 4096
NTMP = 8
XY_ENG = "sp"
RMS_EPS = 1e-6
LN_EPS = 1e-5

C_G1, C_G2, C_G3, C_G4 = 0, 8, 16, 24
C_CB, C_LG, C_LB = 32, 36, 40
C_W4, C_B4 = 44, 60
C_BA, C_BX, C_LAM = 64, 68, 72
C_CW = 76
NCST = 200


class Buf:
    __slots__ = ("name", "last_w", "readers")

    def __init__(self, name):
        self.name = name
        self.last_w = None
        self.readers = []


class Prog:
    ENG = ("pe", "act", "dve", "pool", "sp")

    def __init__(self):
        self.q = {e: [] for e in self.ENG}
        self.count = {e: 0 for e in self.ENG}
        self.seen = {e: {} for e in self.ENG}
        self.dcount = {}
        self.need = {e: set() for e in self.ENG}

    def _deps(self, eng, reads, writes, is_dma):
        deps = []
        for b in reads:
            if b.last_w is not None:
                deps.append((b.last_w, True))
        for b in writes:
            if is_dma and b.readers:
                for t in b.readers:
                    deps.append((t, False))
                continue
            if b.last_w is not None:
                deps.append((b.last_w, False))
            for t in b.readers:
                deps.append((t, False))
        for (key, val), raw in deps:
            if key == eng:
                if eng in ("pe", "sp"):
                    continue
                pass
            if self.seen[eng].get(key, 0) >= val:
                continue
            self.seen[eng][key] = val
            if key in self.need:
                self.need[key].add(val)
            self.q[eng].append(("wait", key, val))

    def _commit(self, tok, reads, writes):
        for b in writes:
            b.last_w = tok
            b.readers = []
        for b in reads:
            if b in writes:
                continue
            rs = [t for t in b.readers if t[0] != tok[0]]
            rs.append(tok)
            b.readers = rs

    def op(self, eng, fn, reads=(), writes=(), inc=True):
        self._deps(eng, reads, writes, False)
        if inc:
            self.count[eng] += 1
            tok = (eng, self.count[eng])
            self.q[eng].append(("op", fn, eng, self.count[eng]))
        else:
            tok = (eng, self.count[eng] + 1)
            self.q[eng].append(("op", fn, None, 0))
        self._commit(tok, reads, writes)
        return tok

    def dma(self, eng, fn, key, reads=(), writes=()):
        self._deps(eng, reads, writes, True)
        self.dcount[key] = self.dcount.get(key, 0) + 16
        tok = (key, self.dcount[key])
        self.q[eng].append(("op", fn, key, 16))
        self._commit(tok, reads, writes)
        return tok

    def wait(self, eng, tok):
        key, val = tok
        if self.seen[eng].get(key, 0) >= val:
            return
        self.seen[eng][key] = val
        if key in self.need:
            self.need[key].add(val)
        self.q[eng].append(("wait", key, val))

    def finalize(self):
        rank = {}
        for e in self.ENG:
            rank[e] = {idx: r + 1 for r, idx in enumerate(sorted(self.need[e]))}
        out = {}
        for e in self.ENG:
            lst = []
            for it in self.q[e]:
                if it[0] == "wait":
                    key, val = it[1], it[2]
                    if key in rank:
                        lst.append(("wait", key, rank[key][val]))
                    else:
                        lst.append(it)
                else:
                    _, fn, key, amt = it
                    if key in rank:
                        if amt in rank[key]:
                            lst.append(("op", fn, key, 1))
                        else:
                            lst.append(("op", fn, None, 0))
                    else:
                        lst.append(it)
            out[e] = lst
        return out


_LAST_PROG = None


def build_nc(S=S):
    NT = S // T
    nc = bass.Bass("TRN2", target_bir_lowering=False)
    P = Prog()
    global _LAST_PROG
    _LAST_PROG = P

    def din(name, shape):
        return nc.dram_tensor(name, list(shape), F32, kind="ExternalInput").ap()

    xT = din("xT", [D, S])
    yT = nc.dram_tensor("yT", [D, S], F32, kind="ExternalOutput").ap()
    cst_d = din("cst", [128, NCST])
    ident_d = din("ident", [128, 128])
    wg_d = [din("wg1", [D, DFF]), din("wg2", [D, DFF])]
    wu_d = [din("wu1", [D, DFF]), din("wu2", [D, DFF])]
    wd_d = [din("wd1", [DFF, D]), din("wd2", [DFF, D])]
    win_d = din("win", [D, 2048])
    wout_d = din("wout", [D, D])
    lwa_d = din("lwa", [8, 64, 64])
    lwx_d = din("lwx", [8, 64, 64])

    def dscr(name, shape):
        return nc.dram_tensor(name, list(shape), BF16, kind="Internal").ap()

    gu_s = [dscr("gu1s", [11, 128, 4096]), dscr("gu2s", [11, 128, 4096])]
    dd_s = [dscr("dd1s", [8, 128, 2816]), dscr("dd2s", [8, 128, 2816])]
    win_s = dscr("wins", [4, 128, 4096])
    wout_s = dscr("wouts", [2, 128, 4096])
    cd_s = dscr("cds", [2, 128, 3968])

    sb = nc.alloc_sbuf_tensor
    cst = sb("cst_sb", [128, NCST], F32)
    der = sb("der_sb", [128, 32], F32)
    ident = sb("ident_sb", [128, 128], F32)
    onesD = sb("onesD", [128, 128], BF16)
    onesC = sb("onesC", [128, 128], BF16)
    gatew = sb("gatew", [128, 8, 128], BF16)
    xs = [sb("x0", [128, KC, T], F32), sb("x1", [128, KC, T], F32)]
    xn = sb("xn", [128, KC, T], BF16)
    xn2 = sb("xn2", [128, KC, T], BF16)
    tmpa = sb("tmpa", [128, 4, T], F32)
    sq = sb("sq", [128, 8, T], BF16)
    h = sb("h", [128, FC, T], BF16)
    ring = sb("ring", [128, NS, SLOT], BF16)
    ubuf = sb("ubuf", [128, 2, 30 + T], F32)
    ubb = sb("ubb", [128, 2, 30 + T], BF16)
    v = sb("v", [128, 4, T], F32)
    xbr = sb("xbr", [128, 4, 3 + T], F32)
    xr = sb("xr", [128, 4, T], F32)
    xrb = sb("xrb", [128, 4, T], BF16)
    hout = sb("hout", [128, 4, T], F32)
    hst = sb("hst", [128, 4], F32)
    mixo = sb("mixo", [128, KC, T], BF16)
    tmpt = sb("tmpt", [128, NTMP, T], F32)
    glt = sb("glt", [128, 4, T], F32)
    igt = sb("igt", [128, 4, T], F32)
    banks = [nc.alloc_psum_tensor("bank%d" % i, [128, T], F32) for i in range(8)]

    CONST = Buf("const")
    DER = Buf("der")
    GATEW = Buf("gatew")
    X = [[Buf("x%d_%d" % (b, k)) for k in range(KC)] for b in range(2)]
    XN = [Buf("xn%d" % k) for k in range(KC)]
    XN2 = [Buf("xn2_%d" % k) for k in range(KC)]
    TMPA = [Buf("tmpa%d" % k) for k in range(4)]
    SQ = [Buf("sq%d" % k) for k in range(8)]
    H = [Buf("h%d" % k) for k in range(FC)]
    RING = [Buf("ring%d" % k) for k in range(NS)]
    UB = [Buf("ub%d" % k) for k in range(4)]
    V = [Buf("v%d" % k) for k in range(4)]
    XBR = [Buf("xbr%d" % k) for k in range(4)]
    XR = [Buf("xr%d" % k) for k in range(4)]
    XRB = [Buf("xrb%d" % k) for k in range(4)]
    HOUT = [Buf("hout%d" % k) for k in range(4)]
    HST = [Buf("hst%d" % k) for k in range(4)]
    MIXO = [Buf("mixo%d" % k) for k in range(KC)]
    TMP = [Buf("tmp%d" % k) for k in range(NTMP)]
    GL = [Buf("gl%d" % k) for k in range(4)]
    IG = [Buf("ig%d" % k) for k in range(4)]
    BANK = [Buf("bank%d" % k) for k in range(8)]
    FAM = {n: Buf(n) for n in ("gu0", "gu1", "dd0", "dd1", "win", "wout", "cd")}

    st = {"bank": 0, "tmp": 0, "ring": 0}

    def bank():
        i = st["bank"]
        st["bank"] = (i + 1) % 8
        return banks[i][:], BANK[i]

    def tmp():
        i = st["tmp"]
        st["tmp"] = (i + 1) % NTMP
        return tmpt[:, i, :], TMP[i]

    def tmp_a():
        i = st.get("tmpa", 0)
        st["tmpa"] = (i + 1) % 4
        return tmpa[:, i, :], TMPA[i]

    def col(c):
        return cst[:, c:c + 1]

    ring_pre = []

    def ring_issue(src, U, fam):
        s = st["ring"]
        st["ring"] = (s + 1) % NS
        dst = ring[:, s, 0:U]
        P.dma("sp", lambda e: e.dma_start(out=dst, in_=src), ("ring", s),
              reads=[FAM[fam]], writes=[RING[s]])
        return ring[:, s, 0:U], RING[s]

    def ring_prefetch(key, src, U, fam):
        ring_pre.append((key, ring_issue(src, U, fam)))

    def ring_load(src, U, fam, key=None):
        if key is not None and ring_pre and ring_pre[0][0] == key:
            return ring_pre.pop(0)[1]
        assert key is None or all(k_ != key for k_, _ in ring_pre)
        return ring_issue(src, U, fam)

    def xview(ap, i):
        return ap.rearrange("(kc p) t -> p kc t", p=128)[:, :, i * T:(i + 1) * T]

    def load_x(i):
        b = i % 2
        dst = xs[b][:]
        src = xview(xT, i)
        P.dma(XY_ENG, lambda e: e.dma_start(out=dst, in_=src), ("xld", b),
              reads=[], writes=X[b])

    def store_y(i):
        b = i % 2
        src = xs[b][:]
        dst = xview(yT, i)
        return P.dma(XY_ENG, lambda e: e.dma_start(out=dst, in_=src), ("yst", b),
                     reads=X[b], writes=[])

    load_x(0)
    P.dma("sp", lambda e: e.dma_start(out=cst[:], in_=cst_d), ("setup", 0), writes=[CONST])
    P.dma("sp", lambda e: e.dma_start(out=ident[:], in_=ident_d), ("setup", 1), writes=[DER])
    P.op("dve", lambda e: e.memset(gatew[:], 0.0), writes=[GATEW])
    P.op("dve", lambda e: e.memset(onesD[:], 1.0 / 1024.0), writes=[CONST])
    P.op("dve", lambda e: e.memset(onesC[:], 1.0 / 512.0), writes=[CONST])
    P.op("dve", lambda e: e.memset(ubuf[:, :, 0:30], 0.0), writes=UB[0:2])
    P.op("dve", lambda e: e.memset(ubb[:, :, 0:30], 0.0), writes=UB[2:4])
    P.op("dve", lambda e: e.memset(xbr[:, :, 0:3], 0.0), writes=XBR)
    P.op("dve", lambda e: e.memset(hst[:], 0.0), writes=HST)
    stg = [xs[1][:].rearrange("p a b -> p (a b)"), tmpt[:].rearrange("p a b -> p (a b)")]
    STG = [Buf("stg0"), Buf("stg1")]
    cvt = {"i": 0, "cast": 0, "slot": 0}

    stgG = stg[0][:, 0:1024].rearrange("p (g m) -> p g m", g=8)
    P.op("dve", lambda e: e.memset(stgG, 0.0), writes=[STG[0]])
    gtok = None
    for g, src in enumerate((lwa_d, lwx_d)):
        for hd in range(8):
            ch, half = hd // 2, hd % 2
            dst = stgG[half * 64:(half + 1) * 64, g * 4 + ch, half * 64:(half + 1) * 64]
            s_ap = src[hd]
            gtok = P.dma("sp", lambda e, dst=dst, s_ap=s_ap: e.dma_start(out=dst, in_=s_ap),
                         ("gwld", 0), reads=[], writes=[STG[0]] if gtok is None else [])
    STG[0].last_w = gtok
    STG[0].readers = []
    P.op("act", lambda e: e.activation(out=gatew[:], in_=stgG, func=AF.Copy), reads=[STG[0]], writes=[GATEW])

    def cv_job(srcs, slot, c0):
        k = cvt["i"] % 2
        cvt["i"] += 1
        off = 0
        tok = None
        for src, n, a_ in srcs:
            dstv = stg[k][:, off:off + n].rearrange("p (a b) -> p a b", a=a_)
            tok = P.dma("sp", lambda e, dstv=dstv, src=src: e.dma_start(out=dstv, in_=src), ("cvld", k),
                        reads=[], writes=[STG[k]] if tok is None else [])
            off += n
        STG[k].last_w = tok
        STG[k].readers = []
        dsts = ring[:, slot, c0:c0 + off]
        srcs_ = stg[k][:, 0:off]
        if cvt["cast"] % 2 == 0:
            P.op("act", lambda e: e.activation(out=dsts, in_=srcs_, func=AF.Copy), reads=[STG[k]], writes=[RING[slot]])
        else:
            P.op("dve", lambda e: e.tensor_copy(out=dsts, in_=srcs_), reads=[STG[k]], writes=[RING[slot]])
        cvt["cast"] += 1

    units = []
    for l in range(2):
        for u in range(11):
            sg_ = wg_d[l].rearrange("(kc p) (u j) -> u p kc j", p=128, j=256)[u]
            su_ = wu_d[l].rearrange("(kc p) (u j) -> u p kc j", p=128, j=256)[u]
            units.append(([[(sg_, 2048, 8), (su_, 2048, 8)]], gu_s[l][u], 4096))
        for u in range(8):
            sd_ = wd_d[l].rearrange("(kc p) (u j) -> u p kc j", p=128, j=128)[u]
            units.append(([[(sd_, 2816, 22)]], dd_s[l][u], 2816))
        if l == 0:
            for u in range(4):
                units.append(([[(win_d.rearrange("(kc p) (u j) -> u p kc j", p=128, j=512)[u], 4096, 8)]], win_s[u], 4096))
            for u in range(2):
                units.append(([[(wout_d.rearrange("(kc p) (u j) -> u p kc j", p=128, j=512)[u], 4096, 8)]], wout_s[u], 4096))
    cv_store_toks = {}

    def cv_store(slot, dram, U):
        cv_store_toks[slot] = P.dma("sp", lambda e: e.dma_start(out=dram, in_=ring[:, slot, 0:U]), ("cvst", slot),
                                    reads=[RING[slot]], writes=[])

    prev = None
    for jobs, dram, U in units:
        slot = cvt["slot"]
        cvt["slot"] = (slot + 1) % NS
        c0 = 0
        for srcs in jobs:
            cv_job(srcs, slot, c0)
            c0 += sum(n for _, n, _ in srcs)
        if prev is not None:
            cv_store(*prev)
        prev = (slot, dram, U)
    cv_store(*prev)

    cd_toks = []
    for j, ch in enumerate((2, 3)):
        hb = H[0] if j == 0 else H[8]
        base = 0 if j == 0 else 8
        stage = h[:, base:base + 8, :].rearrange("p a b -> p (a b)")[:, 0:3968]
        stage3 = stage.rearrange("p (k j) -> p k j", k=31)
        for k in range(31):
            c = C_CW + k * 4 + ch
            P.op("dve", lambda e, k=k, c=c, stage3=stage3: e.tensor_scalar(
                out=stage3[:, k, :], in0=ident[:], scalar1=col(c), scalar2=None, op0=ALU.mult),
                reads=[CONST, DER], writes=[hb] if k == 0 else [], inc=(k == 30))
        hb.last_w = ("dve", P.count["dve"])
        hb.readers = []
        cd_toks.append(P.dma("sp", lambda e, j=j, stage=stage: e.dma_start(out=cd_s[j], in_=stage),
                             ("cdst", j), reads=[hb], writes=[]))

    def dcol(a, n=4):
        return der[:, a:a + n]

    lam = cst[:, C_LAM:C_LAM + 4]
    dops = []

    def dv(fn, reads=(DER, CONST)):
        P.op("dve", fn, reads=list(reads), writes=[DER])

    dv(lambda e: e.tensor_scalar(out=dcol(12), in0=lam, scalar1=-1.0, scalar2=None, op0=ALU.mult))
    dv(lambda e: e.tensor_tensor(out=dcol(16), in0=lam, in1=dcol(12), op=ALU.max))
    P.op("act", lambda e: e.activation(out=dcol(20), in_=dcol(16), func=AF.Exp, scale=-1.0),
         reads=[DER], writes=[DER])
    dv(lambda e: e.tensor_scalar(out=dcol(24), in0=dcol(20), scalar1=2.0, scalar2=None, op0=ALU.add))
    dv(lambda e: e.reciprocal(out=dcol(24), in_=dcol(24)))
    dv(lambda e: e.tensor_tensor(out=dcol(24), in0=dcol(24), in1=dcol(20), op=ALU.mult))
    dv(lambda e: e.tensor_tensor(out=dcol(28), in0=dcol(24), in1=dcol(24), op=ALU.mult))
    dv(lambda e: e.tensor_scalar(out=dcol(20), in0=dcol(28), scalar1=1.0 / 15.0, scalar2=1.0 / 13.0,
                                 op0=ALU.mult, op1=ALU.add))
    for cc in (1.0 / 11.0, 1.0 / 9.0, 1.0 / 7.0, 1.0 / 5.0, 1.0 / 3.0, 1.0):
        dv(lambda e: e.tensor_tensor(out=dcol(20), in0=dcol(20), in1=dcol(28), op=ALU.mult))
        dv(lambda e, cc=cc: e.tensor_scalar(out=dcol(20), in0=dcol(20), scalar1=cc, scalar2=None, op0=ALU.add))
    dv(lambda e: e.tensor_tensor(out=dcol(20), in0=dcol(20), in1=dcol(24), op=ALU.mult))
    dv(lambda e: e.tensor_scalar(out=dcol(12), in0=dcol(12), scalar1=0.0, scalar2=None, op0=ALU.max))
    dv(lambda e: e.scalar_tensor_tensor(out=dcol(12), in0=dcol(20), scalar=2.0, in1=dcol(12),
                                        op0=ALU.mult, op1=ALU.add))
    dv(lambda e: e.tensor_scalar(out=dcol(0), in0=dcol(12), scalar1=-8.0, scalar2=None, op0=ALU.mult))
    dv(lambda e: e.tensor_scalar(out=dcol(4), in0=dcol(12), scalar1=-16.0, scalar2=None, op0=ALU.mult))
    dv(lambda e: e.tensor_scalar(out=dcol(8), in0=dcol(12), scalar1=8.0, scalar2=None, op0=ALU.mult))

    def rms(b, gcol, final, xn_t=None, XN_t=None, tmpf=None):
        tmpf = tmpf or tmp
        ps, psb = bank()
        for kc in range(KC):
            P.op("act", lambda e, kc=kc: e.activation(out=sq[:, kc, :], in_=xs[b][:, kc, :], func=AF.Square),
                 reads=[X[b][kc]], writes=[SQ[kc]])
        for kc in range(KC):
            P.op("pe", lambda e, kc=kc: e.matmul(ps, lhsT=onesD[:], rhs=sq[:, kc, :],
                                                 start=(kc == 0), stop=(kc == KC - 1)),
                 reads=[SQ[kc], CONST], writes=[psb], inc=(kc == KC - 1))
        t1, t1b = tmpf()
        P.op("act", lambda e: e.activation(out=t1, in_=ps, func=AF.Sqrt, bias=RMS_EPS, scale=1.0),
             reads=[psb], writes=[t1b])
        t2, t2b = tmpf()
        P.op("dve", lambda e: e.reciprocal(out=t2, in_=t1), reads=[t1b], writes=[t2b])
        for kc in range(KC):
            out = xs[b][:, kc, :] if final else xn_t[:, kc, :]
            P.op("dve", lambda e, kc=kc, out=out: e.scalar_tensor_tensor(
                out=out, in0=xs[b][:, kc, :], scalar=col(gcol + kc), in1=t2, op0=ALU.mult, op1=ALU.mult),
                reads=[X[b][kc], t2b, CONST], writes=[X[b][kc] if final else XN_t[kc]])

    def ffn(b, l, xn_t, XN_t, tmpf):
        for u in range(11):
            slot, sbuf_ = ring_load(gu_s[l][u], 4096, "gu%d" % l)
            sv = slot.rearrange("p (a k j) -> p a k j", a=2, k=8)
            for jj in range(2):
                f = 2 * u + jj
                pg, pgb = bank()
                pu, pub = bank()
                for a_, (pp, ppb) in enumerate(((pg, pgb), (pu, pub))):
                    for kc in range(KC):
                        P.op("pe", lambda e, a_=a_, kc=kc, pp=pp, sv=sv, jj=jj: e.matmul(
                            pp, lhsT=sv[:, a_, kc, jj * 128:(jj + 1) * 128], rhs=xn_t[:, kc, :],
                            start=(kc == 0), stop=(kc == KC - 1)),
                            reads=[sbuf_, XN_t[kc]], writes=[ppb], inc=(kc == KC - 1))
                t, tb = tmpf()
                P.op("act", lambda e, t=t, pg=pg: e.activation(out=t, in_=pg, func=AF.Silu),
                     reads=[pgb], writes=[tb])
                P.op("dve", lambda e, t=t, pu=pu, f=f: e.tensor_tensor(out=h[:, f, :], in0=t, in1=pu, op=ALU.mult),
                     reads=[tb, pub], writes=[H[f]])
            yield
        for m in range(8):
            slot, sbuf_ = ring_load(dd_s[l][m], 2816, "dd%d" % l)
            sv = slot.rearrange("p (k j) -> p k j", k=FC)
            pd, pdb = bank()
            for kc in range(FC):
                P.op("pe", lambda e, kc=kc, pd=pd, sv=sv: e.matmul(
                    pd, lhsT=sv[:, kc, :], rhs=h[:, kc, :], start=(kc == 0), stop=(kc == FC - 1)),
                    reads=[sbuf_, H[kc]], writes=[pdb], inc=(kc == FC - 1))
            P.op("dve", lambda e, pd=pd, m=m: e.scalar_tensor_tensor(
                out=xs[b][:, m, :], in0=pd, scalar=0.5, in1=xs[b][:, m, :], op0=ALU.mult, op1=ALU.add),
                reads=[pdb, X[b][m]], writes=[X[b][m]])
            yield

    def proj4(unit_ap, fam, evac, key=None):
        slot, sbuf_ = ring_load(unit_ap, 4096, fam, key=key)
        sv = slot.rearrange("p (k j) -> p k j", k=KC)
        for ch in range(4):
            ps, psb = bank()
            for kc in range(KC):
                P.op("pe", lambda e, kc=kc, ps=ps, sv=sv, ch=ch: e.matmul(
                    ps, lhsT=sv[:, kc, ch * 128:(ch + 1) * 128], rhs=xn2[:, kc, :],
                    start=(kc == 0), stop=(kc == KC - 1)),
                    reads=[sbuf_, XN2[kc]], writes=[psb], inc=(kc == KC - 1))
            evac(ch, ps, psb)

    def mixer(b):
        sg = [tmp() for _ in range(4)]

        def ev_gate(ch, ps, psb):
            P.op("act", lambda e: e.activation(out=sg[ch][0], in_=ps, func=AF.Sigmoid),
                 reads=[psb], writes=[sg[ch][1]])

        def ev_val(ch, ps, psb):
            dst_u = ubuf[:, ch, 30:30 + T] if ch < 2 else ubb[:, ch - 2, 30:30 + T]
            P.op("dve", lambda e: e.tensor_tensor(out=dst_u, in0=ps, in1=sg[ch][0], op=ALU.mult),
                 reads=[psb, sg[ch][1]], writes=[UB[ch]])

        def ev_rx(ch, ps, psb):
            P.op("act", lambda e: e.activation(out=xbr[:, ch, 3:3 + T], in_=ps, func=AF.Copy),
                 reads=[psb], writes=[XBR[ch]])

        def ev_rg(ch, ps, psb):
            P.op("act", lambda e: e.activation(out=glt[:, ch, :], in_=ps, func=AF.Gelu_apprx_tanh),
                 reads=[psb], writes=[GL[ch]])

        def conv4_part():
            for ch in range(4):
                P.op("dve", lambda e, ch=ch: e.tensor_scalar(
                    out=xr[:, ch, :], in0=xbr[:, ch, 0:T], scalar1=col(C_W4 + ch), scalar2=col(C_B4 + ch),
                    op0=ALU.mult, op1=ALU.add), reads=[XBR[ch], CONST], writes=[XR[ch]])
                for k in range(1, 4):
                    P.op("dve", lambda e, ch=ch, k=k: e.scalar_tensor_tensor(
                        out=xr[:, ch, :], in0=xbr[:, ch, k:k + T], scalar=col(C_W4 + k * 4 + ch), in1=xr[:, ch, :],
                        op0=ALU.mult, op1=ALU.add), reads=[XBR[ch], XR[ch], CONST], writes=[XR[ch]])
                P.op("act", lambda e, ch=ch: e.activation(out=xbr[:, ch, 0:3], in_=xbr[:, ch, T:T + 3], func=AF.Copy),
                     reads=[XBR[ch]], writes=[XBR[ch]])
                P.op("act", lambda e, ch=ch: e.activation(out=xrb[:, ch, :], in_=xr[:, ch, :], func=AF.Copy),
                     reads=[XR[ch]], writes=[XRB[ch]])

        proj4(win_s[1], "win", ev_gate, key="win1")
        yield
        proj4(win_s[0], "win", ev_val, key="win0")
        yield
        proj4(win_s[2], "win", ev_rx, key="win2")
        yield
        proj4(win_s[3], "win", ev_rg, key="win3")
        yield
        conv4_part()

        for j, ch in enumerate((2, 3)):
            slot, sbuf_ = ring_load(cd_s[j], 3968, "cd")
            sv = slot.rearrange("p (k j) -> p k j", k=31)
            ps, psb = bank()
            for k in range(31):
                P.op("pe", lambda e, k=k, ps=ps, sv=sv, j=j: e.matmul(
                    ps, lhsT=sv[:, k, :], rhs=ubb[:, j, k:k + T], start=(k == 0), stop=(k == 30)),
                    reads=[sbuf_, UB[ch]], writes=[psb], inc=(k == 30))
            P.op("act", lambda e, ps=ps, ch=ch: e.activation(out=v[:, ch, :], in_=ps, func=AF.Identity,
                                                             bias=col(C_CB + ch), scale=1.0),
                 reads=[psb, CONST], writes=[V[ch]])
            yield
        for k in range(31):
            acc, ACC = (v, V) if k % 2 == 0 else (hout, HOUT)
            for ch in range(2):
                if k == 0:
                    P.op("dve", lambda e, ch=ch: e.tensor_scalar(
                        out=v[:, ch, :], in0=ubuf[:, ch, 0:T], scalar1=col(C_CW + ch), scalar2=col(C_CB + ch),
                        op0=ALU.mult, op1=ALU.add), reads=[UB[ch], CONST], writes=[V[ch]], inc=(ch == 1))
                elif k == 1:
                    P.op("dve", lambda e, ch=ch: e.tensor_scalar(
                        out=hout[:, ch, :], in0=ubuf[:, ch, 1:1 + T], scalar1=col(C_CW + 4 + ch), scalar2=None,
                        op0=ALU.mult), reads=[UB[ch], CONST], writes=[HOUT[ch]], inc=(ch == 1))
                else:
                    P.op("dve", lambda e, ch=ch, k=k, acc=acc: e.scalar_tensor_tensor(
                        out=acc[:, ch, :], in0=ubuf[:, ch, k:k + T], scalar=col(C_CW + k * 4 + ch), in1=acc[:, ch, :],
                        op0=ALU.mult, op1=ALU.add), reads=[UB[ch], ACC[ch], CONST], writes=[ACC[ch]], inc=(ch == 1))
            if k % 4 == 3:
                yield
        for ch in range(2):
            P.op("dve", lambda e, ch=ch: e.tensor_tensor(out=v[:, ch, :], in0=v[:, ch, :], in1=hout[:, ch, :], op=ALU.add),
                 reads=[V[ch], HOUT[ch]], writes=[V[ch]])
        for ch in range(4):
            P.op("act", lambda e, ch=ch: e.activation(out=sq[:, 4 + ch, :], in_=v[:, ch, :], func=AF.Square),
                 reads=[V[ch]], writes=[SQ[4 + ch]])
            P.op("act", lambda e, ch=ch: e.activation(out=sq[:, ch, :], in_=v[:, ch, :], func=AF.Copy),
                 reads=[V[ch]], writes=[SQ[ch]])
            ub_ = ubuf[:, ch, :] if ch < 2 else ubb[:, ch - 2, :]
            P.op("act", lambda e, ub_=ub_: e.activation(out=ub_[:, 0:30], in_=ub_[:, T:T + 30], func=AF.Copy),
                 reads=[UB[ch]], writes=[UB[ch]])
        yield
        pm, pmb = bank()
        pe2, pe2b = bank()
        for ch in range(4):
            P.op("pe", lambda e, ch=ch: e.matmul(pm, lhsT=onesC[:], rhs=sq[:, ch, :], start=(ch == 0), stop=(ch == 3)),
                 reads=[SQ[ch], CONST], writes=[pmb], inc=(ch == 3))
        for ch in range(4):
            P.op("pe", lambda e, ch=ch: e.matmul(pe2, lhsT=onesC[:], rhs=sq[:, 4 + ch, :], start=(ch == 0), stop=(ch == 3)),
                 reads=[SQ[4 + ch], CONST], writes=[pe2b], inc=(ch == 3))
        yield
        mean, meanb = tmp()
        msq, msqb = tmp()
        P.op("act", lambda e: e.activation(out=mean, in_=pm, func=AF.Copy), reads=[pmb], writes=[meanb])
        P.op("act", lambda e: e.activation(out=msq, in_=pm, func=AF.Square), reads=[pmb], writes=[msqb])
        var, varb = tmp()
        P.op("dve", lambda e: e.tensor_tensor(out=var, in0=pe2, in1=msq, op=ALU.subtract),
             reads=[pe2b, msqb], writes=[varb])
        sd, sdb = tmp()
        P.op("act", lambda e: e.activation(out=sd, in_=var, func=AF.Sqrt, bias=LN_EPS, scale=1.0),
             reads=[varb], writes=[sdb])
        rstd, rstdb = tmp()
        P.op("dve", lambda e: e.reciprocal(out=rstd, in_=sd), reads=[sdb], writes=[rstdb])
        for ch in range(4):
            t1, t1b = tmp()
            P.op("dve", lambda e, ch=ch, t1=t1: e.tensor_tensor(out=t1, in0=v[:, ch, :], in1=mean, op=ALU.subtract),
                 reads=[V[ch], meanb], writes=[t1b])
            P.op("dve", lambda e, t1=t1: e.tensor_tensor(out=t1, in0=t1, in1=rstd, op=ALU.mult),
                 reads=[t1b, rstdb], writes=[t1b])
            P.op("act", lambda e, ch=ch, t1=t1: e.activation(out=mixo[:, ch, :], in_=t1, func=AF.Silu,
                                                             bias=col(C_LB + ch), scale=col(C_LG + ch)),
                 reads=[t1b, CONST], writes=[MIXO[ch]])

        yield
        for ch in range(4):
            pa, pab = bank()
            px, pxb = bank()
            P.op("pe", lambda e, ch=ch, pa=pa: e.matmul(pa, lhsT=gatew[:, ch, :], rhs=xrb[:, ch, :], start=True, stop=True),
                 reads=[GATEW, XRB[ch]], writes=[pab])
            P.op("pe", lambda e, ch=ch, px=px: e.matmul(px, lhsT=gatew[:, 4 + ch, :], rhs=xrb[:, ch, :], start=True, stop=True),
                 reads=[GATEW, XRB[ch]], writes=[pxb])
            P.op("act", lambda e, ch=ch, pa=pa: e.activation(out=v[:, ch, :], in_=pa, func=AF.Sigmoid,
                                                             bias=col(C_BA + ch), scale=1.0),
                 reads=[pab, CONST], writes=[V[ch]])
            P.op("act", lambda e, ch=ch, px=px: e.activation(out=igt[:, ch, :], in_=px, func=AF.Sigmoid,
                                                             bias=col(C_BX + ch), scale=1.0),
                 reads=[pxb, CONST], writes=[IG[ch]])
        yield
        for ch in range(4):
            P.op("dve", lambda e, ch=ch: e.tensor_tensor(out=igt[:, ch, :], in0=igt[:, ch, :], in1=xr[:, ch, :], op=ALU.mult),
                 reads=[IG[ch], XR[ch]], writes=[IG[ch]])
        yield
        for ch in range(4):
            P.op("act", lambda e, ch=ch: e.activation(out=xr[:, ch, :], in_=v[:, ch, :], func=AF.Exp, scale=der[:, ch:ch + 1]),
                 reads=[V[ch], DER], writes=[XR[ch]])
            P.op("act", lambda e, ch=ch: e.activation(out=hout[:, ch, :], in_=v[:, ch, :], func=AF.Tanh, scale=der[:, 8 + ch:9 + ch]),
                 reads=[V[ch], DER], writes=[HOUT[ch]])
            P.op("act", lambda e, ch=ch: e.activation(out=v[:, ch, :], in_=v[:, ch, :], func=AF.Exp, scale=der[:, 4 + ch:5 + ch]),
                 reads=[V[ch], DER], writes=[V[ch]])
        yield
        for ch in range(4):
            P.op("dve", lambda e, ch=ch: e.scalar_tensor_tensor(out=v[:, ch, :], in0=v[:, ch, :], scalar=1.0, in1=hout[:, ch, :],
                                                                op0=ALU.add, op1=ALU.mult),
                 reads=[V[ch], HOUT[ch]], writes=[V[ch]])
        yield
        for ch in range(4):
            P.op("act", lambda e, ch=ch: e.activation(out=v[:, ch, :], in_=v[:, ch, :], func=AF.Sqrt),
                 reads=[V[ch]], writes=[V[ch]])
        yield
        for ch in range(4):
            P.op("dve", lambda e, ch=ch: e.tensor_tensor(out=igt[:, ch, :], in0=igt[:, ch, :], in1=v[:, ch, :], op=ALU.mult),
                 reads=[IG[ch], V[ch]], writes=[IG[ch]])
            P.op("dve", lambda e, ch=ch: e.tensor_tensor_scan(
                out=hout[:, ch, :], data0=xr[:, ch, :], data1=igt[:, ch, :], initial=hst[:, ch:ch + 1], op0=ALU.mult, op1=ALU.add),
                reads=[XR[ch], IG[ch], HST[ch]], writes=[HOUT[ch]])
            P.op("act", lambda e, ch=ch: e.activation(out=hst[:, ch:ch + 1], in_=hout[:, ch, T - 1:T], func=AF.Copy),
                 reads=[HOUT[ch]], writes=[HST[ch]])
            P.op("dve", lambda e, ch=ch: e.tensor_tensor(out=mixo[:, 4 + ch, :], in0=hout[:, ch, :], in1=glt[:, ch, :], op=ALU.mult),
                 reads=[HOUT[ch], GL[ch]], writes=[MIXO[4 + ch]])

        yield
        yield
        for u in range(2):
            slot, sbuf_ = ring_load(wout_s[u], 4096, "wout")
            sv = slot.rearrange("p (k j) -> p k j", k=KC)
            for jj in range(4):
                m = 4 * u + jj
                ps, psb = bank()
                for kc in range(KC):
                    P.op("pe", lambda e, kc=kc, ps=ps, sv=sv, jj=jj: e.matmul(
                        ps, lhsT=sv[:, kc, jj * 128:(jj + 1) * 128], rhs=mixo[:, kc, :],
                        start=(kc == 0), stop=(kc == KC - 1)),
                        reads=[sbuf_, MIXO[kc]], writes=[psb], inc=(kc == KC - 1))
                P.op("dve", lambda e, ps=ps, m=m: e.tensor_tensor(out=xs[b][:, m, :], in0=ps, in1=xs[b][:, m, :], op=ALU.add),
                     reads=[psb, X[b][m]], writes=[X[b][m]])

    bar = [("act", P.count["act"]), ("dve", P.count["dve"])] + list(cv_store_toks.values()) + list(cd_toks)
    for e_ in ("pe", "act", "dve", "sp"):
        for t_ in bar:
            if t_[0] != e_ and t_[1] > 0:
                P.wait(e_, t_)

    def phaseA(i):
        b = i % 2
        rms(b, C_G1, False, xn, XN, tmp_a)
        yield
        yield from ffn(b, 0, xn, XN, tmp_a)

    def phaseB(i):
        b = i % 2
        rms(b, C_G2, False, xn2, XN2, tmp)
        yield
        yield from mixer(b)

    def phaseC(i, held, gb_next=None):
        b = i % 2
        rms(b, C_G3, False, xn2, XN2, tmp)
        for _ in held:
            pass
        for n_, _ in enumerate(ffn(b, 1, xn2, XN2, tmp)):
            if gb_next is not None and n_ == 15:
                next(gb_next)
        if i + 1 < NT:
            ring_prefetch("win1", win_s[1], 4096, "win")
            ring_prefetch("win0", win_s[0], 4096, "win")
            ring_prefetch("win2", win_s[2], 4096, "win")
            ring_prefetch("win3", win_s[3], 4096, "win")
        rms(b, C_G4, True)
        return store_y(i)

    if NT > 1:
        load_x(1)
    for _ in phaseA(0):
        pass
    last_store = {}
    HOLD = 3
    gb = phaseB(0)
    yoff = 0
    for i in range(NT):
        steps = list(range(21)) if i + 1 < NT else []
        ga = phaseA(i + 1) if i + 1 < NT else iter(())
        given = 0
        for yi, _ in enumerate(gb, start=yoff):
            if yi != 5 and yi < 7:
                continue
            if given < len(steps) - HOLD:
                next(ga, None)
                given += 1
        while given < len(steps) - HOLD:
            next(ga, None)
            given += 1
        gb = phaseB(i + 1) if i + 1 < NT else None
        yoff = 1
        last_store[i % 2] = phaseC(i, ga, gb)
        if i + 2 < NT:
            load_x(i + 2)
    for b in last_store:
        P.wait(XY_ENG, last_store[b])

    sems = {}

    def sem_of(key):
        if key not in sems:
            nm = key if isinstance(key, str) else "%s%d" % key
            sems[key] = nc.alloc_semaphore("s_" + nm)
        return sems[key]

    FQ = P.finalize()
    P.fq = FQ
    for e in Prog.ENG:
        for it in FQ[e]:
            if it[0] == "wait":
                sem_of(it[1])
            elif it[2] is not None:
                sem_of(it[2])

    def emit(eng, name):
        for it in FQ[name]:
            if it[0] == "wait":
                eng.wait_ge(sems[it[1]], it[2])
            else:
                ins = it[1](eng)
                if it[2] is not None:
                    ins.then_inc(sems[it[2]], it[3])

    with nc.Block() as block:
        @block.sync
        def _(e):
            emit(e, "sp")

        @block.scalar
        def _(e):
            emit(e, "act")

        @block.vector
        def _(e):
            emit(e, "dve")

        @block.gpsimd
        def _(e):
            emit(e, "pool")

        @block.tensor
        def _(e):
            emit(e, "pe")
    return nc


def _cols(vv, n):
    return np.ascontiguousarray(np.asarray(vv, np.float32).reshape(n, 128).T)


def kernel(x, ffn1_norm, ffn1_w_gate, ffn1_w_up, ffn1_w_down, mix_norm, w_in,
           conv_dw, conv_dw_bias, conv_ln_g, conv_ln_b, lru_conv_w, lru_conv_b,
           lru_w_a, lru_b_a, lru_w_x, lru_b_x, lru_lambda, w_out,
           ffn2_norm, ffn2_w_gate, ffn2_w_up, ffn2_w_down, final_norm):
    f = lambda a: np.ascontiguousarray(np.asarray(a, np.float32))
    x = f(x)
    cst = np.concatenate([
        _cols(ffn1_norm, 8), _cols(mix_norm, 8), _cols(ffn2_norm, 8), _cols(final_norm, 8),
        _cols(conv_dw_bias, 4), _cols(conv_ln_g, 4), _cols(conv_ln_b, 4),
        f(lru_conv_w).reshape(4, 4, 128).transpose(2, 0, 1).reshape(128, 16),
        _cols(lru_conv_b, 4), _cols(lru_b_a, 4), _cols(lru_b_x, 4), _cols(lru_lambda, 4),
        f(conv_dw).reshape(31, 4, 128).transpose(2, 0, 1).reshape(128, 124),
    ], axis=1)
    cst = np.ascontiguousarray(cst, np.float32)
    assert cst.shape == (128, NCST)
    common = {
        "cst": cst, "ident": np.eye(128, dtype=np.float32),
        "wg1": f(ffn1_w_gate), "wu1": f(ffn1_w_up), "wd1": f(ffn1_w_down),
        "wg2": f(ffn2_w_gate), "wu2": f(ffn2_w_up), "wd2": f(ffn2_w_down),
        "win": f(w_in), "wout": f(w_out), "lwa": f(lru_w_a), "lwx": f(lru_w_x),
    }
    in_maps = []
    for c in range(NCORES):
        m = dict(common)
        m["xT"] = np.ascontiguousarray(x[c].T)
        in_maps.append(m)
    nc = build_nc()
    res = run_bass_kernel_spmd(nc, in_maps, core_ids=list(range(NCORES)))
    out = np.empty((NCORES, S, D), np.float32)
    for c in range(NCORES):
        out[c] = np.asarray(res.results[c]["yT"]).T
    return out
```
